# Optimizing a Trainium2 kernel written in Bass

```python
import math
import jax, jax.numpy as jnp
from jax import lax
import numpy as np

D_MODEL = 2048
BATCH = 4
SEQ = 4096
DEPTH = 2

GRID_W = 64
CTX_LEN = 256
MIX_WIDTH = D_MODEL
CONV_WIDTH = MIX_WIDTH // 2
FOURIER_WIDTH = MIX_WIDTH - CONV_WIDTH
FOURIER_GROUPS = 4
FOURIER_GROUP_DIM = FOURIER_WIDTH // FOURIER_GROUPS
CONV_K = 31
HEAD_DIM = 128
NA_WIDTH = (3 * MIX_WIDTH) // 4
NA_HEADS = NA_WIDTH // HEAD_DIM
SSM_WIDTH = MIX_WIDTH - NA_WIDTH
SSM_GROUP = 16
SSM_GROUPS = SSM_WIDTH // SSM_GROUP
SSM_STATE = 64
NA_ROWS = 8
NA_COLS = 16
N_EVEN = (DEPTH + 1) // 2
N_ODD = DEPTH // 2
EVEN_IN = 3 * CONV_WIDTH + 2 * FOURIER_WIDTH
ODD_IN = 4 * NA_WIDTH + 2 * SSM_WIDTH
EVEN_SPLITS = (CONV_WIDTH, 2 * CONV_WIDTH, 3 * CONV_WIDTH, 3 * CONV_WIDTH + FOURIER_WIDTH)
ODD_SPLITS = (NA_WIDTH, 2 * NA_WIDTH, 3 * NA_WIDTH, 4 * NA_WIDTH, 4 * NA_WIDTH + SSM_WIDTH)
EPS = 1e-6
NEG_INF = -1e30

kernel_name = "hybrid_conv_fourier_natten_s5_dit_block"

F32 = jnp.float32


def rms_norm(x, g):
    xf = x.astype(F32)
    y = xf * lax.rsqrt(jnp.mean(xf * xf, axis=-1, keepdims=True) + EPS)
    return (y * g.astype(F32)).astype(x.dtype)


def layer_norm(x, g, b):
    xf = x.astype(F32)
    mu = jnp.mean(xf, axis=-1, keepdims=True)
    var = jnp.mean(jnp.square(xf - mu), axis=-1, keepdims=True)
    y = (xf - mu) * lax.rsqrt(var + EPS) * g.astype(F32) + b.astype(F32)
    return y.astype(x.dtype)


def modulate(h, shift, scale):
    return h * (1.0 + scale[:, None]) + shift[:, None]


def conv_fourier_mixer(u, w_in, w_out, conv_w, conv_b, ln_g, ln_b, fourier_g):
    bsz, length, _ = u.shape
    p = u @ w_in
    a_val, a_glu, a_gate, b_in, b_gate = jnp.split(p, EVEN_SPLITS, axis=-1)
    a = a_val * jax.nn.sigmoid(a_glu)
    a = lax.conv_general_dilated(
        a, conv_w[:, None, :], window_strides=(1,),
        padding=[(CONV_K // 2, CONV_K // 2)],
        dimension_numbers=("NWC", "WIO", "NWC"),
        feature_group_count=CONV_WIDTH) + conv_b
    a = jax.nn.silu(layer_norm(a, ln_g, ln_b)) * jax.nn.silu(a_gate)
    bn = rms_norm(b_in.reshape(bsz, length, FOURIER_GROUPS, FOURIER_GROUP_DIM), fourier_g)
    f = jnp.fft.fft2(bn.astype(F32), axes=(1, 3), norm="ortho").real.astype(u.dtype)
    b = f.reshape(bsz, length, FOURIER_WIDTH) * jax.nn.silu(b_gate)
    return jnp.concatenate([a, b], axis=-1) @ w_out


def neighbourhood_attention(q, k, v, k_c, v_c, rpb):
    bsz, length = q.shape[:2]
    rows = length // GRID_W
    kr = min(NA_ROWS, rows)
    qg = q.reshape(bsz, rows, GRID_W, NA_HEADS, HEAD_DIM)
    kg = k.reshape(bsz, rows, GRID_W, NA_HEADS, HEAD_DIM)
    vg = v.reshape(bsz, rows, GRID_W, NA_HEADS, HEAD_DIM)
    key_col = jnp.tile(jnp.arange(GRID_W), kr)
    key_row_off = jnp.repeat(jnp.arange(kr), GRID_W)
    q_col = jnp.arange(GRID_W)
    c_start = jnp.clip(q_col - NA_COLS // 2, 0, GRID_W - NA_COLS)
    col_mask = (key_col[None] >= c_start[:, None]) & (key_col[None] < c_start[:, None] + NA_COLS)
    dc = jnp.clip(key_col[None] - q_col[:, None] + NA_COLS - 1, 0, 2 * NA_COLS - 2)
    scale = HEAD_DIM ** -0.5
    n_loc = kr * GRID_W

    def row_block(r):
        r_start = jnp.clip(r - kr // 2, 0, rows - kr)
        q_r = lax.dynamic_index_in_dim(qg, r, axis=1, keepdims=False)
        k_blk = lax.dynamic_slice_in_dim(kg, r_start, kr, axis=1).reshape(bsz, n_loc, NA_HEADS, HEAD_DIM)
        v_blk = lax.dynamic_slice_in_dim(vg, r_start, kr, axis=1).reshape(bsz, n_loc, NA_HEADS, HEAD_DIM)
        dr = jnp.clip(r_start + key_row_off - r + NA_ROWS - 1, 0, 2 * NA_ROWS - 2)
        bias = rpb[:, dr[None, :], dc].astype(F32)
        s_loc = jnp.einsum("bqhd,bkhd->bhqk", q_r, k_blk).astype(F32) * scale + bias
        s_loc = jnp.where(col_mask, s_loc, NEG_INF)
        s_ctx = jnp.einsum("bqhd,bkhd->bhqk", q_r, k_c).astype(F32) * scale
        p = jax.nn.softmax(jnp.concatenate([s_loc, s_ctx], axis=-1), axis=-1).astype(v.dtype)
        return (jnp.einsum("bhqk,bkhd->bqhd", p[..., :n_loc], v_blk)
                + jnp.einsum("bhqk,bkhd->bqhd", p[..., n_loc:], v_c))

    out = lax.map(row_block, jnp.arange(rows))
    return jnp.moveaxis(out, 0, 1).reshape(bsz, length, NA_WIDTH)


def context_attention(q_c, k_c, v_c):
    bsz, lc = q_c.shape[:2]
    s = jnp.einsum("bqhd,bkhd->bhqk", q_c, k_c).astype(F32) * (HEAD_DIM ** -0.5)
    p = jax.nn.softmax(s, axis=-1).astype(v_c.dtype)
    return jnp.einsum("bhqk,bkhd->bqhd", p, v_c).reshape(bsz, lc, NA_WIDTH)


def diag_scan(lam_bar, bu):
    a = jnp.broadcast_to(lam_bar, bu.shape)

    def combine(e1, e2):
        a1, b1 = e1
        a2, b2 = e2
        return a1 * a2, a2 * b1 + b2

    _, h = lax.associative_scan(combine, (a, bu), axis=1)
    return h


def s5_scan_direction(u_c, u_x, a_re, a_im, log_dt, b_re, b_im):
    lam = lax.complex(a_re.astype(F32), a_im.astype(F32))
    dt = jnp.exp(log_dt.astype(F32))[:, None]
    lam_dt = lam * dt
    lam_bar = jnp.exp(lam_dt)
    b_bar = ((lam_bar - 1.0) / lam)[..., None] * lax.complex(b_re.astype(F32), b_im.astype(F32))
    bu_c = jnp.einsum("blgh,gph->blgp", u_c.astype(F32), b_bar)
    bu_x = jnp.einsum("blgh,gph->blgp", u_x.astype(F32), b_bar)
    h_c = diag_scan(lam_bar, bu_c)
    length = u_x.shape[1]
    steps = jnp.arange(1, length + 1, dtype=F32)[:, None, None]
    lam_pow = jnp.exp(lam_dt[None] * steps)
    h_x = diag_scan(lam_bar, bu_x) + lam_pow[None] * h_c[:, -1][:, None]
    return h_c, h_x


def s5_readout(h_f, h_b, u, c_re, c_im, d_skip, w_glu):
    bsz, length = u.shape[:2]
    c_f = lax.complex(c_re[0].astype(F32), c_im[0].astype(F32))
    c_b = lax.complex(c_re[1].astype(F32), c_im[1].astype(F32))
    y = (jnp.einsum("blgp,ghp->blgh", h_f, c_f).real
         + jnp.einsum("blgp,ghp->blgh", h_b, c_b).real
         + d_skip.reshape(SSM_GROUPS, SSM_GROUP).astype(F32) * u.astype(F32))
    y = jax.nn.gelu(y.reshape(bsz, length, SSM_WIDTH)).astype(u.dtype)
    return y * jax.nn.sigmoid(y @ w_glu)


def na_ssm_mixer(u_c, u_x, w_in, w_out, rpb, a_re, a_im, log_dt, b_re, b_im, c_re, c_im,
                 d_skip, w_glu, need_ctx):
    bsz, length, _ = u_x.shape
    lc = u_c.shape[1]
    q_x, k_x, v_x, g_x, d_x, dg_x = jnp.split(u_x @ w_in, ODD_SPLITS, axis=-1)
    q_c, k_c, v_c, g_c, d_c, dg_c = jnp.split(u_c @ w_in, ODD_SPLITS, axis=-1)
    heads = lambda t, n: t.reshape(bsz, n, NA_HEADS, HEAD_DIM)
    k_c4, v_c4 = heads(k_c, lc), heads(v_c, lc)
    na_x = neighbourhood_attention(heads(q_x, length), heads(k_x, length), heads(v_x, length),
                                   k_c4, v_c4, rpb)
    u_dc = d_c.reshape(bsz, lc, SSM_GROUPS, SSM_GROUP)
    u_dx = d_x.reshape(bsz, length, SSM_GROUPS, SSM_GROUP)
    hcf, hxf = s5_scan_direction(u_dc, u_dx, a_re[0], a_im[0], log_dt[0], b_re[0], b_im[0])
    hcb, hxb = s5_scan_direction(u_dc[:, ::-1], u_dx[:, ::-1], a_re[1], a_im[1], log_dt[1],
                                 b_re[1], b_im[1])
    hcb, hxb = hcb[:, ::-1], hxb[:, ::-1]
    ssm_x = s5_readout(hxf, hxb, u_dx, c_re, c_im, d_skip, w_glu)
    out_x = jnp.concatenate([na_x * jax.nn.silu(g_x), ssm_x * jax.nn.silu(dg_x)], axis=-1) @ w_out
    out_c = None
    if need_ctx:
        na_c = context_attention(heads(q_c, lc), k_c4, v_c4)
        ssm_c = s5_readout(hcf, hcb, u_dc, c_re, c_im, d_skip, w_glu)
        out_c = jnp.concatenate([na_c * jax.nn.silu(g_c), ssm_c * jax.nn.silu(dg_c)], axis=-1) @ w_out
    return out_c, out_x


def setup_inputs(seed: int = 0) -> dict:
    key = jax.random.key(seed)
    ks = jax.random.split(key, 32)
    nrm = lambda k, shape, s: jax.random.normal(k, shape, F32) * s
    G, P, H = SSM_GROUPS, SSM_STATE, SSM_GROUP
    a_im_base = math.pi * jnp.arange(P, dtype=F32)
    return {
        "x": nrm(ks[0], (BATCH, SEQ, D_MODEL), 1.0),
        "c": nrm(ks[1], (BATCH, D_MODEL), 1.0),
        "ctx": nrm(ks[2], (BATCH, CTX_LEN, D_MODEL), 1.0),
        "c_ctx": nrm(ks[3], (D_MODEL,), 1.0),
        "pre_g": 1.0 + nrm(ks[4], (DEPTH, D_MODEL), 0.05),
        "post_g": 1.0 + nrm(ks[5], (DEPTH, D_MODEL), 0.05),
        "ada_w": nrm(ks[6], (DEPTH, D_MODEL, 3 * D_MODEL), D_MODEL ** -0.5),
        "ada_b": nrm(ks[7], (DEPTH, 3 * D_MODEL), 0.02),
        "ab_w_in": nrm(ks[8], (N_EVEN, D_MODEL, EVEN_IN), D_MODEL ** -0.5),
        "ab_w_out": nrm(ks[9], (N_EVEN, MIX_WIDTH, D_MODEL), MIX_WIDTH ** -0.5),
        "conv_w": nrm(ks[10], (N_EVEN, CONV_K, CONV_WIDTH), CONV_K ** -0.5),
        "conv_b": nrm(ks[11], (N_EVEN, CONV_WIDTH), 0.02),
        "conv_ln_g": 1.0 + nrm(ks[12], (N_EVEN, CONV_WIDTH), 0.05),
        "conv_ln_b": nrm(ks[13], (N_EVEN, CONV_WIDTH), 0.02),
        "fourier_g": 1.0 + nrm(ks[14], (N_EVEN, FOURIER_GROUPS, FOURIER_GROUP_DIM), 0.05),
        "cd_w_in": nrm(ks[15], (N_ODD, D_MODEL, ODD_IN), D_MODEL ** -0.5),
        "cd_w_out": nrm(ks[16], (N_ODD, MIX_WIDTH, D_MODEL), MIX_WIDTH ** -0.5),
        "na_rpb": nrm(ks[17], (N_ODD, NA_HEADS, 2 * NA_ROWS - 1, 2 * NA_COLS - 1), 0.1),
        "s5_a_re": -0.5 + nrm(ks[18], (N_ODD, 2, G, P), 0.01),
        "s5_a_im": a_im_base + nrm(ks[19], (N_ODD, 2, G, P), 0.01),
        "s5_log_dt": jax.random.uniform(ks[20], (N_ODD, 2, G), F32, math.log(1e-3), math.log(1e-1)),
        "s5_b_re": nrm(ks[21], (N_ODD, 2, G, P, H), (2 * H) ** -0.5),
        "s5_b_im": nrm(ks[22], (N_ODD, 2, G, P, H), (2 * H) ** -0.5),
        "s5_c_re": nrm(ks[23], (N_ODD, 2, G, H, P), P ** -0.5),
        "s5_c_im": nrm(ks[24], (N_ODD, 2, G, H, P), P ** -0.5),
        "s5_d": nrm(ks[25], (N_ODD, SSM_WIDTH), 1.0),
        "s5_w_glu": nrm(ks[26], (N_ODD, SSM_WIDTH, SSM_WIDTH), SSM_WIDTH ** -0.5),
    }


def reference(x, c, ctx, c_ctx, pre_g, post_g, ada_w, ada_b, ab_w_in, ab_w_out, conv_w, conv_b,
              conv_ln_g, conv_ln_b, fourier_g, cd_w_in, cd_w_out, na_rpb, s5_a_re, s5_a_im,
              s5_log_dt, s5_b_re, s5_b_im, s5_c_re, s5_c_im, s5_d, s5_w_glu):
    h_x, h_c = x, ctx
    cond_x = jax.nn.silu(c)
    cond_c = jax.nn.silu(c_ctx)[None]
    for i in range(DEPTH):
        need_ctx = i < DEPTH - 1
        sh_x, sc_x, gt_x = jnp.split(cond_x @ ada_w[i] + ada_b[i], 3, axis=-1)
        sh_c, sc_c, gt_c = jnp.split(cond_c @ ada_w[i] + ada_b[i], 3, axis=-1)
        u_x = modulate(rms_norm(h_x, pre_g[i]), sh_x, sc_x)
        u_c = modulate(rms_norm(h_c, pre_g[i]), sh_c, sc_c)
        j = i // 2
        if i % 2 == 0:
            params = (ab_w_in[j], ab_w_out[j], conv_w[j], conv_b[j], conv_ln_g[j], conv_ln_b[j],
                      fourier_g[j])
            out_x = conv_fourier_mixer(u_x, *params)
            out_c = conv_fourier_mixer(u_c, *params) if need_ctx else None
        else:
            out_c, out_x = na_ssm_mixer(u_c, u_x, cd_w_in[j], cd_w_out[j], na_rpb[j],
                                        s5_a_re[j], s5_a_im[j], s5_log_dt[j], s5_b_re[j],
                                        s5_b_im[j], s5_c_re[j], s5_c_im[j], s5_d[j],
                                        s5_w_glu[j], need_ctx)
        h_x = h_x + gt_x[:, None] * rms_norm(out_x, post_g[i])
        if need_ctx:
            h_c = h_c + gt_c[:, None] * rms_norm(out_c, post_g[i])
    return h_x
```

```python
import math
from contextlib import ExitStack
import numpy as np
import ml_dtypes
import concourse.bass as bass
import concourse.mybir as mybir
from concourse.bass_utils import run_bass_kernel_spmd

F32 = mybir.dt.float32
BF16 = mybir.dt.bfloat16
I32 = mybir.dt.int32
AF = mybir.ActivationFunctionType
ALU = mybir.AluOpType

ENGS = ("tensor", "vector", "scalar", "gpsimd", "sync")
NDMA = 24
D = 2048
L = 4096
LC = 256
TOK = L + LC
OWN = 2048
NKV = 2304
EPS = 1e-6
CHUNKS = [(i * 512, 512) for i in range(8)] + [(4096, 256)]
BF = ml_dtypes.bfloat16


class Res:
    __slots__ = ("w", "r")

    def __init__(self):
        self.w = None
        self.r = {}


class T:
    def __init__(self, t):
        self.t = t
        self.res = Res()

    def __getitem__(self, idx):
        return self.t[idx]


class Sched:
    def __init__(self, nc, sems):
        self.nc = nc
        self.sems = sems
        self.q = {e: [] for e in ENGS}
        self.cnt = {e: 0 for e in ENGS}
        self.seen = {e: {} for e in ENGS}
        self.dma_rr = 0
        self.dma_cnt = [0] * NDMA
        self.out_toks = []
        self.dq = 0
        self.barrier = {}

    def _deps(self, eng, reads, writes, pe_chain):
        need = {}

        def add(tok):
            if tok is None:
                return
            s, v = tok
            if pe_chain and s == "tensor" and eng == "tensor":
                return
            if need.get(s, 0) < v:
                need[s] = v

        for r in reads:
            add(r.res.w)
        for w in writes:
            add(w.res.w)
            for s, v in w.res.r.items():
                add((s, v))
        for s, v in self.barrier.items():
            if need.get(s, 0) < v:
                need[s] = v
        waits = []
        for s, v in need.items():
            if self.seen[eng].get(s, 0) < v:
                waits.append((s, v))
                self.seen[eng][s] = v
        return waits

    def _commit(self, tok, reads, writes):
        s, v = tok
        for r in reads:
            if r.res.r.get(s, 0) < v:
                r.res.r[s] = v
        for w in writes:
            w.res.w = tok
            w.res.r = {}

    def op(self, eng, emit, reads=(), writes=(), pe_chain=False):
        waits = self._deps(eng, reads, writes, pe_chain)
        self.cnt[eng] += 1
        tok = (eng, self.cnt[eng])
        self.q[eng].append((waits, emit, (eng, 1)))
        self._commit(tok, reads, writes)
        return tok

    def dma(self, emit, reads=(), writes=(), q=None, is_output=False):
        if q is None:
            q = ("sync", "gpsimd")[self.dq % 2]
            self.dq += 1
        slot = self.dma_rr % NDMA
        self.dma_rr += 1
        semkey = ("dma", slot)
        waits = self._deps(q, reads, writes, False)
        prev = self.dma_cnt[slot]
        if prev and self.seen[q].get(semkey, 0) < prev:
            waits.append((semkey, prev))
            self.seen[q][semkey] = prev
        self.dma_cnt[slot] = prev + 16
        tok = (semkey, prev + 16)
        self.q[q].append((waits, emit, (semkey, 16)))
        self._commit(tok, reads, writes)
        if is_output:
            self.out_toks.append(tok)
        return tok

    def flush(self, final=False):
        if final:
            fin = {}
            for s, v in self.out_toks:
                fin[s] = max(fin.get(s, 0), v)
            for e in ENGS:
                if self.cnt[e]:
                    fin[e] = max(fin.get(e, 0), self.cnt[e])
            for s in range(NDMA):
                if self.dma_cnt[s]:
                    fin[("dma", s)] = self.dma_cnt[s]
            self.q["sync"].append((list(fin.items()), None, None))
        qs = self.q
        sems = self.sems

        def run(name):
            def body(e):
                for waits, emit, inc in qs[name]:
                    for s, v in waits:
                        e.wait_ge(sems[s], v)
                    if emit is not None:
                        emit(e).then_inc(sems[inc[0]], inc[1])
            return body

        with self.nc.Block() as block:
            block.tensor(run("tensor"))
            block.vector(run("vector"))
            block.scalar(run("scalar"))
            block.gpsimd(run("gpsimd"))
            block.sync(run("sync"))
        self.q = {e: [] for e in ENGS}
        self.barrier = {e: self.cnt[e] for e in ENGS if self.cnt[e]}
        for s_ in range(NDMA):
            if self.dma_cnt[s_]:
                self.barrier[("dma", s_)] = self.dma_cnt[s_]


def _cols(v, n):
    return np.ascontiguousarray(v.reshape(n, 128).T)


def _mt(W):
    K, F = W.shape
    return np.ascontiguousarray(W.reshape(K // 128, 128, F // 128, 128).transpose(2, 1, 0, 3))


def _kt(W):
    K, F = W.shape
    return np.ascontiguousarray(W.reshape(K // 128, 128, F).transpose(1, 0, 2))


def _na_tables(rpb, par):
    out = np.full((3, 12, 640, 128), -30000.0, np.float32)
    for cls, j in enumerate((0, 1, 2)):
        ws = min(max(2 * j - 4, 0), 26)
        qi = np.arange(128)
        qr_o = 2 * j + qi // 64
        qc_o = qi % 64
        ki = np.arange(640)
        kr_o = ws + ki // 64
        kc_o = ki % 64
        if par:
            qr, qc, kr, kc = 63 - qr_o, 63 - qc_o, 63 - kr_o, 63 - kc_o
        else:
            qr, qc, kr, kc = qr_o, qc_o, kr_o, kc_o
        rs = np.clip(qr - 4, 0, 56)
        cs = np.clip(qc - 8, 0, 48)
        ok = ((kr[:, None] >= rs[None]) & (kr[:, None] < rs[None] + 8) &
              (kc[:, None] >= cs[None]) & (kc[:, None] < cs[None] + 16))
        dr = np.clip(kr[:, None] - qr[None] + 7, 0, 14)
        dc = np.clip(kc[:, None] - qc[None] + 15, 0, 30)
        g = rpb[:, dr, dc]
        out[cls] = np.where(ok[None], g, np.float32(-30000.0))
    return np.ascontiguousarray(out.reshape(3, 12, 5, 128, 128).transpose(0, 1, 3, 2, 4))


def _dft_consts(par):
    n = np.arange(256)
    a = 2.0 * np.pi * ((n[:, None] * n[None, :]) % 256) / 256.0
    c256, s256 = np.cos(a), np.sin(a)
    cs256 = np.concatenate([c256, s256], axis=1)
    cs256 = cs256.reshape(2, 128, 512).transpose(1, 0, 2)
    if par:
        cc, sc = c256[::-1, ::-1], s256[::-1, ::-1]
    else:
        cc, sc = c256, s256
    cctx = np.concatenate([cc, -sc], axis=1).reshape(2, 128, 512).transpose(1, 0, 2)
    a = np.arange(64)
    if par:
        e1 = (a[:, None] * (a[None, :] + 1)) % 64
        e3 = ((a[:, None] + 1) * a[None, :]) % 64
        e2 = ((a[:, None] + 1) * (a[None, :] + 1)) % 4096
    else:
        e1 = (a[:, None] * a[None, :]) % 64
        e3 = e1
        e2 = (a[:, None] * a[None, :]) % 4096

    def bd(m):
        z = np.zeros((128, 128))
        z[:64, :64] = m
        z[64:, 64:] = m
        return z

    th1 = 2.0 * np.pi * e1 / 64.0
    th3 = 2.0 * np.pi * e3 / 64.0
    th2 = 2.0 * np.pi * e2 / 4096.0
    w1 = np.stack([bd(np.cos(th1)), bd(-np.sin(th1)), bd(-np.cos(th1))], axis=1)
    w3 = np.stack([bd(np.cos(th3)), bd(np.sin(th3))], axis=1)
    tw = np.stack([np.cos(th2).reshape(-1), np.sin(th2).reshape(-1)], axis=0)
    tw = np.broadcast_to(tw[None], (128, 2, 4096))
    return (np.ascontiguousarray(cs256).astype(BF), np.ascontiguousarray(cctx).astype(BF),
            np.ascontiguousarray(w1).astype(BF), np.ascontiguousarray(w3).astype(BF),
            np.ascontiguousarray(tw).astype(BF))


def _s5_layout(inp, par):
    dirs = (1, 0) if par else (0, 1)
    G, P, H = 32, 64, 16
    out = {}
    a_re = inp["s5_a_re"][0][list(dirs)]
    a_im = inp["s5_a_im"][0][list(dirs)]
    ldt = inp["s5_log_dt"][0][list(dirs)]

    def st(v):
        return np.ascontiguousarray(v.reshape(2, 16, 2, 64).transpose(2, 3, 0, 1).reshape(128, 2, 16))

    out["s5are"] = st(a_re)
    out["s5aim"] = st(a_im)
    out["s5ldt"] = st(np.broadcast_to(ldt[:, :, None], (2, G, P)))
    b_re = inp["s5_b_re"][0][list(dirs)]
    b_im = inp["s5_b_im"][0][list(dirs)]
    c_re = inp["s5_c_re"][0][list(dirs)]
    c_im = inp["s5_c_im"][0][list(dirs)]
    braw = np.zeros((2, 2, 16, 128, 128), np.float32)
    ct = np.zeros((2, 2, 16, 128, 128), np.float32)
    for g in range(G):
        j = g // 2
        r0 = (g % 8) * 16
        s0 = (g % 2) * 64
        for ri, (bb, cc) in enumerate(((b_re, c_re), (b_im, c_im))):
            braw[ri, :, j, r0:r0 + 16, s0:s0 + 64] = bb[:, g].transpose(0, 2, 1)
            ct[ri, :, j, s0:s0 + 64, r0:r0 + 16] = cc[:, g].transpose(0, 2, 1)
    out["s5braw"] = np.ascontiguousarray(braw.transpose(3, 0, 1, 2, 4))
    out["s5ct"] = np.ascontiguousarray(ct.transpose(3, 0, 1, 2, 4))
    return out


def prep_core(inp, core, consts):
    b, par = core // 2, core % 2
    x = inp["x"][b]
    ctx = inp["ctx"][b]
    if par:
        x = x[::-1]
        ctx = ctx[::-1]
    m = {}
    m["xin"] = np.ascontiguousarray(np.concatenate([x, ctx], axis=0))
    m["cvec"] = np.ascontiguousarray(np.stack([_cols(inp["c"][b], 16), _cols(inp["c_ctx"], 16)], axis=2))
    m["ada_w"] = np.ascontiguousarray(inp["ada_w"].reshape(2, 16, 128, 6144).transpose(0, 2, 1, 3))
    m["ada_b"] = np.stack([_cols(inp["ada_b"][l], 48) for l in range(2)], axis=1)
    m["pre_g"] = np.stack([_cols(inp["pre_g"][l], 16) for l in range(2)], axis=1)
    m["post_g"] = np.stack([_cols(inp["post_g"][l], 16) for l in range(2)], axis=1)
    m["w_in0"] = _mt(inp["ab_w_in"][0])
    m["w_out0"] = _kt(inp["ab_w_out"][0])
    cw = inp["conv_w"][0]
    if par:
        cw = cw[::-1]
    m["conv_w"] = np.ascontiguousarray(cw.T.reshape(8, 128, 31).transpose(1, 0, 2))
    m["conv_v"] = np.ascontiguousarray(np.stack([_cols(inp["conv_b"][0], 8), _cols(inp["conv_ln_g"][0], 8),
                                                 _cols(inp["conv_ln_b"][0], 8)], axis=1))
    fg = inp["fourier_g"][0]
    m["four_g"] = np.ascontiguousarray(fg.reshape(4, 2, 128).transpose(2, 0, 1))
    m["w_in1"] = _mt(inp["cd_w_in"][0])
    m["w_out1"] = _kt(inp["cd_w_out"][0])
    m["na_bias"] = consts["na_bias"][par]
    m["s5d"] = _cols(inp["s5_d"][0], 4)
    m["w_glu"] = _mt(inp["s5_w_glu"][0])
    m.update(_s5_layout(inp, par))
    cs256, cctx, fw1, fw3, ftw = consts["dft"][par]
    m["cs256"], m["cctx"], m["fw1"], m["fw3"], m["ftw"] = cs256, cctx, fw1, fw3, ftw
    m["ident"] = np.eye(128, dtype=np.float32)
    m["svals"] = np.ascontiguousarray(np.broadcast_to(np.arange(513, dtype=np.float32)[None], (128, 513)))
    return m


def build(debug=None):
    nc = bass.Bass("TRN2", target_bir_lowering=False)

    def din(name, shape, dtype=F32):
        return nc.dram_tensor(name, list(shape), dtype, kind="ExternalInput").ap()

    def dscr(name, shape, dtype):
        return nc.dram_tensor(name, list(shape), dtype, kind="Internal").ap()

    xin = din("xin", [TOK, D])
    cvec = din("cvec", [128, 16, 2])
    ada_w = din("ada_w", [2, 128, 16, 6144])
    ada_b = din("ada_b", [128, 2, 48])
    pre_g = din("pre_g", [128, 2, 16])
    post_g = din("post_g", [128, 2, 16])
    w_in0 = din("w_in0", [40, 128, 16, 128])
    w_out0 = din("w_out0", [128, 16, 2048])
    conv_w = din("conv_w", [128, 8, 31])
    conv_v = din("conv_v", [128, 3, 8])
    four_g = din("four_g", [128, 4, 2])
    w_in1 = din("w_in1", [56, 128, 16, 128])
    w_out1 = din("w_out1", [128, 16, 2048])
    na_bias = din("na_bias", [3, 12, 128, 5, 128])
    s5d = din("s5d", [128, 4])
    w_glu = din("w_glu", [4, 128, 4, 128])
    s5are = din("s5are", [128, 2, 16])
    s5aim = din("s5aim", [128, 2, 16])
    s5ldt = din("s5ldt", [128, 2, 16])
    s5braw = din("s5braw", [128, 2, 2, 16, 128])
    s5ct = din("s5ct", [128, 2, 2, 16, 128])
    cs256_d = din("cs256", [128, 2, 512], BF16)
    cctx_d = din("cctx", [128, 2, 512], BF16)
    w1_d = din("fw1", [128, 3, 128], BF16)
    w3_d = din("fw3", [128, 2, 128], BF16)
    twd_d = din("ftw", [128, 2, L], BF16)
    ident_d = din("ident", [128, 128])
    svals_d = din("svals", [128, 513])
    out_d = nc.dram_tensor("out", [OWN, D], F32, kind="ExternalOutput").ap()
    h1_kind = "ExternalOutput" if debug == "l0" else "Internal"
    h1_d = nc.dram_tensor("h1", [TOK, D], F32, kind=h1_kind).ap()
    uT_d = dscr("uT", [D, 4608], BF16)
    mixT_d = dscr("mixT", [D, TOK], BF16)
    convT_d = dscr("convT", [1024, TOK], BF16)

    with ExitStack() as top:
        sems = {}
        for e in ENGS:
            sems[e] = top.enter_context(nc.semaphore("s_" + e))
        for i in range(NDMA):
            sems[("dma", i)] = top.enter_context(nc.semaphore("d%d" % i))
        S = Sched(nc, sems)

        uid = [0]

        def sbuf(es, name, shape, dtype):
            uid[0] += 1
            return T(es.enter_context(nc.sbuf_tensor("%s_%d" % (name, uid[0]), list(shape), dtype)))

        def psum(es, name, shape, dtype):
            uid[0] += 1
            return T(es.enter_context(nc.psum_tensor("%s_%d" % (name, uid[0]), list(shape), dtype)))

        R_uT, R_mix, R_conv, R_h1 = T(None), T(None), T(None), T(None)

        identb = sbuf(top, "identb", [128, 128], BF16)
        identf = sbuf(top, "identf", [128, 128], F32)
        onesb = sbuf(top, "onesb", [128, 128], BF16)
        onesf = sbuf(top, "onesf", [128, 128], F32)
        modT = sbuf(top, "modT", [128, 2, 48, 2], F32)
        gsT = sbuf(top, "gsT", [128, 2, 2, 16], F32)
        shT = sbuf(top, "shT", [128, 2, 2, 16], F32)
        gpT = sbuf(top, "gpT", [128, 2, 2, 16], F32)
        preg = sbuf(top, "preg", [128, 2, 16], F32)
        postg = sbuf(top, "postg", [128, 2, 16], F32)

        def V(fn, reads, writes):
            return S.op("vector", fn, reads, writes)

        def A(fn, reads, writes):
            return S.op("scalar", fn, reads, writes)

        def G(fn, reads, writes):
            return S.op("gpsimd", fn, reads, writes)

        def MM(out, lhsT, rhs, start, stop, reads, writes):
            return S.op("tensor", lambda e: e.matmul(out, lhsT=lhsT, rhs=rhs, start=start, stop=stop),
                        reads, writes, pe_chain=True)

        def act(out, in_, func, reads, writes, **kw):
            return A(lambda e: e.activation(out=out, in_=in_, func=func, **kw), reads, writes)

        def tt(out, in0, in1, op, reads, writes, eng="vector"):
            return S.op(eng, lambda e: e.tensor_tensor(out=out, in0=in0, in1=in1, op=op), reads, writes)

        def ts(out, in0, s1, s2, op0, op1, reads, writes, eng="vector"):
            if op1 is None:
                return S.op(eng, lambda e: e.tensor_scalar(out=out, in0=in0, scalar1=s1, scalar2=None, op0=op0),
                            reads, writes)
            return S.op(eng, lambda e: e.tensor_scalar(out=out, in0=in0, scalar1=s1, scalar2=s2, op0=op0, op1=op1),
                        reads, writes)

        def stt(out, in0, scalar, in1, op0, op1, reads, writes):
            return V(lambda e: e.scalar_tensor_tensor(out=out, in0=in0, scalar=scalar, in1=in1, op0=op0, op1=op1),
                     reads, writes)

        def cp(out, in_, reads, writes, eng="vector"):
            return S.op(eng, lambda e: e.tensor_copy(out=out, in_=in_), reads, writes)

        def dma(out, in_, reads, writes, q=None, is_output=False):
            return S.dma(lambda e: e.dma_start(out=out, in_=in_), reads, writes, q=q, is_output=is_output)

        with ExitStack() as ph:
            condT = sbuf(ph, "condT", [128, 16, 2], F32)
            adab = sbuf(ph, "adab", [128, 2, 48], F32)
            aw = [sbuf(ph, "aw%d" % i, [128, 16, 512], F32) for i in range(2)]
            psA = psum(ph, "psA", [128, 2, 48, 2], F32)
            tmp16 = sbuf(ph, "tmp16", [128, 16], F32)
            dma(identf[:], ident_d[:, :], [], [identf], q="sync")
            dma(identb[:], ident_d[:, :], [], [identb], q="gpsimd")
            dma(condT[:], cvec[:, :, :], [], [condT], q="sync")
            dma(adab[:], ada_b[:, :, :], [], [adab], q="sync")
            dma(preg[:], pre_g[:, :, :], [], [preg], q="sync")
            dma(postg[:], post_g[:, :, :], [], [postg], q="sync")
            V(lambda e: e.memset(onesf[:], 1.0), [], [onesf])
            V(lambda e: e.memset(onesb[:], 1.0), [], [onesb])
            act(condT[:], condT[:], AF.Silu, [condT], [condT])
            for l in range(2):
                for cb in range(12):
                    w = aw[(l * 12 + cb) % 2]
                    dma(w[:], ada_w[l, :, :, cb * 512:(cb + 1) * 512], [], [w])
                    for m in range(4):
                        j = cb * 4 + m
                        for k in range(16):
                            MM(psA[:, l, j, :], w[:, k, m * 128:(m + 1) * 128], condT[:, k, :],
                               k == 0, k == 15, [w, condT], [psA])
                for i in range(2):
                    tt(modT[:, l, :, i], psA[:, l, :, i], adab[:, l, :], ALU.add, [psA, adab], [modT])
                for i in range(2):
                    stt(gsT[:, l, i, :], modT[:, l, 16:32, i], 1.0, preg[:, l, :], ALU.add, ALU.mult,
                        [modT, preg], [gsT])
                    cp(shT[:, l, i, :], modT[:, l, 0:16, i], [modT], [shT])
                    tt(gpT[:, l, i, :], modT[:, l, 32:48, i], postg[:, l, :], ALU.mult, [modT, postg], [gpT])
            S.flush()

        def phase_p1(l, h_d, R_h):
            with ExitStack() as ph:
                xs = [sbuf(ph, "xs%d" % i, [128, D], F32) for i in range(2)]
                xn = [sbuf(ph, "xn%d" % i, [128, D], BF16) for i in range(2)]
                junk = sbuf(ph, "junk", [128, D], BF16)
                st = [sbuf(ph, "ust%d" % i, [128, 16, 512], BF16) for i in range(2)]
                ss = sbuf(ph, "ss", [128, 34], F32)
                rs = sbuf(ph, "rs", [128, 34], F32)
                pT = [psum(ph, "pT%d" % i, [128, 16, 128], BF16) for i in range(2)]
                for tix in range(34):
                    i = 0 if tix < 32 else 1
                    x_ = xs[tix % 2]
                    xn_ = xn[tix % 2]
                    p_ = pT[tix % 2]
                    st_ = st[(tix // 4) % 2]
                    dma(x_[:], h_d[tix * 128:(tix + 1) * 128, :], [R_h], [x_])
                    act(junk[:], x_[:], AF.Square, [x_], [junk, ss], accum_out=ss[:, tix:tix + 1])
                    ts(rs[:, tix:tix + 1], ss[:, tix:tix + 1], 1.0 / D, EPS, ALU.mult, ALU.add, [ss], [rs])
                    act(rs[:, tix:tix + 1], rs[:, tix:tix + 1], AF.Sqrt, [rs], [rs])
                    V(lambda e, a=rs[:, tix:tix + 1]: e.reciprocal(out=a, in_=a), [rs], [rs])
                    act(xn_[:], x_[:], AF.Copy, [x_, rs], [xn_], scale=rs[:, tix:tix + 1])
                    for c in range(16):
                        S.op("tensor", lambda e, o=p_[:, c, :], a=xn_[:, c * 128:(c + 1) * 128]:
                             e.transpose(out=o, in_=a, identity=identb[:]), [xn_, identb], [p_], pe_chain=True)
                    q4 = tix % 4
                    for c in range(16):
                        ts(st_[:, c, q4 * 128:(q4 + 1) * 128], p_[:, c, :], gsT[:, l, i, c:c + 1],
                           shT[:, l, i, c:c + 1], ALU.mult, ALU.add, [p_, gsT, shT], [st_])
                    if q4 == 3 or tix == 33:
                        t0 = (tix // 4) * 512
                        n = (q4 + 1) * 128
                        dma(uT_d.rearrange("(c p) t -> p c t", p=128)[:, :, t0:t0 + n], st_[:, :, 0:n], [st_], [R_uT])
                S.flush()

        def load_w(wsb, wsrc, mtiles):
            for i, mt in enumerate(mtiles):
                dma(wsb[:, i, :, :], wsrc[mt, :, :, :], [], [wsb], q="gpsimd")

        rot = {"ps": 0, "ut": 0}

        def proj_fm(ph, wsrc, mtiles, chunks, epilogue, tag, ps_tiles, ut_tiles, wsb=None):
            nm = len(mtiles)
            if wsb is None:
                wsb = sbuf(ph, "w_" + tag, [128, nm, 16, 128], BF16)
                load_w(wsb, wsrc, mtiles)
            for ci, (t0, n) in enumerate(chunks):
                ut = ut_tiles[rot["ut"] % len(ut_tiles)]
                rot["ut"] += 1
                dma(ut[:, :, 0:n], uT_d.rearrange("(c p) t -> p c t", p=128)[:, :, t0:t0 + n], [R_uT], [ut], q="sync")
                for i in range(nm):
                    ps = ps_tiles[rot["ps"] % len(ps_tiles)]
                    rot["ps"] += 1
                    for k in range(16):
                        MM(ps[:, 0:n], wsb[:, i, k, :], ut[:, k, 0:n], k == 0, k == 15, [wsb, ut], [ps])
                    epilogue(i, ci, t0, n, ps)

        def phase_conv():
            with ExitStack() as ph:
                cw = sbuf(ph, "cw", [128, 8, 31], F32)
                cv = sbuf(ph, "cv", [128, 3, 8], F32)
                s1 = sbuf(ph, "s1", [128, TOK], F32)
                s2 = sbuf(ph, "s2", [128, TOK], F32)
                dma(cw[:], conv_w[:, :, :], [], [cw], q="sync")
                dma(cv[:], conv_v[:, :, :], [], [cv], q="sync")
                with ExitStack() as p1:
                    ut_tiles = [sbuf(p1, "ut%d" % i, [128, 16, 512], BF16) for i in range(2)]
                    ps_tiles = [psum(p1, "ps%d" % i, [128, 512], F32) for i in range(4)]
                    pst = [psum(p1, "pst%d" % i, [128, 512], F32) for i in range(2)]
                    apad = [sbuf(p1, "apad%d" % i, [128, 4400], BF16) for i in range(2)]
                    dgk = [sbuf(p1, "dgk%d" % i, [128, 31, 128], BF16) for i in range(2)]
                    pcv = [psum(p1, "pcv%d" % i, [128, 512], F32) for i in range(2)]
                    cvbs = [sbuf(p1, "cvb%d" % i, [128, TOK], BF16) for i in range(2)]
                    sqbs = [sbuf(p1, "sqb%d" % i, [128, TOK], BF16) for i in range(2)]
                    wA = [sbuf(p1, "wA%d" % i, [128, 2, 16, 128], BF16) for i in range(2)]
                    sig = [sbuf(p1, "sig%d" % i, [128, 512], BF16) for i in range(2)]
                    for a_ in apad:
                        V(lambda e, a_=a_: e.memset(a_[:], 0.0), [], [a_])
                    for c in range(8):
                        ap_ = apad[c % 2]
                        dg_ = dgk[c % 2]
                        cvb, sqb = cvbs[c % 2], sqbs[c % 2]
                        load_w(wA[c % 2], w_in0, [8 + c, c])
                        tt(dg_[:], identb[:].unsqueeze(1).to_broadcast([128, 31, 128]),
                           cw[:, c, :].unsqueeze(2).to_broadcast([128, 31, 128]), ALU.mult, [identb, cw], [dg_])

                        def epi(i, ci, t0, n, ps, ap_=ap_):
                            sg = sig[ci % 2]
                            if i == 0:
                                act(sg[:, 0:n], ps[:, 0:n], AF.Sigmoid, [ps], [sg])
                            else:
                                off = 15 + t0 if t0 < L else 4126
                                tt(ap_[:, off:off + n], ps[:, 0:n], sg[:, 0:n], ALU.mult, [ps, sg], [ap_])

                        if True:
                            proj_fm(p1, w_in0, [8 + c, c], CHUNKS, epi, "a1", ps_tiles, ut_tiles, wsb=wA[c % 2])
                            for ci, (t0, n) in enumerate(CHUNKS):
                                i0 = t0 if t0 < L else 4111
                                pc_ = pcv[ci % 2]
                                for k in range(31):
                                    MM(pc_[:, 0:n], dg_[:, k, :], ap_[:, i0 + k:i0 + k + n], k == 0, k == 30,
                                       [dg_, ap_], [pc_])
                                act(cvb[:, t0:t0 + n], pc_[:, 0:n], AF.Identity, [pc_, cv], [cvb], bias=cv[:, 0, c:c + 1])
                                act(sqb[:, t0:t0 + n], pc_[:, 0:n], AF.Square, [pc_, cv], [sqb], bias=cv[:, 0, c:c + 1])
                            for ci, (t0, n) in enumerate(CHUNKS):
                                MM(pst[0][:, 0:n], onesb[:], cvb[:, t0:t0 + n], True, True, [onesb, cvb], [pst[0]])
                                MM(pst[1][:, 0:n], onesb[:], sqb[:, t0:t0 + n], True, True, [onesb, sqb], [pst[1]])
                                if c == 0:
                                    cp(s1[:, t0:t0 + n], pst[0][:, 0:n], [pst[0]], [s1])
                                    cp(s2[:, t0:t0 + n], pst[1][:, 0:n], [pst[1]], [s2])
                                else:
                                    tt(s1[:, t0:t0 + n], pst[0][:, 0:n], s1[:, t0:t0 + n], ALU.add, [pst[0], s1], [s1])
                                    tt(s2[:, t0:t0 + n], pst[1][:, 0:n], s2[:, t0:t0 + n], ALU.add, [pst[1], s2], [s2])
                            dma(convT_d[c * 128:(c + 1) * 128, :], cvb[:], [cvb], [R_conv], q="sync")
                    msq = sbuf(p1, "msq", [128, 512], F32)
                    for (t0, n) in CHUNKS:
                        ts(s1[:, t0:t0 + n], s1[:, t0:t0 + n], 1.0 / 1024, None, ALU.mult, None, [s1], [s1])
                        tt(msq[:, 0:n], s1[:, t0:t0 + n], s1[:, t0:t0 + n], ALU.mult, [s1], [msq])
                        stt(s2[:, t0:t0 + n], s2[:, t0:t0 + n], 1.0 / 1024, msq[:, 0:n], ALU.mult, ALU.subtract,
                            [s2, msq], [s2])
                    ts(s2[:], s2[:], EPS, None, ALU.add, None, [s2], [s2])
                    act(s2[:], s2[:], AF.Sqrt, [s2], [s2])
                    V(lambda e: e.reciprocal(out=s2[:], in_=s2[:]), [s2], [s2])
                    S.flush()
                with ExitStack() as p2:
                    ut_tiles = [sbuf(p2, "ut%d" % i, [128, 16, 512], BF16) for i in range(2)]
                    ps_tiles = [psum(p2, "ps%d" % i, [128, 512], F32) for i in range(4)]
                    cvl = [sbuf(p2, "cvl%d" % i, [128, TOK], BF16) for i in range(2)]
                    sgt = [sbuf(p2, "sgt%d" % i, [128, 512], F32) for i in range(2)]
                    t1 = [sbuf(p2, "t1%d" % i, [128, 512], F32) for i in range(2)]
                    mo = [sbuf(p2, "mo%d" % i, [128, 512], BF16) for i in range(2)]
                    wG = [sbuf(p2, "wG%d" % i, [128, 1, 16, 128], BF16) for i in range(2)]
                    for c in range(8):
                        cvl_ = cvl[c % 2]
                        load_w(wG[c % 2], w_in0, [16 + c])
                        dma(cvl_[:], convT_d[c * 128:(c + 1) * 128, :], [R_conv], [cvl_], q="sync")

                        def epi(i, ci, t0, n, ps, c=c, cvl_=cvl_):
                            sg, t1_, mo_ = sgt[ci % 2], t1[ci % 2], mo[ci % 2]
                            act(sg[:, 0:n], ps[:, 0:n], AF.Silu, [ps], [sg])
                            tt(t1_[:, 0:n], cvl_[:, t0:t0 + n], s1[:, t0:t0 + n], ALU.subtract, [cvl_, s1], [t1_])
                            tt(t1_[:, 0:n], t1_[:, 0:n], s2[:, t0:t0 + n], ALU.mult, [t1_, s2], [t1_])
                            act(t1_[:, 0:n], t1_[:, 0:n], AF.Silu, [t1_, cv], [t1_],
                                scale=cv[:, 1, c:c + 1], bias=cv[:, 2, c:c + 1])
                            tt(mo_[:, 0:n], t1_[:, 0:n], sg[:, 0:n], ALU.mult, [t1_, sg], [mo_])
                            dma(mixT_d[c * 128:(c + 1) * 128, t0:t0 + n], mo_[:, 0:n], [mo_], [R_mix], q="gpsimd")

                        proj_fm(p2, w_in0, [16 + c], CHUNKS, epi, "a2", ps_tiles, ut_tiles, wsb=wG[c % 2])
                    S.flush()

        def phase_fourier():
            with ExitStack() as ph:
                fg = sbuf(ph, "fg", [128, 4, 2], F32)
                cs256 = sbuf(ph, "cs256", [128, 2, 512], BF16)
                cctx = sbuf(ph, "cctx", [128, 2, 512], BF16)
                bnT = sbuf(ph, "bnT", [128, 2, TOK], BF16)
                sgT = sbuf(ph, "sgT", [128, 2, TOK], BF16)
                dma(fg[:], four_g[:, :, :], [], [fg], q="sync")
                dma(cs256[:], cs256_d[:, :, :], [], [cs256], q="sync")
                dma(cctx[:], cctx_d[:, :, :], [], [cctx], q="sync")
                w1t = sbuf(ph, "w1t", [128, 3, 128], BF16)
                w3t = sbuf(ph, "w3t", [128, 2, 128], BF16)
                twd = sbuf(ph, "twd", [128, 2, L], BF16)
                dma(w1t[:], w1_d[:, :, :], [], [w1t], q="sync")
                dma(w3t[:], w3_d[:, :, :], [], [w3t], q="sync")
                dma(twd[:], twd_d[:, :, :], [], [twd], q="sync")
                wB = [sbuf(ph, "wB%d" % i, [128, 4, 16, 128], BF16) for i in range(2)]
                fmt = lambda g_: [24 + 2 * g_, 25 + 2 * g_, 32 + 2 * g_, 33 + 2 * g_]
                load_w(wB[0], w_in0, fmt(0))
                for g in range(4):
                    with ExitStack() as p1:
                        ut_tiles = [sbuf(p1, "ut%d" % i, [128, 16, 512], BF16) for i in range(2)]
                        ps_tiles = [psum(p1, "ps%d" % i, [128, 512], F32) for i in range(6)]
                        pss = [psum(p1, "pss%d" % i, [128, 512], F32) for i in range(2)]
                        sq = [sbuf(p1, "sq%d" % i, [128, 512], BF16) for i in range(4)]
                        rst = [sbuf(p1, "rst%d" % i, [128, 512], F32) for i in range(2)]
                        held = {}

                        def epi(i, ci, t0, n, ps, g=g):
                            if i >= 2:
                                act(sgT[:, i - 2, t0:t0 + n], ps[:, 0:n], AF.Silu, [ps], [sgT])
                                return
                            sq_ = sq[(ci % 2) * 2 + i]
                            act(sq_[:, 0:n], ps[:, 0:n], AF.Square, [ps], [sq_])
                            held[i] = (ps, sq_)
                            if i == 1:
                                pss_ = pss[ci % 2]
                                rst_ = rst[ci % 2]
                                for jj in range(2):
                                    MM(pss_[:, 0:n], onesb[:], held[jj][1][:, 0:n], jj == 0, jj == 1,
                                       [onesb, held[jj][1]], [pss_])
                                ts(rst_[:, 0:n], pss_[:, 0:n], 1.0 / 256, EPS, ALU.mult, ALU.add, [pss_], [rst_])
                                act(rst_[:, 0:n], rst_[:, 0:n], AF.Sqrt, [rst_], [rst_])
                                V(lambda e, a=rst_[:, 0:n]: e.reciprocal(out=a, in_=a), [rst_], [rst_])
                                for jj in range(2):
                                    if t0 < L:
                                        o_ = bnT[:, jj, 0:L].rearrange("p (b a) -> p a b", a=64)[:, 8 * ci:8 * ci + 8, :]
                                        stt(o_, held[jj][0][:, 0:512].rearrange("p (a b) -> p a b", b=64),
                                            fg[:, g, jj:jj + 1], rst_[:, 0:512].rearrange("p (a b) -> p a b", b=64),
                                            ALU.mult, ALU.mult, [held[jj][0], fg, rst_], [bnT])
                                    else:
                                        stt(bnT[:, jj, t0:t0 + n], held[jj][0][:, 0:n], fg[:, g, jj:jj + 1],
                                            rst_[:, 0:n], ALU.mult, ALU.mult, [held[jj][0], fg, rst_], [bnT])

                        proj_fm(p1, w_in0, fmt(g), CHUNKS, epi, "b1", ps_tiles, ut_tiles, wsb=wB[g % 2])
                        S.flush()
                    if g < 3:
                        load_w(wB[(g + 1) % 2], w_in0, fmt(g + 1))
                    with ExitStack() as p2:
                        YB = sbuf(p2, "YB", [128, 32, 512], BF16)
                        Yc_ = sbuf(p2, "Yc_", [128, 2, 512], BF16)
                        Bsb = sbuf(p2, "Bsb", [128, 2, 2, L], BF16)
                        fo = [sbuf(p2, "fo%d" % i, [128, L], BF16) for i in range(2)]
                        foc = sbuf(p2, "foc", [128, 256], BF16)
                        m1 = [sbuf(p2, "fm1%d" % i, [128, 512], F32) for i in range(2)]
                        m2 = [sbuf(p2, "fm2%d" % i, [128, 512], F32) for i in range(2)]
                        psy = [psum(p2, "psy%d" % i, [128, 512], F32) for i in range(2)]
                        pa = [psum(p2, "pa%d" % i, [128, 4, 128], F32) for i in range(4)]
                        ptr = [psum(p2, "ptr%d" % i, [128, 8, 128], BF16) for i in range(2)]
                        for beta in range(32):
                            p_ = psy[beta % 2]
                            for jj in range(2):
                                MM(p_[:], bnT[:, jj, beta * 128:(beta + 1) * 128], cs256[:, jj, :], jj == 0, jj == 1,
                                   [bnT, cs256], [p_])
                            if beta % 2 == 0:
                                cp(YB[:, beta, :], p_[:], [p_], [YB])
                            else:
                                act(YB[:, beta, :], p_[:], AF.Copy, [p_], [YB])
                        for t_ in range(2):
                            p_ = psy[t_ % 2]
                            for jj in range(2):
                                MM(p_[:], bnT[:, jj, L + t_ * 128:L + (t_ + 1) * 128], cs256[:, jj, :], jj == 0, jj == 1,
                                   [bnT, cs256], [p_])
                            cp(Yc_[:, t_, :], p_[:], [p_], [Yc_])
                        sc_x = 1.0 / math.sqrt(L * 256.0)
                        sc_c = 1.0 / math.sqrt(LC * 256.0)
                        kq = 0
                        for kk in range(2):
                            for bq in range(8):
                                par_, pai_ = pa[(kq % 2) * 2], pa[(kq % 2) * 2 + 1]
                                m1_, m2_ = m1[kq % 2], m2[kq % 2]
                                kq += 1
                                for q in range(4):
                                    beta = bq * 4 + q
                                    yc = YB[:, beta, kk * 128:(kk + 1) * 128]
                                    ys = YB[:, beta, 256 + kk * 128:256 + (kk + 1) * 128]
                                    MM(par_[:, q, :], yc, w1t[:, 0, :], True, False, [YB, w1t], [par_])
                                    MM(par_[:, q, :], ys, w1t[:, 1, :], False, True, [YB, w1t], [par_])
                                    MM(pai_[:, q, :], yc, w1t[:, 1, :], True, False, [YB, w1t], [pai_])
                                    MM(pai_[:, q, :], ys, w1t[:, 2, :], False, True, [YB, w1t], [pai_])
                                sl = slice(bq * 512, (bq + 1) * 512)
                                arv = par_[:].rearrange("p a b -> p (a b)")
                                aiv = pai_[:].rearrange("p a b -> p (a b)")
                                tt(m1_[:], arv, twd[:, 0, sl], ALU.mult, [par_, twd], [m1_])
                                tt(m2_[:], aiv, twd[:, 1, sl], ALU.mult, [pai_, twd], [m2_])
                                ob = lambda ri: Bsb[:, kk, ri, :].rearrange("p (m b) -> p b m", b=64)[:, 8 * bq:8 * bq + 8, :]
                                v3 = lambda t_: t_[:].rearrange("p (b m) -> p b m", m=64)
                                tt(ob(0), v3(m1_), v3(m2_), ALU.add, [m1_, m2_], [Bsb], eng="gpsimd")
                                tt(m1_[:], aiv, twd[:, 0, sl], ALU.mult, [pai_, twd], [m1_])
                                tt(m2_[:], arv, twd[:, 1, sl], ALU.mult, [par_, twd], [m2_])
                                tt(ob(1), v3(m1_), v3(m2_), ALU.subtract, [m1_, m2_], [Bsb], eng="gpsimd")
                        BT = YB
                        ke = 0
                        for mq in range(8):
                            for ri in range(2):
                                pt_ = ptr[ke % 2]
                                for q in range(4):
                                    mu = mq * 4 + q
                                    for kk in range(2):
                                        src = Bsb[:, kk, ri, mu * 128:(mu + 1) * 128]
                                        S.op("tensor", lambda e, o=pt_[:, q * 2 + kk, :], a=src:
                                             e.transpose(out=o, in_=a, identity=identb[:]), [Bsb, identb], [pt_],
                                             pe_chain=True)
                                dst = BT[:, mq * 4:(mq + 1) * 4, ri * 256:(ri + 1) * 256]
                                srcp = pt_[:].rearrange("p (q k) c -> p q (k c)", k=2)
                                if ke % 2 == 0:
                                    cp(dst, srcp, [pt_], [BT])
                                else:
                                    act(dst, srcp, AF.Copy, [pt_], [BT])
                                ke += 1
                        kq = 0
                        for kk in range(2):
                            fo_ = fo[kk]
                            fov = fo_[:].rearrange("p (mb ma) -> p ma mb", ma=64)
                            sgv = sgT[:, kk, 0:L].rearrange("p (mb ma) -> p ma mb", ma=64)
                            for mq in range(8):
                                pf_ = pa[kq % 4]
                                kq += 1
                                for q in range(4):
                                    mu = mq * 4 + q
                                    MM(pf_[:, q, :], BT[:, mu, kk * 128:(kk + 1) * 128], w3t[:, 0, :], True, False,
                                       [BT, w3t], [pf_])
                                    MM(pf_[:, q, :], BT[:, mu, 256 + kk * 128:256 + (kk + 1) * 128], w3t[:, 1, :],
                                       False, True, [BT, w3t], [pf_])
                                stt(fov[:, mq * 8:(mq + 1) * 8, :], pf_[:].rearrange("p q (l m) -> p (q l) m", l=2), sc_x,
                                    sgv[:, mq * 8:(mq + 1) * 8, :], ALU.mult, ALU.mult, [pf_, sgT], [fo_])
                            r0 = 1024 + g * 256 + kk * 128
                            dma(mixT_d[r0:r0 + 128, 0:L], fo_[:], [fo_], [R_mix], q="sync")
                        for kk in range(2):
                            pc = psy[kk]
                            for t_ in range(2):
                                MM(pc[:, 0:256], Yc_[:, t_, kk * 128:(kk + 1) * 128], cctx[:, t_, 0:256],
                                   t_ == 0, False, [Yc_, cctx], [pc])
                                MM(pc[:, 0:256], Yc_[:, t_, 256 + kk * 128:256 + (kk + 1) * 128], cctx[:, t_, 256:512],
                                   False, t_ == 1, [Yc_, cctx], [pc])
                            stt(foc[:], pc[:, 0:256], sc_c, sgT[:, kk, L:TOK], ALU.mult, ALU.mult, [pc, sgT], [foc])
                            r0 = 1024 + g * 256 + kk * 128
                            dma(mixT_d[r0:r0 + 128, L:TOK], foc[:], [foc], [R_mix], q="gpsimd")
                        S.flush()

        def phase_out(l, wout_d, hin_d, R_hin, ntiles, hout_d, R_hout, is_output):
            with ExitStack() as ph:
                wo = sbuf(ph, "wo", [128, 16, D], BF16)
                for k4 in range(4):
                    dma(wo[:, k4 * 4:(k4 + 1) * 4, :], wout_d[:, k4 * 4:(k4 + 1) * 4, :], [], [wo], q="gpsimd")
                gbc = [sbuf(ph, "gbc%d" % i, [128, D], F32) for i in range(2)]
                dg = sbuf(ph, "dg", [128, 128], F32)
                mx = [sbuf(ph, "mx%d" % i, [128, 16, 512], BF16) for i in range(2)]
                hr = [sbuf(ph, "hr%d" % i, [128, D], F32) for i in range(2)]
                tm = [sbuf(ph, "tm%d" % i, [128, D], F32) for i in range(2)]
                junk = sbuf(ph, "junk", [128, D], BF16)
                ss = sbuf(ph, "ss", [128, 34], F32)
                po = [psum(ph, "po%d" % i, [128, 4, 512], F32) for i in range(2)]
                nseq = 2 if ntiles > 32 else 1
                for i in range(nseq):
                    for c in range(16):
                        ts(dg[:], identf[:], gpT[:, l, i, c:c + 1], None, ALU.mult, None, [identf, gpT], [dg])
                        MM(po[0][:, c // 4, (c % 4) * 128:(c % 4 + 1) * 128], onesf[:], dg[:], True, True,
                           [onesf, dg], [po[0]])
                    for q in range(4):
                        cp(gbc[i][:, q * 512:(q + 1) * 512], po[0][:, q, :], [po[0]], [gbc[i]])
                for tix in range(ntiles):
                    i = 0 if tix < 32 else 1
                    mx_ = mx[(tix // 4) % 2]
                    if tix % 4 == 0:
                        n = min(512, ntiles * 128 - tix * 128)
                        t0 = tix * 128
                        dma(mx_[:, :, 0:n], mixT_d.rearrange("(c p) t -> p c t", p=128)[:, :, t0:t0 + n],
                            [R_mix], [mx_], q="sync")
                    hr_, tm_, po_ = hr[tix % 2], tm[tix % 2], po[tix % 2]
                    dma(hr_[:], hin_d[tix * 128:(tix + 1) * 128, :], [R_hin], [hr_], q="sync")
                    q4 = tix % 4
                    for nn in range(4):
                        for k in range(16):
                            MM(po_[:, nn, :], mx_[:, k, q4 * 128:(q4 + 1) * 128], wo[:, k, nn * 512:(nn + 1) * 512],
                               k == 0, k == 15, [mx_, wo], [po_])
                    act(junk[:], po_[:].rearrange("p a b -> p (a b)"), AF.Square, [po_], [junk, ss],
                        accum_out=ss[:, tix:tix + 1])
                    ts(ss[:, tix:tix + 1], ss[:, tix:tix + 1], 1.0 / D, EPS, ALU.mult, ALU.add, [ss], [ss])
                    act(ss[:, tix:tix + 1], ss[:, tix:tix + 1], AF.Sqrt, [ss], [ss])
                    V(lambda e, a=ss[:, tix:tix + 1]: e.reciprocal(out=a, in_=a), [ss], [ss])
                    tt(tm_[:], po_[:].rearrange("p a b -> p (a b)"), gbc[i][:], ALU.mult, [po_, gbc[i]], [tm_])
                    stt(tm_[:], tm_[:], ss[:, tix:tix + 1], hr_[:], ALU.mult, ALU.add, [tm_, ss, hr_], [tm_])
                    dma(hout_d[tix * 128:(tix + 1) * 128, :], tm_[:], [tm_], [R_hout], q="gpsimd", is_output=is_output)
                S.flush()

        def phase_na():
            with ExitStack() as ph:
                ur = sbuf(ph, "ur", [128, 16, 2560], BF16)
                uv = uT_d.rearrange("(c p) t -> p c t", p=128)
                for q in range(4):
                    dma(ur[:, q * 4:(q + 1) * 4, 0:NKV], uv[:, q * 4:(q + 1) * 4, 0:NKV], [R_uT], [ur])
                dma(ur[:, :, NKV:2560], uv[:, :, L:TOK], [R_uT], [ur], q="sync")
                wq = [sbuf(ph, "wq%d" % i, [128, 4, 16, 128], BF16) for i in range(2)]
                bt = [sbuf(ph, "bt%d" % i, [128, 3, 5, 128], BF16) for i in range(2)]
                qT = [sbuf(ph, "qT%d" % i, [128, OWN], BF16) for i in range(2)]
                kT = [sbuf(ph, "kT%d" % i, [128, 2560], BF16) for i in range(2)]
                sg = [sbuf(ph, "sg%d" % i, [128, OWN], BF16) for i in range(2)]
                Vh = [sbuf(ph, "Vh%d" % i, [128, 20, 128], BF16) for i in range(2)]
                PT = [sbuf(ph, "PT%d" % i, [128, 7, 128], BF16) for i in range(2)]
                rd = [sbuf(ph, "rd%d" % i, [128, 128], F32) for i in range(2)]
                ot = [sbuf(ph, "ot%d" % i, [128, 128], F32) for i in range(2)]
                naT = [sbuf(ph, "naT%d" % i, [128, OWN], BF16) for i in range(2)]
                pp = [psum(ph, "pp%d" % i, [128, 512], F32) for i in range(2)]
                pS = [psum(ph, "pS%d" % i, [128, 8, 128], F32) for i in range(2)]
                pO = [psum(ph, "pO%d" % i, [128, 2, 128], F32) for i in range(2)]
                kp = 0
                isq = 1.0 / math.sqrt(128.0)
                for h in range(12):
                    w_ = wq[h % 2]
                    bt_ = bt[h % 2]
                    for i, mt in enumerate((h, 12 + h, 24 + h, 36 + h)):
                        dma(w_[:, i, :, :], w_in1[mt, :, :, :], [], [w_], q="gpsimd")
                    dma(bt_[:], na_bias[:, h, :, :, :].rearrange("c p k q -> p c k q"), [], [bt_], q="gpsimd")
                    qT_, kT_, sg_, Vh_, naT_ = qT[h % 2], kT[h % 2], sg[h % 2], Vh[h % 2], naT[h % 2]
                    for ci in range(5):
                        t0 = ci * 512
                        ps = pp[kp % 2]; kp += 1
                        for k in range(16):
                            MM(ps[:], w_[:, 1, k, :], ur[:, k, t0:t0 + 512], k == 0, k == 15, [w_, ur], [ps])
                        cp(kT_[:, t0:t0 + 512], ps[:], [ps], [kT_])
                        if ci < 4:
                            ps = pp[kp % 2]; kp += 1
                            for k in range(16):
                                MM(ps[:], w_[:, 0, k, :], ur[:, k, t0:t0 + 512], k == 0, k == 15, [w_, ur], [ps])
                            act(qT_[:, t0:t0 + 512], ps[:], AF.Copy, [ps], [qT_], scale=isq)
                            ps = pp[kp % 2]; kp += 1
                            for k in range(16):
                                MM(ps[:], w_[:, 3, k, :], ur[:, k, t0:t0 + 512], k == 0, k == 15, [w_, ur], [ps])
                            act(sg_[:, t0:t0 + 512], ps[:], AF.Silu, [ps], [sg_])
                    for t4 in range(5):
                        ps = pp[kp % 2]; kp += 1
                        for q in range(4):
                            tix = t4 * 4 + q
                            for k in range(16):
                                MM(ps[:, q * 128:(q + 1) * 128], ur[:, k, tix * 128:(tix + 1) * 128], w_[:, 2, k, :],
                                   k == 0, k == 15, [ur, w_], [ps])
                        cp(Vh_[:, t4 * 4:(t4 + 1) * 4, :], ps[:].rearrange("p (a b) -> p a b", b=128), [ps], [Vh_])
                    for j in range(16):
                        cls = min(j, 2)
                        ws = min(max(2 * j - 4, 0), 26)
                        pS_, pO_, PT_ = pS[j % 2], pO[j % 2], PT[j % 2]
                        rd_, ot_ = rd[j % 2], ot[j % 2]
                        for kt in range(7):
                            k0 = ws * 64 + kt * 128 if kt < 5 else NKV + (kt - 5) * 128
                            MM(pS_[:, kt, :], kT_[:, k0:k0 + 128], qT_[:, j * 128:(j + 1) * 128], True, kt >= 5,
                               [kT_, qT_], [pS_])
                            if kt < 5:
                                MM(pS_[:, kt, :], identb[:], bt_[:, cls, kt, :], False, True, [identb, bt_], [pS_])
                        act(PT_[:, 0:4, :], pS_[:, 0:4, :], AF.Exp, [pS_], [PT_])
                        act(PT_[:, 4:7, :], pS_[:, 4:7, :], AF.Exp, [pS_], [PT_])
                        for kt in range(7):
                            vt = ws // 2 + kt if kt < 5 else 18 + (kt - 5)
                            MM(pO_[:, 0, :], Vh_[:, vt, :], PT_[:, kt, :], kt == 0, kt == 6, [Vh_, PT_], [pO_])
                        for kt in range(7):
                            MM(pO_[:, 1, :], onesb[:], PT_[:, kt, :], kt == 0, kt == 6, [onesb, PT_], [pO_])
                        V(lambda e, o=rd_[:], a=pO_[:, 1, :]: e.reciprocal(out=o, in_=a), [pO_], [rd_])
                        tt(ot_[:], pO_[:, 0, :], rd_[:], ALU.mult, [pO_, rd_], [ot_])
                        tt(naT_[:, j * 128:(j + 1) * 128], ot_[:], sg_[:, j * 128:(j + 1) * 128], ALU.mult,
                           [ot_, sg_], [naT_])
                    dma(mixT_d[h * 128:(h + 1) * 128, 0:OWN], naT_[:], [naT_], [R_mix], q="sync")
                S.flush()

        def phase_s5():
            with ExitStack() as ph:
                dT = sbuf(ph, "dT", [128, 4, TOK], BF16)
                sdg = sbuf(ph, "sdg", [128, 4, OWN], BF16)
                ysb = sbuf(ph, "ysb", [128, 4, OWN], F32)
                with ExitStack() as p1:
                    ut_tiles = [sbuf(p1, "ut%d" % i, [128, 16, 512], BF16) for i in range(2)]
                    ps_tiles = [psum(p1, "ps%d" % i, [128, 512], F32) for i in range(4)]

                    def epi(i, ci, t0, n, ps):
                        if i < 4:
                            cp(dT[:, i, t0:t0 + n], ps[:, 0:n], [ps], [dT])
                        elif t0 < OWN:
                            act(sdg[:, i - 4, t0:t0 + n], ps[:, 0:n], AF.Silu, [ps], [sdg])

                    proj_fm(p1, w_in1, [48 + i for i in range(8)], CHUNKS, epi, "s5p", ps_tiles, ut_tiles)
                    S.flush()
                with ExitStack() as p2:
                    def small(name):
                        return sbuf(p2, name, [128, 2, 16], F32)
                    are, aim, ldt = small("are"), small("aim"), small("ldt")
                    dtt, rr, thp, cfr, cfi = small("dtt"), small("rr"), small("thp"), small("cfr"), small("cfi")
                    w1, w2, w3, w4 = small("w1"), small("w2"), small("w3"), small("w4")
                    wi = sbuf(p2, "wi", [128, 2, 16], I32)
                    dma(are[:], s5are[:, :, :], [], [are], q="sync")
                    dma(aim[:], s5aim[:, :, :], [], [aim], q="sync")
                    dma(ldt[:], s5ldt[:, :, :], [], [ldt], q="sync")
                    act(dtt[:], ldt[:], AF.Exp, [ldt], [dtt])
                    tt(w1[:], are[:], dtt[:], ALU.mult, [are, dtt], [w1])
                    act(rr[:], w1[:], AF.Exp, [w1], [rr])
                    tt(w1[:], aim[:], dtt[:], ALU.mult, [aim, dtt], [w1])
                    ts(thp[:], w1[:], 1.0 / (2.0 * math.pi), None, ALU.mult, None, [w1], [thp])

                    def sincos(src, sin_out, cos_out):
                        cp(wi[:], src[:], [src], [wi])
                        cp(w2[:], wi[:], [wi], [w2])
                        tt(w2[:], src[:], w2[:], ALU.subtract, [src, w2], [w2])
                        act(sin_out[:], w2[:], AF.Sin, [w2], [sin_out], scale=6.28318)
                        ts(w3[:], src[:], 0.25, None, ALU.add, None, [src], [w3])
                        cp(wi[:], w3[:], [w3], [wi])
                        cp(w2[:], wi[:], [wi], [w2])
                        tt(w2[:], w3[:], w2[:], ALU.subtract, [w3, w2], [w2])
                        act(cos_out[:], w2[:], AF.Sin, [w2], [cos_out], scale=6.28318)

                    sn, cs_ = small("sn"), small("cs_")
                    sincos(thp, sn, cs_)
                    tt(w1[:], rr[:], cs_[:], ALU.mult, [rr, cs_], [w1])
                    ts(w1[:], w1[:], -1.0, None, ALU.add, None, [w1], [w1])
                    tt(w4[:], rr[:], sn[:], ALU.mult, [rr, sn], [w4])
                    tt(w2[:], are[:], are[:], ALU.mult, [are], [w2])
                    tt(w3[:], aim[:], aim[:], ALU.mult, [aim], [w3])
                    tt(w2[:], w2[:], w3[:], ALU.add, [w2, w3], [w2])
                    V(lambda e: e.reciprocal(out=w2[:], in_=w2[:]), [w2], [w2])
                    tt(cfr[:], w1[:], are[:], ALU.mult, [w1, are], [cfr])
                    tt(w3[:], w4[:], aim[:], ALU.mult, [w4, aim], [w3])
                    tt(cfr[:], cfr[:], w3[:], ALU.add, [cfr, w3], [cfr])
                    tt(cfr[:], cfr[:], w2[:], ALU.mult, [cfr, w2], [cfr])
                    tt(cfi[:], w4[:], are[:], ALU.mult, [w4, are], [cfi])
                    tt(w3[:], w1[:], aim[:], ALU.mult, [w1, aim], [w3])
                    tt(cfi[:], cfi[:], w3[:], ALU.subtract, [cfi, w3], [cfi])
                    tt(cfi[:], cfi[:], w2[:], ALU.mult, [cfi, w2], [cfi])

                    braw = sbuf(p2, "braw", [128, 2, 2, 16, 128], BF16)
                    ctw = sbuf(p2, "ctw", [128, 2, 2, 16, 128], BF16)
                    dma(braw[:], s5braw[:, :, :, :, :], [], [braw], q="gpsimd")
                    dma(ctw[:], s5ct[:, :, :, :, :], [], [ctw], q="gpsimd")
                    ts(ctw[:, 1], ctw[:, 1], -1.0, None, ALU.mult, None, [ctw], [ctw])
                    sv = sbuf(p2, "sv", [128, 513], F32)
                    dma(sv[:], svals_d[:, :], [], [sv], q="sync")
                    ones5 = sbuf(p2, "ones5", [128, 512], F32)
                    V(lambda e: e.memset(ones5[:], 1.0), [], [ones5])
                    a1s = [sbuf(p2, "a1%d" % i, [128, 513], F32) for i in range(2)]
                    a2s = [sbuf(p2, "a2%d" % i, [128, 513], F32) for i in range(2)]
                    ais = [sbuf(p2, "ai%d" % i, [128, 513], I32) for i in range(2)]
                    sinTs = [sbuf(p2, "sinT%d" % i, [128, 513], F32) for i in range(2)]
                    cosTs = [sbuf(p2, "cosT%d" % i, [128, 513], F32) for i in range(2)]
                    Er = sbuf(p2, "Er", [128, 512], F32)
                    Ei = sbuf(p2, "Ei", [128, 512], F32)
                    rfill = sbuf(p2, "rfill", [128, 512], F32)
                    m1 = sbuf(p2, "m1", [128, 512], F32)
                    m2 = sbuf(p2, "m2", [128, 512], F32)
                    g1_ = sbuf(p2, "g1_", [128, 512], F32)
                    g2_ = sbuf(p2, "g2_", [128, 512], F32)
                    bpr = sbuf(p2, "bpr", [128, 512], F32)
                    bpi = sbuf(p2, "bpi", [128, 512], F32)
                    krs = [sbuf(p2, "kr%d" % i, [128, 512], F32) for i in range(2)]
                    kis = [sbuf(p2, "ki%d" % i, [128, 512], F32) for i in range(2)]
                    ini = sbuf(p2, "ini", [128, 4], F32)
                    hh = [sbuf(p2, "hh%d" % d_, [128, 2, OWN], BF16) for d_ in range(2)]
                    pbu = [psum(p2, "pbu%d" % i, [128, 512], F32) for i in range(4)]
                    py = [psum(p2, "py%d" % i, [128, 512], F32) for i in range(4)]
                    seqs = [
                        [(L, 256, False, None)] + [(i * 512, 512, False, i * 512) for i in range(4)],
                        [(L, 256, True, None)] + [(i * 512, 512, True, (i * 512 if i < 4 else None))
                                                  for i in range(7, -1, -1)],
                    ]
                    kb = 0
                    for j in range(16):
                        kc = j // 4
                        for d_ in range(2):
                            sinT, cosT = sinTs[d_], cosTs[d_]
                            a1, a2, ai = a1s[d_], a2s[d_], ais[d_]
                            ts(a1[:], sv[:], thp[:, d_, j:j + 1], None, ALU.mult, None, [sv, thp], [a1], eng="gpsimd")
                            cp(ai[:], a1[:], [a1], [ai])
                            cp(a2[:], ai[:], [ai], [a2])
                            tt(a2[:], a1[:], a2[:], ALU.subtract, [a1, a2], [a2], eng="gpsimd")
                            act(sinT[:], a2[:], AF.Sin, [a2], [sinT], scale=6.28318)
                            ts(a1[:], a1[:], 0.25, None, ALU.add, None, [a1], [a1], eng="gpsimd")
                            cp(ai[:], a1[:], [a1], [ai])
                            cp(a2[:], ai[:], [ai], [a2])
                            tt(a2[:], a1[:], a2[:], ALU.subtract, [a1, a2], [a2], eng="gpsimd")
                            act(cosT[:], a2[:], AF.Sin, [a2], [cosT], scale=6.28318)
                            ts(m1[:], sinT[:, 0:512], cfi[:, d_, j:j + 1], None, ALU.mult, None, [sinT, cfi], [m1])
                            stt(Er[:], cosT[:, 0:512], cfr[:, d_, j:j + 1], m1[:], ALU.mult, ALU.add,
                                [cosT, cfr, m1], [Er])
                            ts(m1[:], sinT[:, 0:512], cfr[:, d_, j:j + 1], None, ALU.mult, None, [sinT, cfr], [m1])
                            stt(Ei[:], cosT[:, 0:512], cfi[:, d_, j:j + 1], m1[:], ALU.mult, ALU.subtract,
                                [cosT, cfi, m1], [Ei])
                            ts(rfill[:], ones5[:], rr[:, d_, j:j + 1], None, ALU.mult, None, [ones5, rr], [rfill])
                            first = True
                            for (m0, n, rev, own) in seqs[d_]:
                                pr, pi = pbu[kb % 4], pbu[(kb + 1) % 4]
                                kr, ki = krs[(kb // 2) % 2], kis[(kb // 2) % 2]
                                kb += 2
                                MM(pr[:, 0:n], braw[:, 0, d_, j, :], dT[:, kc, m0:m0 + n], True, True, [braw, dT], [pr])
                                MM(pi[:, 0:n], braw[:, 1, d_, j, :], dT[:, kc, m0:m0 + n], True, True, [braw, dT], [pi])
                                ur_ = pr[:, 0:n][:, ::-1] if rev else pr[:, 0:n]
                                ui_ = pi[:, 0:n][:, ::-1] if rev else pi[:, 0:n]
                                tt(m1[:, 0:n], ur_, Er[:, 0:n], ALU.mult, [pr, Er], [m1])
                                tt(m2[:, 0:n], ui_, Ei[:, 0:n], ALU.mult, [pi, Ei], [m2])
                                tt(bpr[:, 0:n], m1[:, 0:n], m2[:, 0:n], ALU.subtract, [m1, m2], [bpr])
                                tt(m1[:, 0:n], ui_, Er[:, 0:n], ALU.mult, [pi, Er], [m1])
                                tt(m2[:, 0:n], ur_, Ei[:, 0:n], ALU.mult, [pr, Ei], [m2])
                                tt(bpi[:, 0:n], m1[:, 0:n], m2[:, 0:n], ALU.add, [m1, m2], [bpi])
                                i_r = 0.0 if first else ini[:, 0:1]
                                i_i = 0.0 if first else ini[:, 1:2]
                                V(lambda e, o=kr[:, 0:n], a=rfill[:, 0:n], b=bpr[:, 0:n], iv=i_r:
                                  e.tensor_tensor_scan(out=o, data0=a, data1=b, initial=iv, op0=ALU.mult, op1=ALU.add),
                                  [rfill, bpr, ini], [kr])
                                V(lambda e, o=ki[:, 0:n], a=rfill[:, 0:n], b=bpi[:, 0:n], iv=i_i:
                                  e.tensor_tensor_scan(out=o, data0=a, data1=b, initial=iv, op0=ALU.mult, op1=ALU.add),
                                  [rfill, bpi, ini], [ki])
                                first = False
                                tt(ini[:, 2:3], ki[:, n - 1:n], sinT[:, n:n + 1], ALU.mult, [ki, sinT], [ini])
                                tt(ini[:, 3:4], kr[:, n - 1:n], sinT[:, n:n + 1], ALU.mult, [kr, sinT], [ini])
                                stt(ini[:, 0:1], kr[:, n - 1:n], cosT[:, n:n + 1], ini[:, 2:3], ALU.mult, ALU.subtract,
                                    [kr, cosT, ini], [ini])
                                stt(ini[:, 1:2], ki[:, n - 1:n], cosT[:, n:n + 1], ini[:, 3:4], ALU.mult, ALU.add,
                                    [ki, cosT, ini], [ini])
                                if own is not None:
                                    h_ = hh[d_]
                                    o_r = h_[:, 0, own:own + n]
                                    o_i = h_[:, 1, own:own + n]
                                    if rev:
                                        o_r = o_r[:, ::-1]
                                        o_i = o_i[:, ::-1]
                                    tt(g1_[:, 0:n], cosT[:, 0:n], kr[:, 0:n], ALU.mult, [cosT, kr], [g1_], eng="gpsimd")
                                    tt(g2_[:, 0:n], sinT[:, 0:n], ki[:, 0:n], ALU.mult, [sinT, ki], [g2_], eng="gpsimd")
                                    tt(o_r, g1_[:, 0:n], g2_[:, 0:n], ALU.subtract, [g1_, g2_], [h_], eng="gpsimd")
                                    tt(g1_[:, 0:n], cosT[:, 0:n], ki[:, 0:n], ALU.mult, [cosT, ki], [g1_], eng="gpsimd")
                                    tt(g2_[:, 0:n], sinT[:, 0:n], kr[:, 0:n], ALU.mult, [sinT, kr], [g2_], eng="gpsimd")
                                    tt(o_i, g1_[:, 0:n], g2_[:, 0:n], ALU.add, [g1_, g2_], [h_], eng="gpsimd")
                        for tc in range(4):
                            for d_ in range(2):
                                for ri in range(2):
                                    MM(py[tc][:], ctw[:, ri, d_, j, :], hh[d_][:, ri, tc * 512:(tc + 1) * 512],
                                       (j % 4 == 0 and d_ == 0 and ri == 0), (j % 4 == 3 and d_ == 1 and ri == 1),
                                       [ctw, hh[d_]], [py[tc]])
                        if j % 4 == 3:
                            for tc in range(4):
                                cp(ysb[:, kc, tc * 512:(tc + 1) * 512], py[tc][:], [py[tc]], [ysb])
                    S.flush()
                with ExitStack() as p2:
                    pbu = [psum(p2, "pbu%d" % i, [128, 512], F32) for i in range(4)]
                    dcol = sbuf(p2, "dcol", [128, 4], F32)
                    dma(dcol[:], s5d[:, :], [], [dcol], q="sync")
                    wg = sbuf(p2, "wg", [128, 4, 4, 128], BF16)
                    for mt in range(4):
                        dma(wg[:, mt, :, :], w_glu[mt, :, :, :], [], [wg], q="gpsimd")
                    yb = sbuf(p2, "yb", [128, 4, OWN], BF16)
                    g1 = sbuf(p2, "g1", [128, OWN], F32)
                    g2 = sbuf(p2, "g2", [128, OWN], F32)
                    for kc in range(4):
                        stt(ysb[:, kc, :], dT[:, kc, 0:OWN], dcol[:, kc:kc + 1], ysb[:, kc, :], ALU.mult, ALU.add,
                            [dT, dcol, ysb], [ysb])
                        tt(g1[:], ysb[:, kc, :], ysb[:, kc, :], ALU.mult, [ysb], [g1])
                        ts(g1[:], g1[:], 0.044715, 1.0, ALU.mult, ALU.add, [g1], [g1])
                        tt(g1[:], g1[:], ysb[:, kc, :], ALU.mult, [g1, ysb], [g1])
                        act(g2[:], g1[:], AF.Sigmoid, [g1], [g2], scale=1.5957691216057308)
                        tt(yb[:, kc, :], ysb[:, kc, :], g2[:], ALU.mult, [ysb, g2], [yb])
                    mo = [sbuf(p2, "mo%d" % i, [128, 512], BF16) for i in range(2)]
                    km = 0
                    for mt in range(4):
                        for tc in range(4):
                            ps = pbu[km % 4]
                            mo_ = mo[km % 2]
                            km += 1
                            for k in range(4):
                                MM(ps[:], wg[:, mt, k, :], yb[:, k, tc * 512:(tc + 1) * 512], k == 0, k == 3,
                                   [wg, yb], [ps])
                            act(g1[:, 0:512], ps[:], AF.Sigmoid, [ps], [g1])
                            tt(g1[:, 0:512], g1[:, 0:512], yb[:, mt, tc * 512:(tc + 1) * 512], ALU.mult, [g1, yb], [g1])
                            tt(mo_[:], g1[:, 0:512], sdg[:, mt, tc * 512:(tc + 1) * 512], ALU.mult, [g1, sdg], [mo_])
                            r0 = 1536 + mt * 128
                            dma(mixT_d[r0:r0 + 128, tc * 512:(tc + 1) * 512], mo_[:], [mo_], [R_mix], q="gpsimd")
                    S.flush()

        phase_p1(0, xin, T(None))
        phase_conv()
        phase_fourier()
        if debug == "l0":
            phase_out(0, w_out0, xin, T(None), 34, h1_d, R_h1, True)
            V(lambda e: e.memset(onesf[:], 1.0), [], [onesf])
            S.flush(final=True)
            return nc
        phase_out(0, w_out0, xin, T(None), 34, h1_d, R_h1, False)
        phase_p1(1, h1_d, R_h1)
        phase_na()
        phase_s5()
        phase_out(1, w_out1, h1_d, R_h1, 16, out_d, T(None), True)
        V(lambda e: e.memset(onesf[:], 1.0), [], [onesf])
        S.flush(final=True)
    return nc


_CONST_CACHE = {}


def _consts(inp):
    if "dft" not in _CONST_CACHE:
        _CONST_CACHE["dft"] = [_dft_consts(0), _dft_consts(1)]
    rpb = inp["na_rpb"][0]
    return {"dft": _CONST_CACHE["dft"], "na_bias": [_na_tables(rpb, 0), _na_tables(rpb, 1)]}


def kernel(**inputs):
    inp = {k: np.asarray(v) for k, v in inputs.items()}
    consts = _consts(inp)
    nc = build()
    in_maps = [prep_core(inp, core, consts) for core in range(8)]
    res = run_bass_kernel_spmd(nc, in_maps, core_ids=list(range(8)))
    out = np.empty((4, L, D), np.float32)
    for core in range(8):
        b, par = core // 2, core % 2
        o = res.results[core]["out"]
        if par:
            out[b, L - OWN:] = o[::-1]
        else:
            out[b, :OWN] = o
    return out
```

```python
import math
from contextlib import ExitStack
import numpy as np
import ml_dtypes
import concourse.bass as bass
import concourse.mybir as mybir
from concourse.bass_utils import run_bass_kernel_spmd

F32 = mybir.dt.float32
BF16 = mybir.dt.bfloat16
I32 = mybir.dt.int32
AF = mybir.ActivationFunctionType
ALU = mybir.AluOpType

ENGS = ("tensor", "vector", "scalar", "gpsimd", "sync")
NDMA = 24
D = 2048
L = 4096
LC = 256
TOK = L + LC
OWN = 2048
NKV = 2304
EPS = 1e-6
CHUNKS = [(i * 512, 512) for i in range(8)] + [(4096, 256)]
BF = ml_dtypes.bfloat16


class Res:
    __slots__ = ("w", "r")

    def __init__(self):
        self.w = None
        self.r = {}


class T:
    def __init__(self, t):
        self.t = t
        self.res = Res()

    def __getitem__(self, idx):
        return self.t[idx]


class Sched:
    def __init__(self, nc, sems):
        self.nc = nc
        self.sems = sems
        self.q = {e: [] for e in ENGS}
        self.cnt = {e: 0 for e in ENGS}
        self.seen = {e: {} for e in ENGS}
        self.dma_rr = 0
        self.dma_cnt = [0] * NDMA
        self.out_toks = []
        self.dq = 0
        self.barrier = {}

    def _deps(self, eng, reads, writes, pe_chain):
        need = {}

        def add(tok):
            if tok is None:
                return
            s, v = tok
            if pe_chain and s == "tensor" and eng == "tensor":
                return
            if need.get(s, 0) < v:
                need[s] = v

        for r in reads:
            add(r.res.w)
        for w in writes:
            add(w.res.w)
            for s, v in w.res.r.items():
                add((s, v))
        for s, v in self.barrier.items():
            if need.get(s, 0) < v:
                need[s] = v
        waits = []
        for s, v in need.items():
            if self.seen[eng].get(s, 0) < v:
                waits.append((s, v))
                self.seen[eng][s] = v
        return waits

    def _commit(self, tok, reads, writes):
        s, v = tok
        for r in reads:
            if r.res.r.get(s, 0) < v:
                r.res.r[s] = v
        for w in writes:
            w.res.w = tok
            w.res.r = {}

    def op(self, eng, emit, reads=(), writes=(), pe_chain=False):
        waits = self._deps(eng, reads, writes, pe_chain)
        self.cnt[eng] += 1
        tok = (eng, self.cnt[eng])
        self.q[eng].append((waits, emit, (eng, 1)))
        self._commit(tok, reads, writes)
        return tok

    def dma(self, emit, reads=(), writes=(), q=None, is_output=False):
        if q is None:
            q = ("sync", "gpsimd")[self.dq % 2]
            self.dq += 1
        slot = self.dma_rr % NDMA
        self.dma_rr += 1
        semkey = ("dma", slot)
        waits = self._deps(q, reads, writes, False)
        prev = self.dma_cnt[slot]
        if prev and self.seen[q].get(semkey, 0) < prev:
            waits.append((semkey, prev))
            self.seen[q][semkey] = prev
        self.dma_cnt[slot] = prev + 16
        tok = (semkey, prev + 16)
        self.q[q].append((waits, emit, (semkey, 16)))
        self._commit(tok, reads, writes)
        if is_output:
            self.out_toks.append(tok)
        return tok

    def flush(self, final=False):
        if final:
            fin = {}
            for s, v in self.out_toks:
                fin[s] = max(fin.get(s, 0), v)
            for e in ENGS:
                if self.cnt[e]:
                    fin[e] = max(fin.get(e, 0), self.cnt[e])
            for s in range(NDMA):
                if self.dma_cnt[s]:
                    fin[("dma", s)] = self.dma_cnt[s]
            self.q["sync"].append((list(fin.items()), None, None))
        qs = self.q
        sems = self.sems

        def run(name):
            def body(e):
                for waits, emit, inc in qs[name]:
                    for s, v in waits:
                        e.wait_ge(sems[s], v)
                    if emit is not None:
                        emit(e).then_inc(sems[inc[0]], inc[1])
            return body

        with self.nc.Block() as block:
            block.tensor(run("tensor"))
            block.vector(run("vector"))
            block.scalar(run("scalar"))
            block.gpsimd(run("gpsimd"))
            block.sync(run("sync"))
        self.q = {e: [] for e in ENGS}
        self.barrier = {e: self.cnt[e] for e in ENGS if self.cnt[e]}
        for s_ in range(NDMA):
            if self.dma_cnt[s_]:
                self.barrier[("dma", s_)] = self.dma_cnt[s_]


def _cols(v, n):
    return np.ascontiguousarray(v.reshape(n, 128).T)


def _mt(W):
    K, F = W.shape
    return np.ascontiguousarray(W.reshape(K // 128, 128, F // 128, 128).transpose(2, 1, 0, 3))


def _kt(W):
    K, F = W.shape
    return np.ascontiguousarray(W.reshape(K // 128, 128, F).transpose(1, 0, 2))


def _na_tables(rpb, par):
    out = np.full((3, 12, 640, 128), -30000.0, np.float32)
    for cls, j in enumerate((0, 1, 2)):
        ws = min(max(2 * j - 4, 0), 26)
        qi = np.arange(128)
        qr_o = 2 * j + qi // 64
        qc_o = qi % 64
        ki = np.arange(640)
        kr_o = ws + ki // 64
        kc_o = ki % 64
        if par:
            qr, qc, kr, kc = 63 - qr_o, 63 - qc_o, 63 - kr_o, 63 - kc_o
        else:
            qr, qc, kr, kc = qr_o, qc_o, kr_o, kc_o
        rs = np.clip(qr - 4, 0, 56)
        cs = np.clip(qc - 8, 0, 48)
        ok = ((kr[:, None] >= rs[None]) & (kr[:, None] < rs[None] + 8) &
              (kc[:, None] >= cs[None]) & (kc[:, None] < cs[None] + 16))
        dr = np.clip(kr[:, None] - qr[None] + 7, 0, 14)
        dc = np.clip(kc[:, None] - qc[None] + 15, 0, 30)
        g = rpb[:, dr, dc]
        out[cls] = np.where(ok[None], g, np.float32(-30000.0))
    return np.ascontiguousarray(out.reshape(3, 12, 5, 128, 128).transpose(0, 1, 3, 2, 4))


def _dft_consts(par):
    n = np.arange(256)
    a = 2.0 * np.pi * ((n[:, None] * n[None, :]) % 256) / 256.0
    c256, s256 = np.cos(a), np.sin(a)
    cs256 = np.concatenate([c256, s256], axis=1)
    cs256 = cs256.reshape(2, 128, 512).transpose(1, 0, 2)
    if par:
        cc, sc = c256[::-1, ::-1], s256[::-1, ::-1]
    else:
        cc, sc = c256, s256
    cctx = np.concatenate([cc, -sc], axis=1).reshape(2, 128, 512).transpose(1, 0, 2)
    a = np.arange(64)
    if par:
        e1 = (a[:, None] * (a[None, :] + 1)) % 64
        e3 = ((a[:, None] + 1) * a[None, :]) % 64
        e2 = ((a[:, None] + 1) * (a[None, :] + 1)) % 4096
    else:
        e1 = (a[:, None] * a[None, :]) % 64
        e3 = e1
        e2 = (a[:, None] * a[None, :]) % 4096

    def bd(m):
        z = np.zeros((128, 128))
        z[:64, :64] = m
        z[64:, 64:] = m
        return z

    th1 = 2.0 * np.pi * e1 / 64.0
    th3 = 2.0 * np.pi * e3 / 64.0
    th2 = 2.0 * np.pi * e2 / 4096.0
    w1 = np.stack([bd(np.cos(th1)), bd(-np.sin(th1)), bd(-np.cos(th1))], axis=1)
    w3 = np.stack([bd(np.cos(th3)), bd(np.sin(th3))], axis=1)
    tw = np.stack([np.cos(th2).reshape(-1), np.sin(th2).reshape(-1)], axis=0)
    tw = np.broadcast_to(tw[None], (128, 2, 4096))
    return (np.ascontiguousarray(cs256).astype(BF), np.ascontiguousarray(cctx).astype(BF),
            np.ascontiguousarray(w1).astype(BF), np.ascontiguousarray(w3).astype(BF),
            np.ascontiguousarray(tw).astype(BF))


def _s5_layout(inp, par):
    dirs = (1, 0) if par else (0, 1)
    G, P, H = 32, 64, 16
    out = {}
    a_re = inp["s5_a_re"][0][list(dirs)]
    a_im = inp["s5_a_im"][0][list(dirs)]
    ldt = inp["s5_log_dt"][0][list(dirs)]

    def st(v):
        return np.ascontiguousarray(v.reshape(2, 16, 2, 64).transpose(2, 3, 0, 1).reshape(128, 2, 16))

    out["s5are"] = st(a_re)
    out["s5aim"] = st(a_im)
    out["s5ldt"] = st(np.broadcast_to(ldt[:, :, None], (2, G, P)))
    b_re = inp["s5_b_re"][0][list(dirs)]
    b_im = inp["s5_b_im"][0][list(dirs)]
    c_re = inp["s5_c_re"][0][list(dirs)]
    c_im = inp["s5_c_im"][0][list(dirs)]
    braw = np.zeros((2, 2, 16, 128, 128), np.float32)
    ct = np.zeros((2, 2, 16, 128, 128), np.float32)
    for g in range(G):
        j = g // 2
        r0 = (g % 8) * 16
        s0 = (g % 2) * 64
        for ri, (bb, cc) in enumerate(((b_re, c_re), (b_im, c_im))):
            braw[ri, :, j, r0:r0 + 16, s0:s0 + 64] = bb[:, g].transpose(0, 2, 1)
            ct[ri, :, j, s0:s0 + 64, r0:r0 + 16] = cc[:, g].transpose(0, 2, 1)
    out["s5braw"] = np.ascontiguousarray(braw.transpose(3, 0, 1, 2, 4))
    out["s5ct"] = np.ascontiguousarray(ct.transpose(3, 0, 1, 2, 4))
    return out


def prep_core(inp, core, consts):
    b, par = core // 2, core % 2
    x = inp["x"][b]
    ctx = inp["ctx"][b]
    if par:
        x = x[::-1]
        ctx = ctx[::-1]
    m = {}
    m["xin"] = np.ascontiguousarray(np.concatenate([x, ctx], axis=0))
    m["cvec"] = np.ascontiguousarray(np.stack([_cols(inp["c"][b], 16), _cols(inp["c_ctx"], 16)], axis=2))
    m["ada_w"] = np.ascontiguousarray(inp["ada_w"].reshape(2, 16, 128, 6144).transpose(0, 2, 1, 3))
    m["ada_b"] = np.stack([_cols(inp["ada_b"][l], 48) for l in range(2)], axis=1)
    m["pre_g"] = np.stack([_cols(inp["pre_g"][l], 16) for l in range(2)], axis=1)
    m["post_g"] = np.stack([_cols(inp["post_g"][l], 16) for l in range(2)], axis=1)
    m["w_in0"] = _mt(inp["ab_w_in"][0])
    m["w_out0"] = _kt(inp["ab_w_out"][0])
    cw = inp["conv_w"][0]
    if par:
        cw = cw[::-1]
    m["conv_w"] = np.ascontiguousarray(cw.T.reshape(8, 128, 31).transpose(1, 0, 2))
    m["conv_v"] = np.ascontiguousarray(np.stack([_cols(inp["conv_b"][0], 8), _cols(inp["conv_ln_g"][0], 8),
                                                 _cols(inp["conv_ln_b"][0], 8)], axis=1))
    fg = inp["fourier_g"][0]
    m["four_g"] = np.ascontiguousarray(fg.reshape(4, 2, 128).transpose(2, 0, 1))
    m["w_in1"] = _mt(inp["cd_w_in"][0])
    m["w_out1"] = _kt(inp["cd_w_out"][0])
    m["na_bias"] = consts["na_bias"][par]
    m["s5d"] = _cols(inp["s5_d"][0], 4)
    m["w_glu"] = _mt(inp["s5_w_glu"][0])
    m.update(_s5_layout(inp, par))
    cs256, cctx, fw1, fw3, ftw = consts["dft"][par]
    m["cs256"], m["cctx"], m["fw1"], m["fw3"], m["ftw"] = cs256, cctx, fw1, fw3, ftw
    m["ident"] = np.eye(128, dtype=np.float32)
    m["svals"] = np.ascontiguousarray(np.broadcast_to(np.arange(513, dtype=np.float32)[None], (128, 513)))
    return m


def build(debug=None):
    nc = bass.Bass("TRN2", target_bir_lowering=False)

    def din(name, shape, dtype=F32):
        return nc.dram_tensor(name, list(shape), dtype, kind="ExternalInput").ap()

    def dscr(name, shape, dtype):
        return nc.dram_tensor(name, list(shape), dtype, kind="Internal").ap()

    xin = din("xin", [TOK, D])
    cvec = din("cvec", [128, 16, 2])
    ada_w = din("ada_w", [2, 128, 16, 6144])
    ada_b = din("ada_b", [128, 2, 48])
    pre_g = din("pre_g", [128, 2, 16])
    post_g = din("post_g", [128, 2, 16])
    w_in0 = din("w_in0", [40, 128, 16, 128])
    w_out0 = din("w_out0", [128, 16, 2048])
    conv_w = din("conv_w", [128, 8, 31])
    conv_v = din("conv_v", [128, 3, 8])
    four_g = din("four_g", [128, 4, 2])
    w_in1 = din("w_in1", [56, 128, 16, 128])
    w_out1 = din("w_out1", [128, 16, 2048])
    na_bias = din("na_bias", [3, 12, 128, 5, 128])
    s5d = din("s5d", [128, 4])
    w_glu = din("w_glu", [4, 128, 4, 128])
    s5are = din("s5are", [128, 2, 16])
    s5aim = din("s5aim", [128, 2, 16])
    s5ldt = din("s5ldt", [128, 2, 16])
    s5braw = din("s5braw", [128, 2, 2, 16, 128])
    s5ct = din("s5ct", [128, 2, 2, 16, 128])
    cs256_d = din("cs256", [128, 2, 512], BF16)
    cctx_d = din("cctx", [128, 2, 512], BF16)
    w1_d = din("fw1", [128, 3, 128], BF16)
    w3_d = din("fw3", [128, 2, 128], BF16)
    twd_d = din("ftw", [128, 2, L], BF16)
    ident_d = din("ident", [128, 128])
    svals_d = din("svals", [128, 513])
    out_d = nc.dram_tensor("out", [OWN, D], F32, kind="ExternalOutput").ap()
    h1_kind = "ExternalOutput" if debug == "l0" else "Internal"
    h1_d = nc.dram_tensor("h1", [TOK, D], F32, kind=h1_kind).ap()
    uT_d = dscr("uT", [D, 4608], BF16)
    mixT_d = dscr("mixT", [D, TOK], BF16)
    convT_d = dscr("convT", [1024, TOK], BF16)

    with ExitStack() as top:
        sems = {}
        for e in ENGS:
            sems[e] = top.enter_context(nc.semaphore("s_" + e))
        for i in range(NDMA):
            sems[("dma", i)] = top.enter_context(nc.semaphore("d%d" % i))
        S = Sched(nc, sems)

        uid = [0]

        def sbuf(es, name, shape, dtype):
            uid[0] += 1
            return T(es.enter_context(nc.sbuf_tensor("%s_%d" % (name, uid[0]), list(shape), dtype)))

        def psum(es, name, shape, dtype):
            uid[0] += 1
            return T(es.enter_context(nc.psum_tensor("%s_%d" % (name, uid[0]), list(shape), dtype)))

        R_uT, R_mix, R_conv, R_h1 = T(None), T(None), T(None), T(None)

        identb = sbuf(top, "identb", [128, 128], BF16)
        identf = sbuf(top, "identf", [128, 128], F32)
        onesb = sbuf(top, "onesb", [128, 128], BF16)
        onesf = sbuf(top, "onesf", [128, 128], F32)
        modT = sbuf(top, "modT", [128, 2, 48, 2], F32)
        gsT = sbuf(top, "gsT", [128, 2, 2, 16], F32)
        shT = sbuf(top, "shT", [128, 2, 2, 16], F32)
        gpT = sbuf(top, "gpT", [128, 2, 2, 16], F32)
        preg = sbuf(top, "preg", [128, 2, 16], F32)
        postg = sbuf(top, "postg", [128, 2, 16], F32)

        def V(fn, reads, writes):
            return S.op("vector", fn, reads, writes)

        def A(fn, reads, writes):
            return S.op("scalar", fn, reads, writes)

        def G(fn, reads, writes):
            return S.op("gpsimd", fn, reads, writes)

        def MM(out, lhsT, rhs, start, stop, reads, writes):
            return S.op("tensor", lambda e: e.matmul(out, lhsT=lhsT, rhs=rhs, start=start, stop=stop),
                        reads, writes, pe_chain=True)

        def act(out, in_, func, reads, writes, **kw):
            return A(lambda e: e.activation(out=out, in_=in_, func=func, **kw), reads, writes)

        def tt(out, in0, in1, op, reads, writes, eng="vector"):
            return S.op(eng, lambda e: e.tensor_tensor(out=out, in0=in0, in1=in1, op=op), reads, writes)

        def ts(out, in0, s1, s2, op0, op1, reads, writes, eng="vector"):
            if op1 is None:
                return S.op(eng, lambda e: e.tensor_scalar(out=out, in0=in0, scalar1=s1, scalar2=None, op0=op0),
                            reads, writes)
            return S.op(eng, lambda e: e.tensor_scalar(out=out, in0=in0, scalar1=s1, scalar2=s2, op0=op0, op1=op1),
                        reads, writes)

        def stt(out, in0, scalar, in1, op0, op1, reads, writes):
            return V(lambda e: e.scalar_tensor_tensor(out=out, in0=in0, scalar=scalar, in1=in1, op0=op0, op1=op1),
                     reads, writes)

        def cp(out, in_, reads, writes, eng="vector"):
            return S.op(eng, lambda e: e.tensor_copy(out=out, in_=in_), reads, writes)

        def dma(out, in_, reads, writes, q=None, is_output=False):
            return S.dma(lambda e: e.dma_start(out=out, in_=in_), reads, writes, q=q, is_output=is_output)

        with ExitStack() as ph:
            condT = sbuf(ph, "condT", [128, 16, 2], F32)
            adab = sbuf(ph, "adab", [128, 2, 48], F32)
            aw = [sbuf(ph, "aw%d" % i, [128, 16, 512], F32) for i in range(2)]
            psA = psum(ph, "psA", [128, 2, 48, 2], F32)
            tmp16 = sbuf(ph, "tmp16", [128, 16], F32)
            dma(identf[:], ident_d[:, :], [], [identf], q="sync")
            dma(identb[:], ident_d[:, :], [], [identb], q="gpsimd")
            dma(condT[:], cvec[:, :, :], [], [condT], q="sync")
            dma(adab[:], ada_b[:, :, :], [], [adab], q="sync")
            dma(preg[:], pre_g[:, :, :], [], [preg], q="sync")
            dma(postg[:], post_g[:, :, :], [], [postg], q="sync")
            V(lambda e: e.memset(onesf[:], 1.0), [], [onesf])
            V(lambda e: e.memset(onesb[:], 1.0), [], [onesb])
            act(condT[:], condT[:], AF.Silu, [condT], [condT])
            for l in range(2):
                for cb in range(12):
                    w = aw[(l * 12 + cb) % 2]
                    dma(w[:], ada_w[l, :, :, cb * 512:(cb + 1) * 512], [], [w])
                    for m in range(4):
                        j = cb * 4 + m
                        for k in range(16):
                            MM(psA[:, l, j, :], w[:, k, m * 128:(m + 1) * 128], condT[:, k, :],
                               k == 0, k == 15, [w, condT], [psA])
                for i in range(2):
                    tt(modT[:, l, :, i], psA[:, l, :, i], adab[:, l, :], ALU.add, [psA, adab], [modT])
                for i in range(2):
                    stt(gsT[:, l, i, :], modT[:, l, 16:32, i], 1.0, preg[:, l, :], ALU.add, ALU.mult,
                        [modT, preg], [gsT])
                    cp(shT[:, l, i, :], modT[:, l, 0:16, i], [modT], [shT])
                    tt(gpT[:, l, i, :], modT[:, l, 32:48, i], postg[:, l, :], ALU.mult, [modT, postg], [gpT])
            S.flush()

        def phase_p1(l, h_d, R_h):
            with ExitStack() as ph:
                xs = [sbuf(ph, "xs%d" % i, [128, D], F32) for i in range(2)]
                xn = [sbuf(ph, "xn%d" % i, [128, D], BF16) for i in range(2)]
                junk = sbuf(ph, "junk", [128, D], BF16)
                st = [sbuf(ph, "ust%d" % i, [128, 16, 512], BF16) for i in range(2)]
                ss = sbuf(ph, "ss", [128, 34], F32)
                rs = sbuf(ph, "rs", [128, 34], F32)
                pT = [psum(ph, "pT%d" % i, [128, 16, 128], BF16) for i in range(2)]
                for tix in range(34):
                    i = 0 if tix < 32 else 1
                    x_ = xs[tix % 2]
                    xn_ = xn[tix % 2]
                    p_ = pT[tix % 2]
                    st_ = st[(tix // 4) % 2]
                    dma(x_[:], h_d[tix * 128:(tix + 1) * 128, :], [R_h], [x_])
                    act(junk[:], x_[:], AF.Square, [x_], [junk, ss], accum_out=ss[:, tix:tix + 1])
                    ts(rs[:, tix:tix + 1], ss[:, tix:tix + 1], 1.0 / D, EPS, ALU.mult, ALU.add, [ss], [rs])
                    act(rs[:, tix:tix + 1], rs[:, tix:tix + 1], AF.Sqrt, [rs], [rs])
                    V(lambda e, a=rs[:, tix:tix + 1]: e.reciprocal(out=a, in_=a), [rs], [rs])
                    act(xn_[:], x_[:], AF.Copy, [x_, rs], [xn_], scale=rs[:, tix:tix + 1])
                    for c in range(16):
                        S.op("tensor", lambda e, o=p_[:, c, :], a=xn_[:, c * 128:(c + 1) * 128]:
                             e.transpose(out=o, in_=a, identity=identb[:]), [xn_, identb], [p_], pe_chain=True)
                    q4 = tix % 4
                    for c in range(16):
                        ts(st_[:, c, q4 * 128:(q4 + 1) * 128], p_[:, c, :], gsT[:, l, i, c:c + 1],
                           shT[:, l, i, c:c + 1], ALU.mult, ALU.add, [p_, gsT, shT], [st_])
                    if q4 == 3 or tix == 33:
                        t0 = (tix // 4) * 512
                        n = (q4 + 1) * 128
                        dma(uT_d.rearrange("(c p) t -> p c t", p=128)[:, :, t0:t0 + n], st_[:, :, 0:n], [st_], [R_uT])
                S.flush()

        def load_w(wsb, wsrc, mtiles):
            for i, mt in enumerate(mtiles):
                dma(wsb[:, i, :, :], wsrc[mt, :, :, :], [], [wsb], q="gpsimd")

        rot = {"ps": 0, "ut": 0}

        def proj_fm(ph, wsrc, mtiles, chunks, epilogue, tag, ps_tiles, ut_tiles, wsb=None):
            nm = len(mtiles)
            if wsb is None:
                wsb = sbuf(ph, "w_" + tag, [128, nm, 16, 128], BF16)
                load_w(wsb, wsrc, mtiles)
            for ci, (t0, n) in enumerate(chunks):
                ut = ut_tiles[rot["ut"] % len(ut_tiles)]
                rot["ut"] += 1
                dma(ut[:, :, 0:n], uT_d.rearrange("(c p) t -> p c t", p=128)[:, :, t0:t0 + n], [R_uT], [ut], q="sync")
                for i in range(nm):
                    ps = ps_tiles[rot["ps"] % len(ps_tiles)]
                    rot["ps"] += 1
                    for k in range(16):
                        MM(ps[:, 0:n], wsb[:, i, k, :], ut[:, k, 0:n], k == 0, k == 15, [wsb, ut], [ps])
                    epilogue(i, ci, t0, n, ps)

        def phase_conv():
            with ExitStack() as ph:
                cw = sbuf(ph, "cw", [128, 8, 31], F32)
                cv = sbuf(ph, "cv", [128, 3, 8], F32)
                s1 = sbuf(ph, "s1", [128, TOK], F32)
                s2 = sbuf(ph, "s2", [128, TOK], F32)
                dma(cw[:], conv_w[:, :, :], [], [cw], q="sync")
                dma(cv[:], conv_v[:, :, :], [], [cv], q="sync")
                with ExitStack() as p1:
                    ut_tiles = [sbuf(p1, "ut%d" % i, [128, 16, 512], BF16) for i in range(2)]
                    ps_tiles = [psum(p1, "ps%d" % i, [128, 512], F32) for i in range(4)]
                    pst = [psum(p1, "pst%d" % i, [128, 512], F32) for i in range(2)]
                    apad = [sbuf(p1, "apad%d" % i, [128, 4400], BF16) for i in range(4)]
                    dgk = [sbuf(p1, "dgk%d" % i, [128, 31, 128], BF16) for i in range(2)]
                    pcv = [psum(p1, "pcv%d" % i, [128, 512], F32) for i in range(2)]
                    cvbs = [sbuf(p1, "cvb%d" % i, [128, TOK], BF16) for i in range(2)]
                    sqbs = [sbuf(p1, "sqb%d" % i, [128, TOK], BF16) for i in range(2)]
                    wA8 = sbuf(p1, "wA8", [128, 8, 16, 128], BF16)
                    sig = [sbuf(p1, "sig%d" % i, [128, 512], BF16) for i in range(2)]
                    for a_ in apad:
                        V(lambda e, a_=a_: e.memset(a_[:], 0.0), [], [a_])
                    for half in range(2):
                        cs = [4 * half + q for q in range(4)]
                        mts = []
                        for c in cs:
                            mts += [8 + c, c]
                        load_w(wA8, w_in0, mts)

                        def epi(i, ci, t0, n, ps):
                            sg = sig[ci % 2]
                            ap_ = apad[i // 2]
                            if i % 2 == 0:
                                act(sg[:, 0:n], ps[:, 0:n], AF.Sigmoid, [ps], [sg])
                            else:
                                off = 15 + t0 if t0 < L else 4126
                                tt(ap_[:, off:off + n], ps[:, 0:n], sg[:, 0:n], ALU.mult, [ps, sg], [ap_])

                        proj_fm(p1, w_in0, mts, CHUNKS, epi, "a1", ps_tiles, ut_tiles, wsb=wA8)
                        for q, c in enumerate(cs):
                            ap_ = apad[q]
                            dg_ = dgk[c % 2]
                            cvb, sqb = cvbs[c % 2], sqbs[c % 2]
                            tt(dg_[:], identb[:].unsqueeze(1).to_broadcast([128, 31, 128]),
                               cw[:, c, :].unsqueeze(2).to_broadcast([128, 31, 128]), ALU.mult, [identb, cw], [dg_])
                            for ci, (t0, n) in enumerate(CHUNKS):
                                i0 = t0 if t0 < L else 4111
                                pc_ = pcv[ci % 2]
                                for k in range(31):
                                    MM(pc_[:, 0:n], dg_[:, k, :], ap_[:, i0 + k:i0 + k + n], k == 0, k == 30,
                                       [dg_, ap_], [pc_])
                                act(cvb[:, t0:t0 + n], pc_[:, 0:n], AF.Identity, [pc_, cv], [cvb], bias=cv[:, 0, c:c + 1])
                                act(sqb[:, t0:t0 + n], pc_[:, 0:n], AF.Square, [pc_, cv], [sqb], bias=cv[:, 0, c:c + 1])
                            for ci, (t0, n) in enumerate(CHUNKS):
                                MM(pst[0][:, 0:n], onesb[:], cvb[:, t0:t0 + n], True, True, [onesb, cvb], [pst[0]])
                                MM(pst[1][:, 0:n], onesb[:], sqb[:, t0:t0 + n], True, True, [onesb, sqb], [pst[1]])
                                if c == 0:
                                    cp(s1[:, t0:t0 + n], pst[0][:, 0:n], [pst[0]], [s1])
                                    cp(s2[:, t0:t0 + n], pst[1][:, 0:n], [pst[1]], [s2])
                                else:
                                    tt(s1[:, t0:t0 + n], pst[0][:, 0:n], s1[:, t0:t0 + n], ALU.add, [pst[0], s1], [s1])
                                    tt(s2[:, t0:t0 + n], pst[1][:, 0:n], s2[:, t0:t0 + n], ALU.add, [pst[1], s2], [s2])
                            dma(convT_d[c * 128:(c + 1) * 128, :], cvb[:], [cvb], [R_conv], q="sync")
                    msq = sbuf(p1, "msq", [128, 512], F32)
                    for (t0, n) in CHUNKS:
                        ts(s1[:, t0:t0 + n], s1[:, t0:t0 + n], 1.0 / 1024, None, ALU.mult, None, [s1], [s1])
                        tt(msq[:, 0:n], s1[:, t0:t0 + n], s1[:, t0:t0 + n], ALU.mult, [s1], [msq])
                        stt(s2[:, t0:t0 + n], s2[:, t0:t0 + n], 1.0 / 1024, msq[:, 0:n], ALU.mult, ALU.subtract,
                            [s2, msq], [s2])
                    ts(s2[:], s2[:], EPS, None, ALU.add, None, [s2], [s2])
                    act(s2[:], s2[:], AF.Sqrt, [s2], [s2])
                    V(lambda e: e.reciprocal(out=s2[:], in_=s2[:]), [s2], [s2])
                    S.flush()
                with ExitStack() as p2:
                    ut_tiles = [sbuf(p2, "ut%d" % i, [128, 16, 512], BF16) for i in range(2)]
                    ps_tiles = [psum(p2, "ps%d" % i, [128, 512], F32) for i in range(4)]
                    cvl = [sbuf(p2, "cvl%d" % i, [128, 8, 512], BF16) for i in range(2)]
                    sgt = [sbuf(p2, "sgt%d" % i, [128, 512], F32) for i in range(2)]
                    t1 = [sbuf(p2, "t1%d" % i, [128, 512], F32) for i in range(2)]
                    mo = [sbuf(p2, "mo%d" % i, [128, 512], BF16) for i in range(2)]
                    wG8 = sbuf(p2, "wG8", [128, 8, 16, 128], BF16)
                    load_w(wG8, w_in0, [16 + c for c in range(8)])
                    kk2 = [0]

                    def epi(i, ci, t0, n, ps):
                        c = i
                        cvl_ = cvl[ci % 2]
                        if c == 0:
                            dma(cvl_[:, :, 0:n], convT_d.rearrange("(c p) t -> p c t", p=128)[:, :, t0:t0 + n],
                                [R_conv], [cvl_], q="sync")
                        k2 = kk2[0]
                        kk2[0] += 1
                        sg, t1_, mo_ = sgt[k2 % 2], t1[k2 % 2], mo[k2 % 2]
                        act(sg[:, 0:n], ps[:, 0:n], AF.Silu, [ps], [sg])
                        tt(t1_[:, 0:n], cvl_[:, c, 0:n], s1[:, t0:t0 + n], ALU.subtract, [cvl_, s1], [t1_])
                        tt(t1_[:, 0:n], t1_[:, 0:n], s2[:, t0:t0 + n], ALU.mult, [t1_, s2], [t1_])
                        act(t1_[:, 0:n], t1_[:, 0:n], AF.Silu, [t1_, cv], [t1_],
                            scale=cv[:, 1, c:c + 1], bias=cv[:, 2, c:c + 1])
                        tt(mo_[:, 0:n], t1_[:, 0:n], sg[:, 0:n], ALU.mult, [t1_, sg], [mo_])
                        dma(mixT_d[c * 128:(c + 1) * 128, t0:t0 + n], mo_[:, 0:n], [mo_], [R_mix], q="gpsimd")

                    proj_fm(p2, w_in0, [16 + c for c in range(8)], CHUNKS, epi, "a2", ps_tiles, ut_tiles, wsb=wG8)
                    S.flush()

        def phase_fourier():
            with ExitStack() as ph:
                fg = sbuf(ph, "fg", [128, 4, 2], F32)
                cs256 = sbuf(ph, "cs256", [128, 2, 512], BF16)
                cctx = sbuf(ph, "cctx", [128, 2, 512], BF16)
                bnT = sbuf(ph, "bnT", [128, 2, TOK], BF16)
                sgT = sbuf(ph, "sgT", [128, 2, TOK], BF16)
                dma(fg[:], four_g[:, :, :], [], [fg], q="sync")
                dma(cs256[:], cs256_d[:, :, :], [], [cs256], q="sync")
                dma(cctx[:], cctx_d[:, :, :], [], [cctx], q="sync")
                w1t = sbuf(ph, "w1t", [128, 3, 128], BF16)
                w3t = sbuf(ph, "w3t", [128, 2, 128], BF16)
                twd = sbuf(ph, "twd", [128, 2, L], BF16)
                dma(w1t[:], w1_d[:, :, :], [], [w1t], q="sync")
                dma(w3t[:], w3_d[:, :, :], [], [w3t], q="sync")
                dma(twd[:], twd_d[:, :, :], [], [twd], q="sync")
                wB = [sbuf(ph, "wB%d" % i, [128, 4, 16, 128], BF16) for i in range(2)]
                fmt = lambda g_: [24 + 2 * g_, 25 + 2 * g_, 32 + 2 * g_, 33 + 2 * g_]
                load_w(wB[0], w_in0, fmt(0))
                for g in range(4):
                    with ExitStack() as p1:
                        ut_tiles = [sbuf(p1, "ut%d" % i, [128, 16, 512], BF16) for i in range(2)]
                        ps_tiles = [psum(p1, "ps%d" % i, [128, 512], F32) for i in range(6)]
                        pss = [psum(p1, "pss%d" % i, [128, 512], F32) for i in range(2)]
                        sq = [sbuf(p1, "sq%d" % i, [128, 512], BF16) for i in range(4)]
                        rst = [sbuf(p1, "rst%d" % i, [128, 512], F32) for i in range(2)]
                        held = {}

                        def epi(i, ci, t0, n, ps, g=g):
                            if i >= 2:
                                act(sgT[:, i - 2, t0:t0 + n], ps[:, 0:n], AF.Silu, [ps], [sgT])
                                return
                            sq_ = sq[(ci % 2) * 2 + i]
                            act(sq_[:, 0:n], ps[:, 0:n], AF.Square, [ps], [sq_])
                            held[i] = (ps, sq_)
                            if i == 1:
                                pss_ = pss[ci % 2]
                                rst_ = rst[ci % 2]
                                for jj in range(2):
                                    MM(pss_[:, 0:n], onesb[:], held[jj][1][:, 0:n], jj == 0, jj == 1,
                                       [onesb, held[jj][1]], [pss_])
                                ts(rst_[:, 0:n], pss_[:, 0:n], 1.0 / 256, EPS, ALU.mult, ALU.add, [pss_], [rst_])
                                act(rst_[:, 0:n], rst_[:, 0:n], AF.Sqrt, [rst_], [rst_])
                                V(lambda e, a=rst_[:, 0:n]: e.reciprocal(out=a, in_=a), [rst_], [rst_])
                                for jj in range(2):
                                    if t0 < L:
                                        o_ = bnT[:, jj, 0:L].rearrange("p (b a) -> p a b", a=64)[:, 8 * ci:8 * ci + 8, :]
                                        stt(o_, held[jj][0][:, 0:512].rearrange("p (a b) -> p a b", b=64),
                                            fg[:, g, jj:jj + 1], rst_[:, 0:512].rearrange("p (a b) -> p a b", b=64),
                                            ALU.mult, ALU.mult, [held[jj][0], fg, rst_], [bnT])
                                    else:
                                        stt(bnT[:, jj, t0:t0 + n], held[jj][0][:, 0:n], fg[:, g, jj:jj + 1],
                                            rst_[:, 0:n], ALU.mult, ALU.mult, [held[jj][0], fg, rst_], [bnT])

                        proj_fm(p1, w_in0, fmt(g), CHUNKS, epi, "b1", ps_tiles, ut_tiles, wsb=wB[g % 2])
                        S.flush()
                    if g < 3:
                        load_w(wB[(g + 1) % 2], w_in0, fmt(g + 1))
                    with ExitStack() as p2:
                        YB = sbuf(p2, "YB", [128, 32, 512], BF16)
                        Yc_ = sbuf(p2, "Yc_", [128, 2, 512], BF16)
                        Bsb = sbuf(p2, "Bsb", [128, 2, 2, L], BF16)
                        fo = [sbuf(p2, "fo%d" % i, [128, L], BF16) for i in range(2)]
                        foc = sbuf(p2, "foc", [128, 256], BF16)
                        m1 = [sbuf(p2, "fm1%d" % i, [128, 512], F32) for i in range(2)]
                        m2 = [sbuf(p2, "fm2%d" % i, [128, 512], F32) for i in range(2)]
                        psy = [psum(p2, "psy%d" % i, [128, 512], F32) for i in range(2)]
                        pa = [psum(p2, "pa%d" % i, [128, 4, 128], F32) for i in range(4)]
                        ptr = [psum(p2, "ptr%d" % i, [128, 8, 128], BF16) for i in range(2)]
                        for beta in range(32):
                            p_ = psy[beta % 2]
                            for jj in range(2):
                                MM(p_[:], bnT[:, jj, beta * 128:(beta + 1) * 128], cs256[:, jj, :], jj == 0, jj == 1,
                                   [bnT, cs256], [p_])
                            if beta % 2 == 0:
                                cp(YB[:, beta, :], p_[:], [p_], [YB])
                            else:
                                act(YB[:, beta, :], p_[:], AF.Copy, [p_], [YB])
                        for t_ in range(2):
                            p_ = psy[t_ % 2]
                            for jj in range(2):
                                MM(p_[:], bnT[:, jj, L + t_ * 128:L + (t_ + 1) * 128], cs256[:, jj, :], jj == 0, jj == 1,
                                   [bnT, cs256], [p_])
                            cp(Yc_[:, t_, :], p_[:], [p_], [Yc_])
                        sc_x = 1.0 / math.sqrt(L * 256.0)
                        sc_c = 1.0 / math.sqrt(LC * 256.0)
                        kq = 0
                        for kk in range(2):
                            for bq in range(8):
                                par_, pai_ = pa[(kq % 2) * 2], pa[(kq % 2) * 2 + 1]
                                m1_, m2_ = m1[kq % 2], m2[kq % 2]
                                kq += 1
                                for q in range(4):
                                    beta = bq * 4 + q
                                    yc = YB[:, beta, kk * 128:(kk + 1) * 128]
                                    ys = YB[:, beta, 256 + kk * 128:256 + (kk + 1) * 128]
                                    MM(par_[:, q, :], yc, w1t[:, 0, :], True, False, [YB, w1t], [par_])
                                    MM(par_[:, q, :], ys, w1t[:, 1, :], False, True, [YB, w1t], [par_])
                                    MM(pai_[:, q, :], yc, w1t[:, 1, :], True, False, [YB, w1t], [pai_])
                                    MM(pai_[:, q, :], ys, w1t[:, 2, :], False, True, [YB, w1t], [pai_])
                                sl = slice(bq * 512, (bq + 1) * 512)
                                arv = par_[:].rearrange("p a b -> p (a b)")
                                aiv = pai_[:].rearrange("p a b -> p (a b)")
                                tt(m1_[:], arv, twd[:, 0, sl], ALU.mult, [par_, twd], [m1_])
                                tt(m2_[:], aiv, twd[:, 1, sl], ALU.mult, [pai_, twd], [m2_])
                                ob = lambda ri: Bsb[:, kk, ri, :].rearrange("p (m b) -> p b m", b=64)[:, 8 * bq:8 * bq + 8, :]
                                v3 = lambda t_: t_[:].rearrange("p (b m) -> p b m", m=64)
                                tt(ob(0), v3(m1_), v3(m2_), ALU.add, [m1_, m2_], [Bsb], eng="gpsimd")
                                tt(m1_[:], aiv, twd[:, 0, sl], ALU.mult, [pai_, twd], [m1_])
                                tt(m2_[:], arv, twd[:, 1, sl], ALU.mult, [par_, twd], [m2_])
                                tt(ob(1), v3(m1_), v3(m2_), ALU.subtract, [m1_, m2_], [Bsb], eng="gpsimd")
                        BT = YB
                        ke = 0
                        for mq in range(8):
                            for ri in range(2):
                                pt_ = ptr[ke % 2]
                                for q in range(4):
                                    mu = mq * 4 + q
                                    for kk in range(2):
                                        src = Bsb[:, kk, ri, mu * 128:(mu + 1) * 128]
                                        S.op("tensor", lambda e, o=pt_[:, q * 2 + kk, :], a=src:
                                             e.transpose(out=o, in_=a, identity=identb[:]), [Bsb, identb], [pt_],
                                             pe_chain=True)
                                dst = BT[:, mq * 4:(mq + 1) * 4, ri * 256:(ri + 1) * 256]
                                srcp = pt_[:].rearrange("p (q k) c -> p q (k c)", k=2)
                                if ke % 2 == 0:
                                    cp(dst, srcp, [pt_], [BT])
                                else:
                                    act(dst, srcp, AF.Copy, [pt_], [BT])
                                ke += 1
                        kq = 0
                        for kk in range(2):
                            fo_ = fo[kk]
                            fov = fo_[:].rearrange("p (mb ma) -> p ma mb", ma=64)
                            sgv = sgT[:, kk, 0:L].rearrange("p (mb ma) -> p ma mb", ma=64)
                            for mq in range(8):
                                pf_ = pa[kq % 4]
                                kq += 1
                                for q in range(4):
                                    mu = mq * 4 + q
                                    MM(pf_[:, q, :], BT[:, mu, kk * 128:(kk + 1) * 128], w3t[:, 0, :], True, False,
                                       [BT, w3t], [pf_])
                                    MM(pf_[:, q, :], BT[:, mu, 256 + kk * 128:256 + (kk + 1) * 128], w3t[:, 1, :],
                                       False, True, [BT, w3t], [pf_])
                                stt(fov[:, mq * 8:(mq + 1) * 8, :], pf_[:].rearrange("p q (l m) -> p (q l) m", l=2), sc_x,
                                    sgv[:, mq * 8:(mq + 1) * 8, :], ALU.mult, ALU.mult, [pf_, sgT], [fo_])
                            r0 = 1024 + g * 256 + kk * 128
                            dma(mixT_d[r0:r0 + 128, 0:L], fo_[:], [fo_], [R_mix], q="sync")
                        for kk in range(2):
                            pc = psy[kk]
                            for t_ in range(2):
                                MM(pc[:, 0:256], Yc_[:, t_, kk * 128:(kk + 1) * 128], cctx[:, t_, 0:256],
                                   t_ == 0, False, [Yc_, cctx], [pc])
                                MM(pc[:, 0:256], Yc_[:, t_, 256 + kk * 128:256 + (kk + 1) * 128], cctx[:, t_, 256:512],
                                   False, t_ == 1, [Yc_, cctx], [pc])
                            stt(foc[:], pc[:, 0:256], sc_c, sgT[:, kk, L:TOK], ALU.mult, ALU.mult, [pc, sgT], [foc])
                            r0 = 1024 + g * 256 + kk * 128
                            dma(mixT_d[r0:r0 + 128, L:TOK], foc[:], [foc], [R_mix], q="gpsimd")
                        S.flush()

        def phase_out(l, wout_d, hin_d, R_hin, ntiles, hout_d, R_hout, is_output):
            with ExitStack() as ph:
                wo = sbuf(ph, "wo", [128, 16, D], BF16)
                for k4 in range(4):
                    dma(wo[:, k4 * 4:(k4 + 1) * 4, :], wout_d[:, k4 * 4:(k4 + 1) * 4, :], [], [wo], q="gpsimd")
                gbc = [sbuf(ph, "gbc%d" % i, [128, D], F32) for i in range(2)]
                dg = sbuf(ph, "dg", [128, 128], F32)
                mx = [sbuf(ph, "mx%d" % i, [128, 16, 512], BF16) for i in range(2)]
                hr = [sbuf(ph, "hr%d" % i, [128, D], F32) for i in range(2)]
                tm = [sbuf(ph, "tm%d" % i, [128, D], F32) for i in range(2)]
                junk = sbuf(ph, "junk", [128, D], BF16)
                ss = sbuf(ph, "ss", [128, 34], F32)
                po = [psum(ph, "po%d" % i, [128, 4, 512], F32) for i in range(2)]
                nseq = 2 if ntiles > 32 else 1
                for i in range(nseq):
                    for c in range(16):
                        ts(dg[:], identf[:], gpT[:, l, i, c:c + 1], None, ALU.mult, None, [identf, gpT], [dg])
                        MM(po[0][:, c // 4, (c % 4) * 128:(c % 4 + 1) * 128], onesf[:], dg[:], True, True,
                           [onesf, dg], [po[0]])
                    for q in range(4):
                        cp(gbc[i][:, q * 512:(q + 1) * 512], po[0][:, q, :], [po[0]], [gbc[i]])
                for tix in range(ntiles):
                    i = 0 if tix < 32 else 1
                    mx_ = mx[(tix // 4) % 2]
                    if tix % 4 == 0:
                        n = min(512, ntiles * 128 - tix * 128)
                        t0 = tix * 128
                        dma(mx_[:, :, 0:n], mixT_d.rearrange("(c p) t -> p c t", p=128)[:, :, t0:t0 + n],
                            [R_mix], [mx_], q="sync")
                    hr_, tm_, po_ = hr[tix % 2], tm[tix % 2], po[tix % 2]
                    dma(hr_[:], hin_d[tix * 128:(tix + 1) * 128, :], [R_hin], [hr_], q="sync")
                    q4 = tix % 4
                    for nn in range(4):
                        for k in range(16):
                            MM(po_[:, nn, :], mx_[:, k, q4 * 128:(q4 + 1) * 128], wo[:, k, nn * 512:(nn + 1) * 512],
                               k == 0, k == 15, [mx_, wo], [po_])
                    act(junk[:], po_[:].rearrange("p a b -> p (a b)"), AF.Square, [po_], [junk, ss],
                        accum_out=ss[:, tix:tix + 1])
                    ts(ss[:, tix:tix + 1], ss[:, tix:tix + 1], 1.0 / D, EPS, ALU.mult, ALU.add, [ss], [ss])
                    act(ss[:, tix:tix + 1], ss[:, tix:tix + 1], AF.Sqrt, [ss], [ss])
                    V(lambda e, a=ss[:, tix:tix + 1]: e.reciprocal(out=a, in_=a), [ss], [ss])
                    tt(tm_[:], po_[:].rearrange("p a b -> p (a b)"), gbc[i][:], ALU.mult, [po_, gbc[i]], [tm_])
                    stt(tm_[:], tm_[:], ss[:, tix:tix + 1], hr_[:], ALU.mult, ALU.add, [tm_, ss, hr_], [tm_])
                    dma(hout_d[tix * 128:(tix + 1) * 128, :], tm_[:], [tm_], [R_hout], q="gpsimd", is_output=is_output)
                S.flush()

        def phase_na():
            with ExitStack() as ph:
                ur = sbuf(ph, "ur", [128, 16, 2560], BF16)
                uv = uT_d.rearrange("(c p) t -> p c t", p=128)
                for q in range(4):
                    dma(ur[:, q * 4:(q + 1) * 4, 0:NKV], uv[:, q * 4:(q + 1) * 4, 0:NKV], [R_uT], [ur])
                dma(ur[:, :, NKV:2560], uv[:, :, L:TOK], [R_uT], [ur], q="sync")
                wq = [sbuf(ph, "wq%d" % i, [128, 4, 16, 128], BF16) for i in range(2)]
                bt = [sbuf(ph, "bt%d" % i, [128, 3, 5, 128], BF16) for i in range(2)]
                qT = [sbuf(ph, "qT%d" % i, [128, OWN], BF16) for i in range(2)]
                kT = [sbuf(ph, "kT%d" % i, [128, 2560], BF16) for i in range(2)]
                sg = [sbuf(ph, "sg%d" % i, [128, OWN], BF16) for i in range(2)]
                Vh = [sbuf(ph, "Vh%d" % i, [128, 20, 128], BF16) for i in range(2)]
                PT = [sbuf(ph, "PT%d" % i, [128, 7, 128], BF16) for i in range(2)]
                rd = [sbuf(ph, "rd%d" % i, [128, 128], F32) for i in range(2)]
                ot = [sbuf(ph, "ot%d" % i, [128, 128], F32) for i in range(2)]
                naT = [sbuf(ph, "naT%d" % i, [128, OWN], BF16) for i in range(2)]
                pp = [psum(ph, "pp%d" % i, [128, 512], F32) for i in range(2)]
                pS = [psum(ph, "pS%d" % i, [128, 8, 128], F32) for i in range(2)]
                pO = [psum(ph, "pO%d" % i, [128, 2, 128], F32) for i in range(2)]
                kp = 0
                isq = 1.0 / math.sqrt(128.0)
                for h in range(12):
                    w_ = wq[h % 2]
                    bt_ = bt[h % 2]
                    for i, mt in enumerate((h, 12 + h, 24 + h, 36 + h)):
                        dma(w_[:, i, :, :], w_in1[mt, :, :, :], [], [w_], q="gpsimd")
                    dma(bt_[:], na_bias[:, h, :, :, :].rearrange("c p k q -> p c k q"), [], [bt_], q="gpsimd")
                    qT_, kT_, sg_, Vh_, naT_ = qT[h % 2], kT[h % 2], sg[h % 2], Vh[h % 2], naT[h % 2]
                    for ci in range(5):
                        t0 = ci * 512
                        ps = pp[kp % 2]; kp += 1
                        for k in range(16):
                            MM(ps[:], w_[:, 1, k, :], ur[:, k, t0:t0 + 512], k == 0, k == 15, [w_, ur], [ps])
                        cp(kT_[:, t0:t0 + 512], ps[:], [ps], [kT_])
                        if ci < 4:
                            ps = pp[kp % 2]; kp += 1
                            for k in range(16):
                                MM(ps[:], w_[:, 0, k, :], ur[:, k, t0:t0 + 512], k == 0, k == 15, [w_, ur], [ps])
                            act(qT_[:, t0:t0 + 512], ps[:], AF.Copy, [ps], [qT_], scale=isq)
                            ps = pp[kp % 2]; kp += 1
                            for k in range(16):
                                MM(ps[:], w_[:, 3, k, :], ur[:, k, t0:t0 + 512], k == 0, k == 15, [w_, ur], [ps])
                            act(sg_[:, t0:t0 + 512], ps[:], AF.Silu, [ps], [sg_])
                    for t4 in range(5):
                        ps = pp[kp % 2]; kp += 1
                        for q in range(4):
                            tix = t4 * 4 + q
                            for k in range(16):
                                MM(ps[:, q * 128:(q + 1) * 128], ur[:, k, tix * 128:(tix + 1) * 128], w_[:, 2, k, :],
                                   k == 0, k == 15, [ur, w_], [ps])
                        cp(Vh_[:, t4 * 4:(t4 + 1) * 4, :], ps[:].rearrange("p (a b) -> p a b", b=128), [ps], [Vh_])
                    for j in range(16):
                        cls = min(j, 2)
                        ws = min(max(2 * j - 4, 0), 26)
                        pS_, pO_, PT_ = pS[j % 2], pO[j % 2], PT[j % 2]
                        rd_, ot_ = rd[j % 2], ot[j % 2]
                        for kt in range(7):
                            k0 = ws * 64 + kt * 128 if kt < 5 else NKV + (kt - 5) * 128
                            MM(pS_[:, kt, :], kT_[:, k0:k0 + 128], qT_[:, j * 128:(j + 1) * 128], True, kt >= 5,
                               [kT_, qT_], [pS_])
                            if kt < 5:
                                MM(pS_[:, kt, :], identb[:], bt_[:, cls, kt, :], False, True, [identb, bt_], [pS_])
                        act(PT_[:, 0:4, :], pS_[:, 0:4, :], AF.Exp, [pS_], [PT_])
                        act(PT_[:, 4:7, :], pS_[:, 4:7, :], AF.Exp, [pS_], [PT_])
                        for kt in range(7):
                            vt = ws // 2 + kt if kt < 5 else 18 + (kt - 5)
                            MM(pO_[:, 0, :], Vh_[:, vt, :], PT_[:, kt, :], kt == 0, kt == 6, [Vh_, PT_], [pO_])
                        for kt in range(7):
                            MM(pO_[:, 1, :], onesb[:], PT_[:, kt, :], kt == 0, kt == 6, [onesb, PT_], [pO_])
                        V(lambda e, o=rd_[:], a=pO_[:, 1, :]: e.reciprocal(out=o, in_=a), [pO_], [rd_])
                        tt(ot_[:], pO_[:, 0, :], rd_[:], ALU.mult, [pO_, rd_], [ot_])
                        tt(naT_[:, j * 128:(j + 1) * 128], ot_[:], sg_[:, j * 128:(j + 1) * 128], ALU.mult,
                           [ot_, sg_], [naT_])
                    dma(mixT_d[h * 128:(h + 1) * 128, 0:OWN], naT_[:], [naT_], [R_mix], q="sync")
                S.flush()

        def phase_s5():
            with ExitStack() as ph:
                dT = sbuf(ph, "dT", [128, 4, TOK], BF16)
                sdg = sbuf(ph, "sdg", [128, 4, OWN], BF16)
                ysb = sbuf(ph, "ysb", [128, 4, OWN], F32)
                with ExitStack() as p1:
                    ut_tiles = [sbuf(p1, "ut%d" % i, [128, 16, 512], BF16) for i in range(2)]
                    ps_tiles = [psum(p1, "ps%d" % i, [128, 512], F32) for i in range(4)]

                    def epi(i, ci, t0, n, ps):
                        if i < 4:
                            cp(dT[:, i, t0:t0 + n], ps[:, 0:n], [ps], [dT])
                        elif t0 < OWN:
                            act(sdg[:, i - 4, t0:t0 + n], ps[:, 0:n], AF.Silu, [ps], [sdg])

                    proj_fm(p1, w_in1, [48 + i for i in range(8)], CHUNKS, epi, "s5p", ps_tiles, ut_tiles)
                    S.flush()
                with ExitStack() as p2:
                    def small(name):
                        return sbuf(p2, name, [128, 2, 16], F32)
                    are, aim, ldt = small("are"), small("aim"), small("ldt")
                    dtt, rr, thp, cfr, cfi = small("dtt"), small("rr"), small("thp"), small("cfr"), small("cfi")
                    w1, w2, w3, w4 = small("w1"), small("w2"), small("w3"), small("w4")
                    wi = sbuf(p2, "wi", [128, 2, 16], I32)
                    dma(are[:], s5are[:, :, :], [], [are], q="sync")
                    dma(aim[:], s5aim[:, :, :], [], [aim], q="sync")
                    dma(ldt[:], s5ldt[:, :, :], [], [ldt], q="sync")
                    act(dtt[:], ldt[:], AF.Exp, [ldt], [dtt])
                    tt(w1[:], are[:], dtt[:], ALU.mult, [are, dtt], [w1])
                    act(rr[:], w1[:], AF.Exp, [w1], [rr])
                    tt(w1[:], aim[:], dtt[:], ALU.mult, [aim, dtt], [w1])
                    ts(thp[:], w1[:], 1.0 / (2.0 * math.pi), None, ALU.mult, None, [w1], [thp])

                    def sincos(src, sin_out, cos_out):
                        cp(wi[:], src[:], [src], [wi])
                        cp(w2[:], wi[:], [wi], [w2])
                        tt(w2[:], src[:], w2[:], ALU.subtract, [src, w2], [w2])
                        act(sin_out[:], w2[:], AF.Sin, [w2], [sin_out], scale=6.28318)
                        ts(w3[:], src[:], 0.25, None, ALU.add, None, [src], [w3])
                        cp(wi[:], w3[:], [w3], [wi])
                        cp(w2[:], wi[:], [wi], [w2])
                        tt(w2[:], w3[:], w2[:], ALU.subtract, [w3, w2], [w2])
                        act(cos_out[:], w2[:], AF.Sin, [w2], [cos_out], scale=6.28318)

                    sn, cs_ = small("sn"), small("cs_")
                    sincos(thp, sn, cs_)
                    tt(w1[:], rr[:], cs_[:], ALU.mult, [rr, cs_], [w1])
                    ts(w1[:], w1[:], -1.0, None, ALU.add, None, [w1], [w1])
                    tt(w4[:], rr[:], sn[:], ALU.mult, [rr, sn], [w4])
                    tt(w2[:], are[:], are[:], ALU.mult, [are], [w2])
                    tt(w3[:], aim[:], aim[:], ALU.mult, [aim], [w3])
                    tt(w2[:], w2[:], w3[:], ALU.add, [w2, w3], [w2])
                    V(lambda e: e.reciprocal(out=w2[:], in_=w2[:]), [w2], [w2])
                    tt(cfr[:], w1[:], are[:], ALU.mult, [w1, are], [cfr])
                    tt(w3[:], w4[:], aim[:], ALU.mult, [w4, aim], [w3])
                    tt(cfr[:], cfr[:], w3[:], ALU.add, [cfr, w3], [cfr])
                    tt(cfr[:], cfr[:], w2[:], ALU.mult, [cfr, w2], [cfr])
                    tt(cfi[:], w4[:], are[:], ALU.mult, [w4, are], [cfi])
                    tt(w3[:], w1[:], aim[:], ALU.mult, [w1, aim], [w3])
                    tt(cfi[:], cfi[:], w3[:], ALU.subtract, [cfi, w3], [cfi])
                    tt(cfi[:], cfi[:], w2[:], ALU.mult, [cfi, w2], [cfi])

                    braw = sbuf(p2, "braw", [128, 2, 2, 16, 128], BF16)
                    ctw = sbuf(p2, "ctw", [128, 2, 2, 16, 128], BF16)
                    dma(braw[:], s5braw[:, :, :, :, :], [], [braw], q="gpsimd")
                    dma(ctw[:], s5ct[:, :, :, :, :], [], [ctw], q="gpsimd")
                    ts(ctw[:, 1], ctw[:, 1], -1.0, None, ALU.mult, None, [ctw], [ctw])
                    ctw2 = sbuf(p2, "ctw2", [128, 2, 2, 16, 128], BF16)
                    ctmp = sbuf(p2, "ctmp", [128, 128], F32)
                    for d_ in range(2):
                        for j in range(16):
                            c0, c1 = ctw[:, 0, d_, j, :], ctw[:, 1, d_, j, :]
                            ts(ctmp[:], c1, cfi[:, d_, j:j + 1], None, ALU.mult, None, [ctw, cfi], [ctmp])
                            stt(ctw2[:, 0, d_, j, :], c0, cfr[:, d_, j:j + 1], ctmp[:], ALU.mult, ALU.add,
                                [ctw, cfr, ctmp], [ctw2])
                            ts(ctmp[:], c0, cfi[:, d_, j:j + 1], None, ALU.mult, None, [ctw, cfi], [ctmp])
                            stt(ctw2[:, 1, d_, j, :], c1, cfr[:, d_, j:j + 1], ctmp[:], ALU.mult, ALU.subtract,
                                [ctw, cfr, ctmp], [ctw2])
                    sv = sbuf(p2, "sv", [128, 513], F32)
                    dma(sv[:], svals_d[:, :], [], [sv], q="sync")
                    ones5 = sbuf(p2, "ones5", [128, 512], F32)
                    V(lambda e: e.memset(ones5[:], 1.0), [], [ones5])
                    a1s = [sbuf(p2, "a1%d" % i, [128, 513], F32) for i in range(2)]
                    a2s = [sbuf(p2, "a2%d" % i, [128, 513], F32) for i in range(2)]
                    ais = [sbuf(p2, "ai%d" % i, [128, 513], I32) for i in range(2)]
                    sinTs = [sbuf(p2, "sinT%d" % i, [128, 513], F32) for i in range(2)]
                    cosTs = [sbuf(p2, "cosT%d" % i, [128, 513], F32) for i in range(2)]
                    rfill = sbuf(p2, "rfill", [128, 512], F32)
                    m1 = sbuf(p2, "m1", [128, 512], F32)
                    m2 = sbuf(p2, "m2", [128, 512], F32)
                    g1_ = sbuf(p2, "g1_", [128, 512], F32)
                    g2_ = sbuf(p2, "g2_", [128, 512], F32)
                    bpr = sbuf(p2, "bpr", [128, 512], F32)
                    bpi = sbuf(p2, "bpi", [128, 512], F32)
                    krs = [sbuf(p2, "kr%d" % i, [128, 512], F32) for i in range(2)]
                    kis = [sbuf(p2, "ki%d" % i, [128, 512], F32) for i in range(2)]
                    ini = sbuf(p2, "ini", [128, 4], F32)
                    hh = [sbuf(p2, "hh%d" % d_, [128, 2, OWN], BF16) for d_ in range(2)]
                    pbu = [psum(p2, "pbu%d" % i, [128, 512], F32) for i in range(4)]
                    py = [psum(p2, "py%d" % i, [128, 512], F32) for i in range(4)]
                    seqs = [
                        [(L, 256, False, None)] + [(i * 512, 512, False, i * 512) for i in range(4)],
                        [(L, 256, True, None)] + [(i * 512, 512, True, (i * 512 if i < 4 else None))
                                                  for i in range(7, -1, -1)],
                    ]
                    kb = 0
                    for j in range(16):
                        kc = j // 4
                        for d_ in range(2):
                            sinT, cosT = sinTs[d_], cosTs[d_]
                            a1, a2, ai = a1s[d_], a2s[d_], ais[d_]
                            ts(a1[:], sv[:], thp[:, d_, j:j + 1], None, ALU.mult, None, [sv, thp], [a1])
                            cp(ai[:], a1[:], [a1], [ai])
                            cp(a2[:], ai[:], [ai], [a2])
                            tt(a2[:], a1[:], a2[:], ALU.subtract, [a1, a2], [a2])
                            act(sinT[:], a2[:], AF.Sin, [a2], [sinT], scale=6.28318)
                            ts(a1[:], a1[:], 0.25, None, ALU.add, None, [a1], [a1])
                            cp(ai[:], a1[:], [a1], [ai])
                            cp(a2[:], ai[:], [ai], [a2])
                            tt(a2[:], a1[:], a2[:], ALU.subtract, [a1, a2], [a2])
                            act(cosT[:], a2[:], AF.Sin, [a2], [cosT], scale=6.28318)
                            ts(rfill[:], ones5[:], rr[:, d_, j:j + 1], None, ALU.mult, None, [ones5, rr], [rfill])
                            first = True
                            for (m0, n, rev, own) in seqs[d_]:
                                pr, pi = pbu[kb % 4], pbu[(kb + 1) % 4]
                                kr, ki = krs[(kb // 2) % 2], kis[(kb // 2) % 2]
                                kb += 2
                                MM(pr[:, 0:n], braw[:, 0, d_, j, :], dT[:, kc, m0:m0 + n], True, True, [braw, dT], [pr])
                                MM(pi[:, 0:n], braw[:, 1, d_, j, :], dT[:, kc, m0:m0 + n], True, True, [braw, dT], [pi])
                                ur_ = pr[:, 0:n][:, ::-1] if rev else pr[:, 0:n]
                                ui_ = pi[:, 0:n][:, ::-1] if rev else pi[:, 0:n]
                                tt(m1[:, 0:n], ur_, cosT[:, 0:n], ALU.mult, [pr, cosT], [m1])
                                tt(m2[:, 0:n], ui_, sinT[:, 0:n], ALU.mult, [pi, sinT], [m2])
                                tt(bpr[:, 0:n], m1[:, 0:n], m2[:, 0:n], ALU.add, [m1, m2], [bpr])
                                tt(m1[:, 0:n], ui_, cosT[:, 0:n], ALU.mult, [pi, cosT], [m1])
                                tt(m2[:, 0:n], ur_, sinT[:, 0:n], ALU.mult, [pr, sinT], [m2])
                                tt(bpi[:, 0:n], m1[:, 0:n], m2[:, 0:n], ALU.subtract, [m1, m2], [bpi])
                                i_r = 0.0 if first else ini[:, 0:1]
                                i_i = 0.0 if first else ini[:, 1:2]
                                V(lambda e, o=kr[:, 0:n], a=rfill[:, 0:n], b=bpr[:, 0:n], iv=i_r:
                                  e.tensor_tensor_scan(out=o, data0=a, data1=b, initial=iv, op0=ALU.mult, op1=ALU.add),
                                  [rfill, bpr, ini], [kr])
                                V(lambda e, o=ki[:, 0:n], a=rfill[:, 0:n], b=bpi[:, 0:n], iv=i_i:
                                  e.tensor_tensor_scan(out=o, data0=a, data1=b, initial=iv, op0=ALU.mult, op1=ALU.add),
                                  [rfill, bpi, ini], [ki])
                                first = False
                                tt(ini[:, 2:3], ki[:, n - 1:n], sinT[:, n:n + 1], ALU.mult, [ki, sinT], [ini])
                                tt(ini[:, 3:4], kr[:, n - 1:n], sinT[:, n:n + 1], ALU.mult, [kr, sinT], [ini])
                                stt(ini[:, 0:1], kr[:, n - 1:n], cosT[:, n:n + 1], ini[:, 2:3], ALU.mult, ALU.subtract,
                                    [kr, cosT, ini], [ini])
                                stt(ini[:, 1:2], ki[:, n - 1:n], cosT[:, n:n + 1], ini[:, 3:4], ALU.mult, ALU.add,
                                    [ki, cosT, ini], [ini])
                                if own is not None:
                                    h_ = hh[d_]
                                    o_r = h_[:, 0, own:own + n]
                                    o_i = h_[:, 1, own:own + n]
                                    if rev:
                                        o_r = o_r[:, ::-1]
                                        o_i = o_i[:, ::-1]
                                    tt(g1_[:, 0:n], cosT[:, 0:n], kr[:, 0:n], ALU.mult, [cosT, kr], [g1_], eng="gpsimd")
                                    tt(g2_[:, 0:n], sinT[:, 0:n], ki[:, 0:n], ALU.mult, [sinT, ki], [g2_], eng="gpsimd")
                                    tt(o_r, g1_[:, 0:n], g2_[:, 0:n], ALU.subtract, [g1_, g2_], [h_], eng="gpsimd")
                                    tt(g1_[:, 0:n], cosT[:, 0:n], ki[:, 0:n], ALU.mult, [cosT, ki], [g1_], eng="gpsimd")
                                    tt(g2_[:, 0:n], sinT[:, 0:n], kr[:, 0:n], ALU.mult, [sinT, kr], [g2_], eng="gpsimd")
                                    tt(o_i, g1_[:, 0:n], g2_[:, 0:n], ALU.add, [g1_, g2_], [h_], eng="gpsimd")
                        for tc in range(4):
                            for d_ in range(2):
                                for ri in range(2):
                                    MM(py[tc][:], ctw2[:, ri, d_, j, :], hh[d_][:, ri, tc * 512:(tc + 1) * 512],
                                       (j % 4 == 0 and d_ == 0 and ri == 0), (j % 4 == 3 and d_ == 1 and ri == 1),
                                       [ctw2, hh[d_]], [py[tc]])
                        if j % 4 == 3:
                            for tc in range(4):
                                cp(ysb[:, kc, tc * 512:(tc + 1) * 512], py[tc][:], [py[tc]], [ysb])
                    S.flush()
                with ExitStack() as p2:
                    pbu = [psum(p2, "pbu%d" % i, [128, 512], F32) for i in range(4)]
                    dcol = sbuf(p2, "dcol", [128, 4], F32)
                    dma(dcol[:], s5d[:, :], [], [dcol], q="sync")
                    wg = sbuf(p2, "wg", [128, 4, 4, 128], BF16)
                    for mt in range(4):
                        dma(wg[:, mt, :, :], w_glu[mt, :, :, :], [], [wg], q="gpsimd")
                    yb = sbuf(p2, "yb", [128, 4, OWN], BF16)
                    g1 = sbuf(p2, "g1", [128, OWN], F32)
                    g2 = sbuf(p2, "g2", [128, OWN], F32)
                    for kc in range(4):
                        stt(ysb[:, kc, :], dT[:, kc, 0:OWN], dcol[:, kc:kc + 1], ysb[:, kc, :], ALU.mult, ALU.add,
                            [dT, dcol, ysb], [ysb])
                        tt(g1[:], ysb[:, kc, :], ysb[:, kc, :], ALU.mult, [ysb], [g1])
                        ts(g1[:], g1[:], 0.044715, 1.0, ALU.mult, ALU.add, [g1], [g1])
                        tt(g1[:], g1[:], ysb[:, kc, :], ALU.mult, [g1, ysb], [g1])
                        act(g2[:], g1[:], AF.Sigmoid, [g1], [g2], scale=1.5957691216057308)
                        tt(yb[:, kc, :], ysb[:, kc, :], g2[:], ALU.mult, [ysb, g2], [yb])
                    mo = [sbuf(p2, "mo%d" % i, [128, 512], BF16) for i in range(2)]
                    km = 0
                    for mt in range(4):
                        for tc in range(4):
                            ps = pbu[km % 4]
                            mo_ = mo[km % 2]
                            km += 1
                            for k in range(4):
                                MM(ps[:], wg[:, mt, k, :], yb[:, k, tc * 512:(tc + 1) * 512], k == 0, k == 3,
                                   [wg, yb], [ps])
                            act(g1[:, 0:512], ps[:], AF.Sigmoid, [ps], [g1])
                            tt(g1[:, 0:512], g1[:, 0:512], yb[:, mt, tc * 512:(tc + 1) * 512], ALU.mult, [g1, yb], [g1])
                            tt(mo_[:], g1[:, 0:512], sdg[:, mt, tc * 512:(tc + 1) * 512], ALU.mult, [g1, sdg], [mo_])
                            r0 = 1536 + mt * 128
                            dma(mixT_d[r0:r0 + 128, tc * 512:(tc + 1) * 512], mo_[:], [mo_], [R_mix], q="gpsimd")
                    S.flush()

        phase_p1(0, xin, T(None))
        phase_conv()
        phase_fourier()
        if debug == "l0":
            phase_out(0, w_out0, xin, T(None), 34, h1_d, R_h1, True)
            V(lambda e: e.memset(onesf[:], 1.0), [], [onesf])
            S.flush(final=True)
            return nc
        phase_out(0, w_out0, xin, T(None), 34, h1_d, R_h1, False)
        phase_p1(1, h1_d, R_h1)
        phase_na()
        phase_s5()
        phase_out(1, w_out1, h1_d, R_h1, 16, out_d, T(None), True)
        V(lambda e: e.memset(onesf[:], 1.0), [], [onesf])
        S.flush(final=True)
    return nc


_CONST_CACHE = {}


def _consts(inp):
    if "dft" not in _CONST_CACHE:
        _CONST_CACHE["dft"] = [_dft_consts(0), _dft_consts(1)]
    rpb = inp["na_rpb"][0]
    return {"dft": _CONST_CACHE["dft"], "na_bias": [_na_tables(rpb, 0), _na_tables(rpb, 1)]}


def kernel(**inputs):
    inp = {k: np.asarray(v) for k, v in inputs.items()}
    consts = _consts(inp)
    nc = build()
    in_maps = [prep_core(inp, core, consts) for core in range(8)]
    res = run_bass_kernel_spmd(nc, in_maps, core_ids=list(range(8)))
    out = np.empty((4, L, D), np.float32)
    for core in range(8):
        b, par = core // 2, core % 2
        o = res.results[core]["out"]
        if par:
            out[b, L - OWN:] = o[::-1]
        else:
            out[b, :OWN] = o
    return out
```

```python
import math
from contextlib import ExitStack
import numpy as np
import ml_dtypes
import concourse.bass as bass
import concourse.mybir as mybir
from concourse.bass_utils import run_bass_kernel_spmd

F32 = mybir.dt.float32
BF16 = mybir.dt.bfloat16
I32 = mybir.dt.int32
AF = mybir.ActivationFunctionType
ALU = mybir.AluOpType

ENGS = ("tensor", "vector", "scalar", "gpsimd", "sync")
NDMA = 24
D = 2048
L = 4096
LC = 256
TOK = L + LC
OWN = 2048
NKV = 2304
EPS = 1e-6
CHUNKS = [(i * 512, 512) for i in range(8)] + [(4096, 256)]
BF = ml_dtypes.bfloat16


class Res:
    __slots__ = ("w", "r")

    def __init__(self):
        self.w = None
        self.r = {}


class T:
    def __init__(self, t):
        self.t = t
        self.res = Res()

    def __getitem__(self, idx):
        return self.t[idx]


class Sched:
    def __init__(self, nc, sems):
        self.nc = nc
        self.sems = sems
        self.q = {e: [] for e in ENGS}
        self.cnt = {e: 0 for e in ENGS}
        self.seen = {e: {} for e in ENGS}
        self.dma_rr = 0
        self.dma_cnt = [0] * NDMA
        self.out_toks = []
        self.dq = 0
        self.barrier = {}

    def _deps(self, eng, reads, writes, pe_chain):
        need = {}

        def add(tok):
            if tok is None:
                return
            s, v = tok
            if pe_chain and s == "tensor" and eng == "tensor":
                return
            if need.get(s, 0) < v:
                need[s] = v

        for r in reads:
            add(r.res.w)
        for w in writes:
            add(w.res.w)
            for s, v in w.res.r.items():
                add((s, v))
        for s, v in self.barrier.items():
            if need.get(s, 0) < v:
                need[s] = v
        waits = []
        for s, v in need.items():
            if self.seen[eng].get(s, 0) < v:
                waits.append((s, v))
                self.seen[eng][s] = v
        return waits

    def _commit(self, tok, reads, writes):
        s, v = tok
        for r in reads:
            if r.res.r.get(s, 0) < v:
                r.res.r[s] = v
        for w in writes:
            w.res.w = tok
            w.res.r = {}

    def op(self, eng, emit, reads=(), writes=(), pe_chain=False):
        waits = self._deps(eng, reads, writes, pe_chain)
        self.cnt[eng] += 1
        tok = (eng, self.cnt[eng])
        self.q[eng].append((waits, emit, (eng, 1)))
        self._commit(tok, reads, writes)
        return tok

    def dma(self, emit, reads=(), writes=(), q=None, is_output=False):
        if q is None:
            q = ("sync", "gpsimd")[self.dq % 2]
            self.dq += 1
        slot = self.dma_rr % NDMA
        self.dma_rr += 1
        semkey = ("dma", slot)
        waits = self._deps(q, reads, writes, False)
        prev = self.dma_cnt[slot]
        if prev and self.seen[q].get(semkey, 0) < prev:
            waits.append((semkey, prev))
            self.seen[q][semkey] = prev
        self.dma_cnt[slot] = prev + 16
        tok = (semkey, prev + 16)
        self.q[q].append((waits, emit, (semkey, 16)))
        self._commit(tok, reads, writes)
        if is_output:
            self.out_toks.append(tok)
        return tok

    def flush(self, final=False):
        if final:
            fin = {}
            for s, v in self.out_toks:
                fin[s] = max(fin.get(s, 0), v)
            for e in ENGS:
                if self.cnt[e]:
                    fin[e] = max(fin.get(e, 0), self.cnt[e])
            for s in range(NDMA):
                if self.dma_cnt[s]:
                    fin[("dma", s)] = self.dma_cnt[s]
            self.q["sync"].append((list(fin.items()), None, None))
        qs = self.q
        sems = self.sems

        def run(name):
            def body(e):
                for waits, emit, inc in qs[name]:
                    for s, v in waits:
                        e.wait_ge(sems[s], v)
                    if emit is not None:
                        emit(e).then_inc(sems[inc[0]], inc[1])
            return body

        with self.nc.Block() as block:
            block.tensor(run("tensor"))
            block.vector(run("vector"))
            block.scalar(run("scalar"))
            block.gpsimd(run("gpsimd"))
            block.sync(run("sync"))
        self.q = {e: [] for e in ENGS}
        self.barrier = {e: self.cnt[e] for e in ENGS if self.cnt[e]}
        for s_ in range(NDMA):
            if self.dma_cnt[s_]:
                self.barrier[("dma", s_)] = self.dma_cnt[s_]


def _cols(v, n):
    return np.ascontiguousarray(v.reshape(n, 128).T)


def _mt(W):
    K, F = W.shape
    return np.ascontiguousarray(W.reshape(K // 128, 128, F // 128, 128).transpose(2, 1, 0, 3))


def _kt(W):
    K, F = W.shape
    return np.ascontiguousarray(W.reshape(K // 128, 128, F).transpose(1, 0, 2))


def _na_tables(rpb, par):
    out = np.full((3, 12, 640, 128), -30000.0, np.float32)
    for cls, j in enumerate((0, 1, 2)):
        ws = min(max(2 * j - 4, 0), 26)
        qi = np.arange(128)
        qr_o = 2 * j + qi // 64
        qc_o = qi % 64
        ki = np.arange(640)
        kr_o = ws + ki // 64
        kc_o = ki % 64
        if par:
            qr, qc, kr, kc = 63 - qr_o, 63 - qc_o, 63 - kr_o, 63 - kc_o
        else:
            qr, qc, kr, kc = qr_o, qc_o, kr_o, kc_o
        rs = np.clip(qr - 4, 0, 56)
        cs = np.clip(qc - 8, 0, 48)
        ok = ((kr[:, None] >= rs[None]) & (kr[:, None] < rs[None] + 8) &
              (kc[:, None] >= cs[None]) & (kc[:, None] < cs[None] + 16))
        dr = np.clip(kr[:, None] - qr[None] + 7, 0, 14)
        dc = np.clip(kc[:, None] - qc[None] + 15, 0, 30)
        g = rpb[:, dr, dc]
        out[cls] = np.where(ok[None], g, np.float32(-30000.0))
    return np.ascontiguousarray(out.reshape(3, 12, 5, 128, 128).transpose(0, 1, 3, 2, 4))


def _dft_consts(par):
    n = np.arange(256)
    a = 2.0 * np.pi * ((n[:, None] * n[None, :]) % 256) / 256.0
    c256, s256 = np.cos(a), np.sin(a)
    cs256 = np.concatenate([c256, s256], axis=1)
    cs256 = cs256.reshape(2, 128, 512).transpose(1, 0, 2)
    if par:
        cc, sc = c256[::-1, ::-1], s256[::-1, ::-1]
    else:
        cc, sc = c256, s256
    cctx = np.concatenate([cc, -sc], axis=1).reshape(2, 128, 512).transpose(1, 0, 2)
    a = np.arange(64)
    if par:
        e1 = (a[:, None] * (a[None, :] + 1)) % 64
        e3 = ((a[:, None] + 1) * a[None, :]) % 64
        e2 = ((a[:, None] + 1) * (a[None, :] + 1)) % 4096
    else:
        e1 = (a[:, None] * a[None, :]) % 64
        e3 = e1
        e2 = (a[:, None] * a[None, :]) % 4096

    def bd(m):
        z = np.zeros((128, 128))
        z[:64, :64] = m
        z[64:, 64:] = m
        return z

    th1 = 2.0 * np.pi * e1 / 64.0
    th3 = 2.0 * np.pi * e3 / 64.0
    th2 = 2.0 * np.pi * e2 / 4096.0
    w1 = np.stack([bd(np.cos(th1)), bd(-np.sin(th1)), bd(-np.cos(th1))], axis=1)
    w3 = np.stack([bd(np.cos(th3)), bd(np.sin(th3))], axis=1)
    tw = np.stack([np.cos(th2).reshape(-1), np.sin(th2).reshape(-1)], axis=0)
    tw = np.broadcast_to(tw[None], (128, 2, 4096))
    return (np.ascontiguousarray(cs256).astype(BF), np.ascontiguousarray(cctx).astype(BF),
            np.ascontiguousarray(w1).astype(BF), np.ascontiguousarray(w3).astype(BF),
            np.ascontiguousarray(tw).astype(BF))


def _s5_layout(inp, par):
    dirs = (1, 0) if par else (0, 1)
    G, P, H = 32, 64, 16
    out = {}
    a_re = inp["s5_a_re"][0][list(dirs)]
    a_im = inp["s5_a_im"][0][list(dirs)]
    ldt = inp["s5_log_dt"][0][list(dirs)]

    def st(v):
        return np.ascontiguousarray(v.reshape(2, 16, 2, 64).transpose(2, 3, 0, 1).reshape(128, 2, 16))

    out["s5are"] = st(a_re)
    out["s5aim"] = st(a_im)
    out["s5ldt"] = st(np.broadcast_to(ldt[:, :, None], (2, G, P)))
    b_re = inp["s5_b_re"][0][list(dirs)]
    b_im = inp["s5_b_im"][0][list(dirs)]
    c_re = inp["s5_c_re"][0][list(dirs)]
    c_im = inp["s5_c_im"][0][list(dirs)]
    braw = np.zeros((2, 2, 16, 128, 128), np.float32)
    ct = np.zeros((2, 2, 16, 128, 128), np.float32)
    for g in range(G):
        j = g // 2
        r0 = (g % 8) * 16
        s0 = (g % 2) * 64
        for ri, (bb, cc) in enumerate(((b_re, c_re), (b_im, c_im))):
            braw[ri, :, j, r0:r0 + 16, s0:s0 + 64] = bb[:, g].transpose(0, 2, 1)
            ct[ri, :, j, s0:s0 + 64, r0:r0 + 16] = cc[:, g].transpose(0, 2, 1)
    out["s5braw"] = np.ascontiguousarray(braw.transpose(3, 0, 1, 2, 4))
    out["s5ct"] = np.ascontiguousarray(ct.transpose(3, 0, 1, 2, 4))
    return out


def prep_core(inp, core, consts):
    b, par = core // 2, core % 2
    x = inp["x"][b]
    ctx = inp["ctx"][b]
    if par:
        x = x[::-1]
        ctx = ctx[::-1]
    m = {}
    m["xin"] = np.ascontiguousarray(np.concatenate([x, ctx], axis=0))
    m["cvec"] = np.ascontiguousarray(np.stack([_cols(inp["c"][b], 16), _cols(inp["c_ctx"], 16)], axis=2))
    m["ada_w"] = np.ascontiguousarray(inp["ada_w"].reshape(2, 16, 128, 6144).transpose(0, 2, 1, 3))
    m["ada_b"] = np.stack([_cols(inp["ada_b"][l], 48) for l in range(2)], axis=1)
    m["pre_g"] = np.stack([_cols(inp["pre_g"][l], 16) for l in range(2)], axis=1)
    m["post_g"] = np.stack([_cols(inp["post_g"][l], 16) for l in range(2)], axis=1)
    m["w_in0"] = _mt(inp["ab_w_in"][0])
    m["w_out0"] = _kt(inp["ab_w_out"][0])
    cw = inp["conv_w"][0]
    if par:
        cw = cw[::-1]
    m["conv_w"] = np.ascontiguousarray(cw.T.reshape(8, 128, 31).transpose(1, 0, 2))
    m["conv_v"] = np.ascontiguousarray(np.stack([_cols(inp["conv_b"][0], 8), _cols(inp["conv_ln_g"][0], 8),
                                                 _cols(inp["conv_ln_b"][0], 8)], axis=1))
    fg = inp["fourier_g"][0]
    m["four_g"] = np.ascontiguousarray(fg.reshape(4, 2, 128).transpose(2, 0, 1))
    m["w_in1"] = _mt(inp["cd_w_in"][0])
    m["w_out1"] = _kt(inp["cd_w_out"][0])
    m["na_bias"] = consts["na_bias"][par]
    m["s5d"] = _cols(inp["s5_d"][0], 4)
    m["w_glu"] = _mt(inp["s5_w_glu"][0])
    m.update(_s5_layout(inp, par))
    cs256, cctx, fw1, fw3, ftw = consts["dft"][par]
    m["cs256"], m["cctx"], m["fw1"], m["fw3"], m["ftw"] = cs256, cctx, fw1, fw3, ftw
    m["ident"] = np.eye(128, dtype=np.float32)
    m["svals"] = np.ascontiguousarray(np.broadcast_to(np.arange(513, dtype=np.float32)[None], (128, 513)))
    return m


def build(debug=None):
    nc = bass.Bass("TRN2", target_bir_lowering=False)

    def din(name, shape, dtype=F32):
        return nc.dram_tensor(name, list(shape), dtype, kind="ExternalInput").ap()

    def dscr(name, shape, dtype):
        return nc.dram_tensor(name, list(shape), dtype, kind="Internal").ap()

    xin = din("xin", [TOK, D])
    cvec = din("cvec", [128, 16, 2])
    ada_w = din("ada_w", [2, 128, 16, 6144])
    ada_b = din("ada_b", [128, 2, 48])
    pre_g = din("pre_g", [128, 2, 16])
    post_g = din("post_g", [128, 2, 16])
    w_in0 = din("w_in0", [40, 128, 16, 128])
    w_out0 = din("w_out0", [128, 16, 2048])
    conv_w = din("conv_w", [128, 8, 31])
    conv_v = din("conv_v", [128, 3, 8])
    four_g = din("four_g", [128, 4, 2])
    w_in1 = din("w_in1", [56, 128, 16, 128])
    w_out1 = din("w_out1", [128, 16, 2048])
    na_bias = din("na_bias", [3, 12, 128, 5, 128])
    s5d = din("s5d", [128, 4])
    w_glu = din("w_glu", [4, 128, 4, 128])
    s5are = din("s5are", [128, 2, 16])
    s5aim = din("s5aim", [128, 2, 16])
    s5ldt = din("s5ldt", [128, 2, 16])
    s5braw = din("s5braw", [128, 2, 2, 16, 128])
    s5ct = din("s5ct", [128, 2, 2, 16, 128])
    cs256_d = din("cs256", [128, 2, 512], BF16)
    cctx_d = din("cctx", [128, 2, 512], BF16)
    w1_d = din("fw1", [128, 3, 128], BF16)
    w3_d = din("fw3", [128, 2, 128], BF16)
    twd_d = din("ftw", [128, 2, L], BF16)
    ident_d = din("ident", [128, 128])
    svals_d = din("svals", [128, 513])
    out_d = nc.dram_tensor("out", [OWN, D], F32, kind="ExternalOutput").ap()
    h1_kind = "ExternalOutput" if debug == "l0" else "Internal"
    h1_d = nc.dram_tensor("h1", [TOK, D], F32, kind=h1_kind).ap()
    uT_d = dscr("uT", [D, 4608], BF16)
    mixT_d = dscr("mixT", [D, TOK], BF16)
    convT_d = dscr("convT", [1024, TOK], BF16)

    with ExitStack() as top:
        sems = {}
        for e in ENGS:
            sems[e] = top.enter_context(nc.semaphore("s_" + e))
        for i in range(NDMA):
            sems[("dma", i)] = top.enter_context(nc.semaphore("d%d" % i))
        S = Sched(nc, sems)

        uid = [0]

        def sbuf(es, name, shape, dtype):
            uid[0] += 1
            return T(es.enter_context(nc.sbuf_tensor("%s_%d" % (name, uid[0]), list(shape), dtype)))

        def psum(es, name, shape, dtype):
            uid[0] += 1
            return T(es.enter_context(nc.psum_tensor("%s_%d" % (name, uid[0]), list(shape), dtype)))

        R_uT, R_mix, R_conv, R_h1 = T(None), T(None), T(None), T(None)

        identb = sbuf(top, "identb", [128, 128], BF16)
        identf = sbuf(top, "identf", [128, 128], F32)
        onesb = sbuf(top, "onesb", [128, 128], BF16)
        onesf = sbuf(top, "onesf", [128, 128], F32)
        modT = sbuf(top, "modT", [128, 2, 48, 2], F32)
        gsT = sbuf(top, "gsT", [128, 2, 2, 16], F32)
        shT = sbuf(top, "shT", [128, 2, 2, 16], F32)
        gpT = sbuf(top, "gpT", [128, 2, 2, 16], F32)
        preg = sbuf(top, "preg", [128, 2, 16], F32)
        postg = sbuf(top, "postg", [128, 2, 16], F32)

        def V(fn, reads, writes):
            return S.op("vector", fn, reads, writes)

        def A(fn, reads, writes):
            return S.op("scalar", fn, reads, writes)

        def G(fn, reads, writes):
            return S.op("gpsimd", fn, reads, writes)

        def MM(out, lhsT, rhs, start, stop, reads, writes):
            return S.op("tensor", lambda e: e.matmul(out, lhsT=lhsT, rhs=rhs, start=start, stop=stop),
                        reads, writes, pe_chain=True)

        def act(out, in_, func, reads, writes, **kw):
            return A(lambda e: e.activation(out=out, in_=in_, func=func, **kw), reads, writes)

        def tt(out, in0, in1, op, reads, writes, eng="vector"):
            return S.op(eng, lambda e: e.tensor_tensor(out=out, in0=in0, in1=in1, op=op), reads, writes)

        def ts(out, in0, s1, s2, op0, op1, reads, writes, eng="vector"):
            if op1 is None:
                return S.op(eng, lambda e: e.tensor_scalar(out=out, in0=in0, scalar1=s1, scalar2=None, op0=op0),
                            reads, writes)
            return S.op(eng, lambda e: e.tensor_scalar(out=out, in0=in0, scalar1=s1, scalar2=s2, op0=op0, op1=op1),
                        reads, writes)

        def stt(out, in0, scalar, in1, op0, op1, reads, writes):
            return V(lambda e: e.scalar_tensor_tensor(out=out, in0=in0, scalar=scalar, in1=in1, op0=op0, op1=op1),
                     reads, writes)

        def cp(out, in_, reads, writes, eng="vector"):
            return S.op(eng, lambda e: e.tensor_copy(out=out, in_=in_), reads, writes)

        def dma(out, in_, reads, writes, q=None, is_output=False):
            return S.dma(lambda e: e.dma_start(out=out, in_=in_), reads, writes, q=q, is_output=is_output)

        condT = sbuf(top, "condT", [128, 16, 2], F32)
        adab = sbuf(top, "adab", [128, 2, 48], F32)

        def ada_block(l, cb, aw, psA):
            w = aw[cb % 2]
            dma(w[:], ada_w[l, :, :, cb * 512:(cb + 1) * 512], [], [w])
            for m in range(4):
                j = cb * 4 + m
                for k in range(16):
                    MM(psA[:, j, :], w[:, k, m * 128:(m + 1) * 128], condT[:, k, :],
                       k == 0, k == 15, [w, condT], [psA])

        def ada_finalize(l, psA):
            for i in range(2):
                tt(modT[:, l, :, i], psA[:, :, i], adab[:, l, :], ALU.add, [psA, adab], [modT])
            for i in range(2):
                stt(gsT[:, l, i, :], modT[:, l, 16:32, i], 1.0, preg[:, l, :], ALU.add, ALU.mult,
                    [modT, preg], [gsT])
                cp(shT[:, l, i, :], modT[:, l, 0:16, i], [modT], [shT])
                tt(gpT[:, l, i, :], modT[:, l, 32:48, i], postg[:, l, :], ALU.mult, [modT, postg], [gpT])

        with ExitStack() as ph:
            aw = [sbuf(ph, "aw%d" % i, [128, 16, 512], F32) for i in range(2)]
            psA = psum(ph, "psA", [128, 48, 2], F32)
            dma(identf[:], ident_d[:, :], [], [identf], q="sync")
            dma(identb[:], ident_d[:, :], [], [identb], q="gpsimd")
            dma(condT[:], cvec[:, :, :], [], [condT], q="sync")
            dma(adab[:], ada_b[:, :, :], [], [adab], q="sync")
            dma(preg[:], pre_g[:, :, :], [], [preg], q="sync")
            dma(postg[:], post_g[:, :, :], [], [postg], q="sync")
            V(lambda e: e.memset(onesf[:], 1.0), [], [onesf])
            V(lambda e: e.memset(onesb[:], 1.0), [], [onesb])
            act(condT[:], condT[:], AF.Silu, [condT], [condT])
            for cb in range(12):
                ada_block(0, cb, aw, psA)
            ada_finalize(0, psA)
            S.flush()

        def phase_p1(l, h_d, R_h):
            with ExitStack() as ph:
                xs = [sbuf(ph, "xs%d" % i, [128, D], F32) for i in range(2)]
                xn = [sbuf(ph, "xn%d" % i, [128, D], BF16) for i in range(2)]
                junk = sbuf(ph, "junk", [128, D], BF16)
                st = [sbuf(ph, "ust%d" % i, [128, 16, 512], BF16) for i in range(2)]
                ss = sbuf(ph, "ss", [128, 34], F32)
                rs = sbuf(ph, "rs", [128, 34], F32)
                pT = [psum(ph, "pT%d" % i, [128, 16, 128], BF16) for i in range(2)]
                if l == 0:
                    aw1 = [sbuf(ph, "aw1%d" % i, [128, 16, 512], F32) for i in range(2)]
                    psA1 = psum(ph, "psA1", [128, 48, 2], F32)
                for tix in range(34):
                    if l == 0 and tix % 2 == 1 and tix // 2 < 12:
                        ada_block(1, tix // 2, aw1, psA1)
                    i = 0 if tix < 32 else 1
                    x_ = xs[tix % 2]
                    xn_ = xn[tix % 2]
                    p_ = pT[tix % 2]
                    st_ = st[(tix // 4) % 2]
                    dma(x_[:], h_d[tix * 128:(tix + 1) * 128, :], [R_h], [x_])
                    act(junk[:], x_[:], AF.Square, [x_], [junk, ss], accum_out=ss[:, tix:tix + 1])
                    ts(rs[:, tix:tix + 1], ss[:, tix:tix + 1], 1.0 / D, EPS, ALU.mult, ALU.add, [ss], [rs])
                    act(rs[:, tix:tix + 1], rs[:, tix:tix + 1], AF.Sqrt, [rs], [rs])
                    V(lambda e, a=rs[:, tix:tix + 1]: e.reciprocal(out=a, in_=a), [rs], [rs])
                    act(xn_[:], x_[:], AF.Copy, [x_, rs], [xn_], scale=rs[:, tix:tix + 1])
                    for c in range(16):
                        S.op("tensor", lambda e, o=p_[:, c, :], a=xn_[:, c * 128:(c + 1) * 128]:
                             e.transpose(out=o, in_=a, identity=identb[:]), [xn_, identb], [p_], pe_chain=True)
                    q4 = tix % 4
                    for c in range(16):
                        ts(st_[:, c, q4 * 128:(q4 + 1) * 128], p_[:, c, :], gsT[:, l, i, c:c + 1],
                           shT[:, l, i, c:c + 1], ALU.mult, ALU.add, [p_, gsT, shT], [st_])
                    if q4 == 3 or tix == 33:
                        t0 = (tix // 4) * 512
                        n = (q4 + 1) * 128
                        dma(uT_d.rearrange("(c p) t -> p c t", p=128)[:, :, t0:t0 + n], st_[:, :, 0:n], [st_], [R_uT])
                if l == 0:
                    ada_finalize(1, psA1)
                S.flush()

        def load_w(wsb, wsrc, mtiles):
            for i, mt in enumerate(mtiles):
                dma(wsb[:, i, :, :], wsrc[mt, :, :, :], [], [wsb], q="gpsimd")

        rot = {"ps": 0, "ut": 0}

        def proj_fm(ph, wsrc, mtiles, chunks, epilogue, tag, ps_tiles, ut_tiles, wsb=None):
            nm = len(mtiles)
            if wsb is None:
                wsb = sbuf(ph, "w_" + tag, [128, nm, 16, 128], BF16)
                load_w(wsb, wsrc, mtiles)
            for ci, (t0, n) in enumerate(chunks):
                ut = ut_tiles[rot["ut"] % len(ut_tiles)]
                rot["ut"] += 1
                dma(ut[:, :, 0:n], uT_d.rearrange("(c p) t -> p c t", p=128)[:, :, t0:t0 + n], [R_uT], [ut], q="sync")
                for i in range(nm):
                    ps = ps_tiles[rot["ps"] % len(ps_tiles)]
                    rot["ps"] += 1
                    for k in range(16):
                        MM(ps[:, 0:n], wsb[:, i, k, :], ut[:, k, 0:n], k == 0, k == 15, [wsb, ut], [ps])
                    epilogue(i, ci, t0, n, ps)

        def phase_conv():
            with ExitStack() as ph:
                cw = sbuf(ph, "cw", [128, 8, 31], F32)
                cv = sbuf(ph, "cv", [128, 3, 8], F32)
                s1 = sbuf(ph, "s1", [128, TOK], F32)
                s2 = sbuf(ph, "s2", [128, TOK], F32)
                dma(cw[:], conv_w[:, :, :], [], [cw], q="sync")
                dma(cv[:], conv_v[:, :, :], [], [cv], q="sync")
                with ExitStack() as p1:
                    ut_tiles = [sbuf(p1, "ut%d" % i, [128, 16, 512], BF16) for i in range(2)]
                    ps_tiles = [psum(p1, "ps%d" % i, [128, 512], F32) for i in range(4)]
                    pst = [psum(p1, "pst%d" % i, [128, 512], F32) for i in range(2)]
                    apad = [sbuf(p1, "apad%d" % i, [128, 4400], BF16) for i in range(4)]
                    dgk = [sbuf(p1, "dgk%d" % i, [128, 31, 128], BF16) for i in range(2)]
                    pcv = [psum(p1, "pcv%d" % i, [128, 512], F32) for i in range(2)]
                    cvbs = [sbuf(p1, "cvb%d" % i, [128, TOK], BF16) for i in range(2)]
                    sqbs = [sbuf(p1, "sqb%d" % i, [128, TOK], BF16) for i in range(2)]
                    wA8 = sbuf(p1, "wA8", [128, 8, 16, 128], BF16)
                    sig = [sbuf(p1, "sig%d" % i, [128, 512], BF16) for i in range(2)]
                    for a_ in apad:
                        V(lambda e, a_=a_: e.memset(a_[:], 0.0), [], [a_])
                    for half in range(2):
                        cs = [4 * half + q for q in range(4)]
                        mts = []
                        for c in cs:
                            mts += [8 + c, c]
                        load_w(wA8, w_in0, mts)

                        def epi(i, ci, t0, n, ps):
                            sg = sig[ci % 2]
                            ap_ = apad[i // 2]
                            if i % 2 == 0:
                                act(sg[:, 0:n], ps[:, 0:n], AF.Sigmoid, [ps], [sg])
                            else:
                                off = 15 + t0 if t0 < L else 4126
                                tt(ap_[:, off:off + n], ps[:, 0:n], sg[:, 0:n], ALU.mult, [ps, sg], [ap_])

                        proj_fm(p1, w_in0, mts, CHUNKS, epi, "a1", ps_tiles, ut_tiles, wsb=wA8)
                        for q, c in enumerate(cs):
                            ap_ = apad[q]
                            dg_ = dgk[c % 2]
                            cvb, sqb = cvbs[c % 2], sqbs[c % 2]
                            tt(dg_[:], identb[:].unsqueeze(1).to_broadcast([128, 31, 128]),
                               cw[:, c, :].unsqueeze(2).to_broadcast([128, 31, 128]), ALU.mult, [identb, cw], [dg_])
                            for ci, (t0, n) in enumerate(CHUNKS):
                                i0 = t0 if t0 < L else 4111
                                pc_ = pcv[ci % 2]
                                for k in range(31):
                                    MM(pc_[:, 0:n], dg_[:, k, :], ap_[:, i0 + k:i0 + k + n], k == 0, k == 30,
                                       [dg_, ap_], [pc_])
                                act(cvb[:, t0:t0 + n], pc_[:, 0:n], AF.Identity, [pc_, cv], [cvb], bias=cv[:, 0, c:c + 1])
                                act(sqb[:, t0:t0 + n], pc_[:, 0:n], AF.Square, [pc_, cv], [sqb], bias=cv[:, 0, c:c + 1])
                            for ci, (t0, n) in enumerate(CHUNKS):
                                MM(pst[0][:, 0:n], onesb[:], cvb[:, t0:t0 + n], True, True, [onesb, cvb], [pst[0]])
                                MM(pst[1][:, 0:n], onesb[:], sqb[:, t0:t0 + n], True, True, [onesb, sqb], [pst[1]])
                                if c == 0:
                                    cp(s1[:, t0:t0 + n], pst[0][:, 0:n], [pst[0]], [s1])
                                    cp(s2[:, t0:t0 + n], pst[1][:, 0:n], [pst[1]], [s2])
                                else:
                                    tt(s1[:, t0:t0 + n], pst[0][:, 0:n], s1[:, t0:t0 + n], ALU.add, [pst[0], s1], [s1])
                                    tt(s2[:, t0:t0 + n], pst[1][:, 0:n], s2[:, t0:t0 + n], ALU.add, [pst[1], s2], [s2])
                            dma(convT_d[c * 128:(c + 1) * 128, :], cvb[:], [cvb], [R_conv], q="sync")
                    msq = sbuf(p1, "msq", [128, 512], F32)
                    for (t0, n) in CHUNKS:
                        ts(s1[:, t0:t0 + n], s1[:, t0:t0 + n], 1.0 / 1024, None, ALU.mult, None, [s1], [s1])
                        tt(msq[:, 0:n], s1[:, t0:t0 + n], s1[:, t0:t0 + n], ALU.mult, [s1], [msq])
                        stt(s2[:, t0:t0 + n], s2[:, t0:t0 + n], 1.0 / 1024, msq[:, 0:n], ALU.mult, ALU.subtract,
                            [s2, msq], [s2])
                    ts(s2[:], s2[:], EPS, None, ALU.add, None, [s2], [s2])
                    act(s2[:], s2[:], AF.Sqrt, [s2], [s2])
                    V(lambda e: e.reciprocal(out=s2[:], in_=s2[:]), [s2], [s2])
                    S.flush()
                with ExitStack() as p2:
                    ut_tiles = [sbuf(p2, "ut%d" % i, [128, 16, 512], BF16) for i in range(2)]
                    ps_tiles = [psum(p2, "ps%d" % i, [128, 512], F32) for i in range(4)]
                    cvl = [sbuf(p2, "cvl%d" % i, [128, 8, 512], BF16) for i in range(2)]
                    sgt = [sbuf(p2, "sgt%d" % i, [128, 512], F32) for i in range(2)]
                    t1 = [sbuf(p2, "t1%d" % i, [128, 512], F32) for i in range(2)]
                    mo = [sbuf(p2, "mo%d" % i, [128, 512], BF16) for i in range(2)]
                    wG8 = sbuf(p2, "wG8", [128, 8, 16, 128], BF16)
                    load_w(wG8, w_in0, [16 + c for c in range(8)])
                    kk2 = [0]

                    def epi(i, ci, t0, n, ps):
                        c = i
                        cvl_ = cvl[ci % 2]
                        if c == 0:
                            dma(cvl_[:, :, 0:n], convT_d.rearrange("(c p) t -> p c t", p=128)[:, :, t0:t0 + n],
                                [R_conv], [cvl_], q="sync")
                        k2 = kk2[0]
                        kk2[0] += 1
                        sg, t1_, mo_ = sgt[k2 % 2], t1[k2 % 2], mo[k2 % 2]
                        act(sg[:, 0:n], ps[:, 0:n], AF.Silu, [ps], [sg])
                        tt(t1_[:, 0:n], cvl_[:, c, 0:n], s1[:, t0:t0 + n], ALU.subtract, [cvl_, s1], [t1_])
                        tt(t1_[:, 0:n], t1_[:, 0:n], s2[:, t0:t0 + n], ALU.mult, [t1_, s2], [t1_])
                        act(t1_[:, 0:n], t1_[:, 0:n], AF.Silu, [t1_, cv], [t1_],
                            scale=cv[:, 1, c:c + 1], bias=cv[:, 2, c:c + 1])
                        tt(mo_[:, 0:n], t1_[:, 0:n], sg[:, 0:n], ALU.mult, [t1_, sg], [mo_])
                        dma(mixT_d[c * 128:(c + 1) * 128, t0:t0 + n], mo_[:, 0:n], [mo_], [R_mix], q="gpsimd")

                    proj_fm(p2, w_in0, [16 + c for c in range(8)], CHUNKS, epi, "a2", ps_tiles, ut_tiles, wsb=wG8)
                    S.flush()

        def phase_fourier():
            with ExitStack() as ph:
                fg = sbuf(ph, "fg", [128, 4, 2], F32)
                cs256 = sbuf(ph, "cs256", [128, 2, 512], BF16)
                cctx = sbuf(ph, "cctx", [128, 2, 512], BF16)
                bnT = sbuf(ph, "bnT", [128, 2, TOK], BF16)
                sgT = sbuf(ph, "sgT", [128, 2, TOK], BF16)
                dma(fg[:], four_g[:, :, :], [], [fg], q="sync")
                dma(cs256[:], cs256_d[:, :, :], [], [cs256], q="sync")
                dma(cctx[:], cctx_d[:, :, :], [], [cctx], q="sync")
                w1t = sbuf(ph, "w1t", [128, 3, 128], BF16)
                w3t = sbuf(ph, "w3t", [128, 2, 128], BF16)
                twd = sbuf(ph, "twd", [128, 2, L], BF16)
                dma(w1t[:], w1_d[:, :, :], [], [w1t], q="sync")
                dma(w3t[:], w3_d[:, :, :], [], [w3t], q="sync")
                dma(twd[:], twd_d[:, :, :], [], [twd], q="sync")
                wB = [sbuf(ph, "wB%d" % i, [128, 4, 16, 128], BF16) for i in range(2)]
                fmt = lambda g_: [24 + 2 * g_, 25 + 2 * g_, 32 + 2 * g_, 33 + 2 * g_]
                load_w(wB[0], w_in0, fmt(0))
                for g in range(4):
                    with ExitStack() as p1:
                        ut_tiles = [sbuf(p1, "ut%d" % i, [128, 16, 512], BF16) for i in range(2)]
                        ps_tiles = [psum(p1, "ps%d" % i, [128, 512], F32) for i in range(6)]
                        pss = [psum(p1, "pss%d" % i, [128, 512], F32) for i in range(2)]
                        sq = [sbuf(p1, "sq%d" % i, [128, 512], BF16) for i in range(4)]
                        rst = [sbuf(p1, "rst%d" % i, [128, 512], F32) for i in range(2)]
                        held = {}

                        def epi(i, ci, t0, n, ps, g=g):
                            if i >= 2:
                                act(sgT[:, i - 2, t0:t0 + n], ps[:, 0:n], AF.Silu, [ps], [sgT])
                                return
                            sq_ = sq[(ci % 2) * 2 + i]
                            act(sq_[:, 0:n], ps[:, 0:n], AF.Square, [ps], [sq_])
                            held[i] = (ps, sq_)
                            if i == 1:
                                pss_ = pss[ci % 2]
                                rst_ = rst[ci % 2]
                                for jj in range(2):
                                    MM(pss_[:, 0:n], onesb[:], held[jj][1][:, 0:n], jj == 0, jj == 1,
                                       [onesb, held[jj][1]], [pss_])
                                ts(rst_[:, 0:n], pss_[:, 0:n], 1.0 / 256, EPS, ALU.mult, ALU.add, [pss_], [rst_])
                                act(rst_[:, 0:n], rst_[:, 0:n], AF.Sqrt, [rst_], [rst_])
                                V(lambda e, a=rst_[:, 0:n]: e.reciprocal(out=a, in_=a), [rst_], [rst_])
                                for jj in range(2):
                                    if t0 < L:
                                        o_ = bnT[:, jj, 0:L].rearrange("p (b a) -> p a b", a=64)[:, 8 * ci:8 * ci + 8, :]
                                        stt(o_, held[jj][0][:, 0:512].rearrange("p (a b) -> p a b", b=64),
                                            fg[:, g, jj:jj + 1], rst_[:, 0:512].rearrange("p (a b) -> p a b", b=64),
                                            ALU.mult, ALU.mult, [held[jj][0], fg, rst_], [bnT])
                                    else:
                                        stt(bnT[:, jj, t0:t0 + n], held[jj][0][:, 0:n], fg[:, g, jj:jj + 1],
                                            rst_[:, 0:n], ALU.mult, ALU.mult, [held[jj][0], fg, rst_], [bnT])

                        proj_fm(p1, w_in0, fmt(g), CHUNKS, epi, "b1", ps_tiles, ut_tiles, wsb=wB[g % 2])
                        S.flush()
                    if g < 3:
                        load_w(wB[(g + 1) % 2], w_in0, fmt(g + 1))
                    with ExitStack() as p2:
                        YB = sbuf(p2, "YB", [128, 32, 512], BF16)
                        Yc_ = sbuf(p2, "Yc_", [128, 2, 512], BF16)
                        Bsb = sbuf(p2, "Bsb", [128, 2, 2, L], BF16)
                        fo = [sbuf(p2, "fo%d" % i, [128, L], BF16) for i in range(2)]
                        foc = sbuf(p2, "foc", [128, 256], BF16)
                        m1 = [sbuf(p2, "fm1%d" % i, [128, 512], F32) for i in range(2)]
                        m2 = [sbuf(p2, "fm2%d" % i, [128, 512], F32) for i in range(2)]
                        psy = [psum(p2, "psy%d" % i, [128, 512], F32) for i in range(2)]
                        pa = [psum(p2, "pa%d" % i, [128, 4, 128], F32) for i in range(4)]
                        ptr = [psum(p2, "ptr%d" % i, [128, 8, 128], BF16) for i in range(2)]
                        for beta in range(32):
                            p_ = psy[beta % 2]
                            for jj in range(2):
                                MM(p_[:], bnT[:, jj, beta * 128:(beta + 1) * 128], cs256[:, jj, :], jj == 0, jj == 1,
                                   [bnT, cs256], [p_])
                            if beta % 2 == 0:
                                cp(YB[:, beta, :], p_[:], [p_], [YB])
                            else:
                                act(YB[:, beta, :], p_[:], AF.Copy, [p_], [YB])
                        for t_ in range(2):
                            p_ = psy[t_ % 2]
                            for jj in range(2):
                                MM(p_[:], bnT[:, jj, L + t_ * 128:L + (t_ + 1) * 128], cs256[:, jj, :], jj == 0, jj == 1,
                                   [bnT, cs256], [p_])
                            cp(Yc_[:, t_, :], p_[:], [p_], [Yc_])
                        sc_x = 1.0 / math.sqrt(L * 256.0)
                        sc_c = 1.0 / math.sqrt(LC * 256.0)
                        kq = 0
                        for kk in range(2):
                            for bq in range(8):
                                par_, pai_ = pa[(kq % 2) * 2], pa[(kq % 2) * 2 + 1]
                                m1_, m2_ = m1[kq % 2], m2[kq % 2]
                                kq += 1
                                for q in range(4):
                                    beta = bq * 4 + q
                                    yc = YB[:, beta, kk * 128:(kk + 1) * 128]
                                    ys = YB[:, beta, 256 + kk * 128:256 + (kk + 1) * 128]
                                    MM(par_[:, q, :], yc, w1t[:, 0, :], True, False, [YB, w1t], [par_])
                                    MM(par_[:, q, :], ys, w1t[:, 1, :], False, True, [YB, w1t], [par_])
                                    MM(pai_[:, q, :], yc, w1t[:, 1, :], True, False, [YB, w1t], [pai_])
                                    MM(pai_[:, q, :], ys, w1t[:, 2, :], False, True, [YB, w1t], [pai_])
                                sl = slice(bq * 512, (bq + 1) * 512)
                                arv = par_[:].rearrange("p a b -> p (a b)")
                                aiv = pai_[:].rearrange("p a b -> p (a b)")
                                tt(m1_[:], arv, twd[:, 0, sl], ALU.mult, [par_, twd], [m1_])
                                tt(m2_[:], aiv, twd[:, 1, sl], ALU.mult, [pai_, twd], [m2_])
                                ob = lambda ri: Bsb[:, kk, ri, :].rearrange("p (m b) -> p b m", b=64)[:, 8 * bq:8 * bq + 8, :]
                                v3 = lambda t_: t_[:].rearrange("p (b m) -> p b m", m=64)
                                tt(ob(0), v3(m1_), v3(m2_), ALU.add, [m1_, m2_], [Bsb], eng="gpsimd")
                                tt(m1_[:], aiv, twd[:, 0, sl], ALU.mult, [pai_, twd], [m1_])
                                tt(m2_[:], arv, twd[:, 1, sl], ALU.mult, [par_, twd], [m2_])
                                tt(ob(1), v3(m1_), v3(m2_), ALU.subtract, [m1_, m2_], [Bsb], eng="gpsimd")
                        BT = YB
                        ke = 0
                        for mq in range(8):
                            for ri in range(2):
                                pt_ = ptr[ke % 2]
                                for q in range(4):
                                    mu = mq * 4 + q
                                    for kk in range(2):
                                        src = Bsb[:, kk, ri, mu * 128:(mu + 1) * 128]
                                        S.op("tensor", lambda e, o=pt_[:, q * 2 + kk, :], a=src:
                                             e.transpose(out=o, in_=a, identity=identb[:]), [Bsb, identb], [pt_],
                                             pe_chain=True)
                                dst = BT[:, mq * 4:(mq + 1) * 4, ri * 256:(ri + 1) * 256]
                                srcp = pt_[:].rearrange("p (q k) c -> p q (k c)", k=2)
                                if ke % 2 == 0:
                                    cp(dst, srcp, [pt_], [BT])
                                else:
                                    act(dst, srcp, AF.Copy, [pt_], [BT])
                                ke += 1
                        kq = 0
                        for kk in range(2):
                            fo_ = fo[kk]
                            fov = fo_[:].rearrange("p (mb ma) -> p ma mb", ma=64)
                            sgv = sgT[:, kk, 0:L].rearrange("p (mb ma) -> p ma mb", ma=64)
                            for mq in range(8):
                                pf_ = pa[kq % 4]
                                kq += 1
                                for q in range(4):
                                    mu = mq * 4 + q
                                    MM(pf_[:, q, :], BT[:, mu, kk * 128:(kk + 1) * 128], w3t[:, 0, :], True, False,
                                       [BT, w3t], [pf_])
                                    MM(pf_[:, q, :], BT[:, mu, 256 + kk * 128:256 + (kk + 1) * 128], w3t[:, 1, :],
                                       False, True, [BT, w3t], [pf_])
                                stt(fov[:, mq * 8:(mq + 1) * 8, :], pf_[:].rearrange("p q (l m) -> p (q l) m", l=2), sc_x,
                                    sgv[:, mq * 8:(mq + 1) * 8, :], ALU.mult, ALU.mult, [pf_, sgT], [fo_])
                            r0 = 1024 + g * 256 + kk * 128
                            dma(mixT_d[r0:r0 + 128, 0:L], fo_[:], [fo_], [R_mix], q="sync")
                        for kk in range(2):
                            pc = psy[kk]
                            for t_ in range(2):
                                MM(pc[:, 0:256], Yc_[:, t_, kk * 128:(kk + 1) * 128], cctx[:, t_, 0:256],
                                   t_ == 0, False, [Yc_, cctx], [pc])
                                MM(pc[:, 0:256], Yc_[:, t_, 256 + kk * 128:256 + (kk + 1) * 128], cctx[:, t_, 256:512],
                                   False, t_ == 1, [Yc_, cctx], [pc])
                            stt(foc[:], pc[:, 0:256], sc_c, sgT[:, kk, L:TOK], ALU.mult, ALU.mult, [pc, sgT], [foc])
                            r0 = 1024 + g * 256 + kk * 128
                            dma(mixT_d[r0:r0 + 128, L:TOK], foc[:], [foc], [R_mix], q="gpsimd")
                        S.flush()

        def phase_out(l, wout_d, hin_d, R_hin, ntiles, hout_d, R_hout, is_output):
            with ExitStack() as ph:
                wo = sbuf(ph, "wo", [128, 16, D], BF16)
                for k4 in range(4):
                    dma(wo[:, k4 * 4:(k4 + 1) * 4, :], wout_d[:, k4 * 4:(k4 + 1) * 4, :], [], [wo], q="gpsimd")
                gbc = [sbuf(ph, "gbc%d" % i, [128, D], F32) for i in range(2)]
                dg = sbuf(ph, "dg", [128, 128], F32)
                mx = [sbuf(ph, "mx%d" % i, [128, 16, 512], BF16) for i in range(2)]
                hr = [sbuf(ph, "hr%d" % i, [128, D], F32) for i in range(2)]
                tm = [sbuf(ph, "tm%d" % i, [128, D], F32) for i in range(2)]
                junk = sbuf(ph, "junk", [128, D], BF16)
                ss = sbuf(ph, "ss", [128, 34], F32)
                po = [psum(ph, "po%d" % i, [128, 4, 512], F32) for i in range(2)]
                nseq = 2 if ntiles > 32 else 1
                for i in range(nseq):
                    for c in range(16):
                        ts(dg[:], identf[:], gpT[:, l, i, c:c + 1], None, ALU.mult, None, [identf, gpT], [dg])
                        MM(po[0][:, c // 4, (c % 4) * 128:(c % 4 + 1) * 128], onesf[:], dg[:], True, True,
                           [onesf, dg], [po[0]])
                    for q in range(4):
                        cp(gbc[i][:, q * 512:(q + 1) * 512], po[0][:, q, :], [po[0]], [gbc[i]])
                for tix in range(ntiles):
                    i = 0 if tix < 32 else 1
                    mx_ = mx[(tix // 4) % 2]
                    if tix % 4 == 0:
                        n = min(512, ntiles * 128 - tix * 128)
                        t0 = tix * 128
                        dma(mx_[:, :, 0:n], mixT_d.rearrange("(c p) t -> p c t", p=128)[:, :, t0:t0 + n],
                            [R_mix], [mx_], q="sync")
                    hr_, tm_, po_ = hr[tix % 2], tm[tix % 2], po[tix % 2]
                    dma(hr_[:], hin_d[tix * 128:(tix + 1) * 128, :], [R_hin], [hr_], q="sync")
                    q4 = tix % 4
                    for nn in range(4):
                        for k in range(16):
                            MM(po_[:, nn, :], mx_[:, k, q4 * 128:(q4 + 1) * 128], wo[:, k, nn * 512:(nn + 1) * 512],
                               k == 0, k == 15, [mx_, wo], [po_])
                    act(junk[:], po_[:].rearrange("p a b -> p (a b)"), AF.Square, [po_], [junk, ss],
                        accum_out=ss[:, tix:tix + 1])
                    ts(ss[:, tix:tix + 1], ss[:, tix:tix + 1], 1.0 / D, EPS, ALU.mult, ALU.add, [ss], [ss])
                    act(ss[:, tix:tix + 1], ss[:, tix:tix + 1], AF.Sqrt, [ss], [ss])
                    V(lambda e, a=ss[:, tix:tix + 1]: e.reciprocal(out=a, in_=a), [ss], [ss])
                    tt(tm_[:], po_[:].rearrange("p a b -> p (a b)"), gbc[i][:], ALU.mult, [po_, gbc[i]], [tm_])
                    stt(tm_[:], tm_[:], ss[:, tix:tix + 1], hr_[:], ALU.mult, ALU.add, [tm_, ss, hr_], [tm_])
                    dma(hout_d[tix * 128:(tix + 1) * 128, :], tm_[:], [tm_], [R_hout], q="gpsimd", is_output=is_output)
                S.flush()

        def phase_na():
            with ExitStack() as ph:
                ur = sbuf(ph, "ur", [128, 16, 2560], BF16)
                uv = uT_d.rearrange("(c p) t -> p c t", p=128)
                for q in range(4):
                    dma(ur[:, q * 4:(q + 1) * 4, 0:NKV], uv[:, q * 4:(q + 1) * 4, 0:NKV], [R_uT], [ur])
                dma(ur[:, :, NKV:2560], uv[:, :, L:TOK], [R_uT], [ur], q="sync")
                wq = [sbuf(ph, "wq%d" % i, [128, 4, 16, 128], BF16) for i in range(2)]
                bt = [sbuf(ph, "bt%d" % i, [128, 3, 5, 128], BF16) for i in range(2)]
                qT = [sbuf(ph, "qT%d" % i, [128, OWN], BF16) for i in range(2)]
                kT = [sbuf(ph, "kT%d" % i, [128, 2560], BF16) for i in range(2)]
                sg = [sbuf(ph, "sg%d" % i, [128, OWN], BF16) for i in range(2)]
                Vh = [sbuf(ph, "Vh%d" % i, [128, 20, 128], BF16) for i in range(2)]
                PT = [sbuf(ph, "PT%d" % i, [128, 7, 128], BF16) for i in range(2)]
                vTf = sbuf(ph, "vTf", [128, 2560], F32)
                rd = [sbuf(ph, "rd%d" % i, [128, 128], F32) for i in range(2)]
                ot = [sbuf(ph, "ot%d" % i, [128, 128], F32) for i in range(2)]
                naT = [sbuf(ph, "naT%d" % i, [128, OWN], BF16) for i in range(2)]
                pp = [psum(ph, "pp%d" % i, [128, 512], F32) for i in range(2)]
                pS = [psum(ph, "pS%d" % i, [128, 8, 128], F32) for i in range(2)]
                pO = [psum(ph, "pO%d" % i, [128, 2, 128], F32) for i in range(2)]
                kp = 0
                isq = 1.0 / math.sqrt(128.0)
                for h in range(12):
                    w_ = wq[h % 2]
                    bt_ = bt[h % 2]
                    for i, mt in enumerate((h, 12 + h, 24 + h, 36 + h)):
                        dma(w_[:, i, :, :], w_in1[mt, :, :, :], [], [w_], q="gpsimd")
                    dma(bt_[:], na_bias[:, h, :, :, :].rearrange("c p k q -> p c k q"), [], [bt_], q="gpsimd")
                    qT_, kT_, sg_, Vh_, naT_ = qT[h % 2], kT[h % 2], sg[h % 2], Vh[h % 2], naT[h % 2]
                    for ci in range(5):
                        t0 = ci * 512
                        ps = pp[kp % 2]; kp += 1
                        for k in range(16):
                            MM(ps[:], w_[:, 1, k, :], ur[:, k, t0:t0 + 512], k == 0, k == 15, [w_, ur], [ps])
                        cp(kT_[:, t0:t0 + 512], ps[:], [ps], [kT_])
                        if ci < 4:
                            ps = pp[kp % 2]; kp += 1
                            for k in range(16):
                                MM(ps[:], w_[:, 0, k, :], ur[:, k, t0:t0 + 512], k == 0, k == 15, [w_, ur], [ps])
                            act(qT_[:, t0:t0 + 512], ps[:], AF.Copy, [ps], [qT_], scale=isq)
                            ps = pp[kp % 2]; kp += 1
                            for k in range(16):
                                MM(ps[:], w_[:, 3, k, :], ur[:, k, t0:t0 + 512], k == 0, k == 15, [w_, ur], [ps])
                            act(sg_[:, t0:t0 + 512], ps[:], AF.Silu, [ps], [sg_])
                    for ci in range(5):
                        t0 = ci * 512
                        ps = pp[kp % 2]; kp += 1
                        for k in range(16):
                            MM(ps[:], w_[:, 2, k, :], ur[:, k, t0:t0 + 512], k == 0, k == 15, [w_, ur], [ps])
                        act(vTf[:, t0:t0 + 512], ps[:], AF.Copy, [ps], [vTf])
                    for t4 in range(5):
                        ps = pp[kp % 2]; kp += 1
                        for q in range(4):
                            tix = t4 * 4 + q
                            S.op("tensor", lambda e, o=ps[:, q * 128:(q + 1) * 128], a=vTf[:, tix * 128:(tix + 1) * 128]:
                                 e.transpose(out=o, in_=a, identity=identf[:]), [vTf, identf], [ps], pe_chain=True)
                        cp(Vh_[:, t4 * 4:(t4 + 1) * 4, :], ps[:].rearrange("p (a b) -> p a b", b=128), [ps], [Vh_])
                    for j in range(16):
                        cls = min(j, 2)
                        ws = min(max(2 * j - 4, 0), 26)
                        pS_, pO_, PT_ = pS[j % 2], pO[j % 2], PT[j % 2]
                        rd_, ot_ = rd[j % 2], ot[j % 2]
                        for kt in range(7):
                            k0 = ws * 64 + kt * 128 if kt < 5 else NKV + (kt - 5) * 128
                            MM(pS_[:, kt, :], kT_[:, k0:k0 + 128], qT_[:, j * 128:(j + 1) * 128], True, kt >= 5,
                               [kT_, qT_], [pS_])
                            if kt < 5:
                                MM(pS_[:, kt, :], identb[:], bt_[:, cls, kt, :], False, True, [identb, bt_], [pS_])
                        act(PT_[:, 0:4, :], pS_[:, 0:4, :], AF.Exp, [pS_], [PT_])
                        act(PT_[:, 4:7, :], pS_[:, 4:7, :], AF.Exp, [pS_], [PT_])
                        for kt in range(7):
                            vt = ws // 2 + kt if kt < 5 else 18 + (kt - 5)
                            MM(pO_[:, 0, :], Vh_[:, vt, :], PT_[:, kt, :], kt == 0, kt == 6, [Vh_, PT_], [pO_])
                        for kt in range(7):
                            MM(pO_[:, 1, :], onesb[:], PT_[:, kt, :], kt == 0, kt == 6, [onesb, PT_], [pO_])
                        V(lambda e, o=rd_[:], a=pO_[:, 1, :]: e.reciprocal(out=o, in_=a), [pO_], [rd_])
                        tt(ot_[:], pO_[:, 0, :], rd_[:], ALU.mult, [pO_, rd_], [ot_])
                        tt(naT_[:, j * 128:(j + 1) * 128], ot_[:], sg_[:, j * 128:(j + 1) * 128], ALU.mult,
                           [ot_, sg_], [naT_])
                    dma(mixT_d[h * 128:(h + 1) * 128, 0:OWN], naT_[:], [naT_], [R_mix], q="sync")
                S.flush()

        def phase_s5():
            with ExitStack() as ph:
                dT = sbuf(ph, "dT", [128, 4, TOK], BF16)
                sdg = sbuf(ph, "sdg", [128, 4, OWN], BF16)
                ysb = sbuf(ph, "ysb", [128, 4, OWN], F32)
                with ExitStack() as p1:
                    ut_tiles = [sbuf(p1, "ut%d" % i, [128, 16, 512], BF16) for i in range(2)]
                    ps_tiles = [psum(p1, "ps%d" % i, [128, 512], F32) for i in range(4)]

                    def epi(i, ci, t0, n, ps):
                        if i < 4:
                            cp(dT[:, i, t0:t0 + n], ps[:, 0:n], [ps], [dT])
                        elif t0 < OWN:
                            act(sdg[:, i - 4, t0:t0 + n], ps[:, 0:n], AF.Silu, [ps], [sdg])

                    proj_fm(p1, w_in1, [48 + i for i in range(8)], CHUNKS, epi, "s5p", ps_tiles, ut_tiles)
                    S.flush()
                with ExitStack() as p2:
                    def small(name):
                        return sbuf(p2, name, [128, 2, 16], F32)
                    are, aim, ldt = small("are"), small("aim"), small("ldt")
                    dtt, rr, thp, cfr, cfi = small("dtt"), small("rr"), small("thp"), small("cfr"), small("cfi")
                    w1, w2, w3, w4 = small("w1"), small("w2"), small("w3"), small("w4")
                    wi = sbuf(p2, "wi", [128, 2, 16], I32)
                    dma(are[:], s5are[:, :, :], [], [are], q="sync")
                    dma(aim[:], s5aim[:, :, :], [], [aim], q="sync")
                    dma(ldt[:], s5ldt[:, :, :], [], [ldt], q="sync")
                    act(dtt[:], ldt[:], AF.Exp, [ldt], [dtt])
                    tt(w1[:], are[:], dtt[:], ALU.mult, [are, dtt], [w1])
                    act(rr[:], w1[:], AF.Exp, [w1], [rr])
                    tt(w1[:], aim[:], dtt[:], ALU.mult, [aim, dtt], [w1])
                    ts(thp[:], w1[:], 1.0 / (2.0 * math.pi), None, ALU.mult, None, [w1], [thp])

                    def sincos(src, sin_out, cos_out):
                        cp(wi[:], src[:], [src], [wi])
                        cp(w2[:], wi[:], [wi], [w2])
                        tt(w2[:], src[:], w2[:], ALU.subtract, [src, w2], [w2])
                        act(sin_out[:], w2[:], AF.Sin, [w2], [sin_out], scale=6.28318)
                        ts(w3[:], src[:], 0.25, None, ALU.add, None, [src], [w3])
                        cp(wi[:], w3[:], [w3], [wi])
                        cp(w2[:], wi[:], [wi], [w2])
                        tt(w2[:], w3[:], w2[:], ALU.subtract, [w3, w2], [w2])
                        act(cos_out[:], w2[:], AF.Sin, [w2], [cos_out], scale=6.28318)

                    sn, cs_ = small("sn"), small("cs_")
                    sincos(thp, sn, cs_)
                    tt(w1[:], rr[:], cs_[:], ALU.mult, [rr, cs_], [w1])
                    ts(w1[:], w1[:], -1.0, None, ALU.add, None, [w1], [w1])
                    tt(w4[:], rr[:], sn[:], ALU.mult, [rr, sn], [w4])
                    tt(w2[:], are[:], are[:], ALU.mult, [are], [w2])
                    tt(w3[:], aim[:], aim[:], ALU.mult, [aim], [w3])
                    tt(w2[:], w2[:], w3[:], ALU.add, [w2, w3], [w2])
                    V(lambda e: e.reciprocal(out=w2[:], in_=w2[:]), [w2], [w2])
                    tt(cfr[:], w1[:], are[:], ALU.mult, [w1, are], [cfr])
                    tt(w3[:], w4[:], aim[:], ALU.mult, [w4, aim], [w3])
                    tt(cfr[:], cfr[:], w3[:], ALU.add, [cfr, w3], [cfr])
                    tt(cfr[:], cfr[:], w2[:], ALU.mult, [cfr, w2], [cfr])
                    tt(cfi[:], w4[:], are[:], ALU.mult, [w4, are], [cfi])
                    tt(w3[:], w1[:], aim[:], ALU.mult, [w1, aim], [w3])
                    tt(cfi[:], cfi[:], w3[:], ALU.subtract, [cfi, w3], [cfi])
                    tt(cfi[:], cfi[:], w2[:], ALU.mult, [cfi, w2], [cfi])

                    braw = sbuf(p2, "braw", [128, 2, 2, 16, 128], BF16)
                    ctw = sbuf(p2, "ctw", [128, 2, 2, 16, 128], BF16)
                    dma(braw[:], s5braw[:, :, :, :, :], [], [braw], q="gpsimd")
                    dma(ctw[:], s5ct[:, :, :, :, :], [], [ctw], q="gpsimd")
                    ts(ctw[:, 1], ctw[:, 1], -1.0, None, ALU.mult, None, [ctw], [ctw])
                    ctw2 = ctw
                    ctmp = sbuf(p2, "ctmp", [128, 128], F32)
                    ctmpb = sbuf(p2, "ctmpb", [128, 128], F32)
                    for d_ in range(2):
                        for j in range(16):
                            c0, c1 = ctw[:, 0, d_, j, :], ctw[:, 1, d_, j, :]
                            ts(ctmp[:], c1, cfi[:, d_, j:j + 1], None, ALU.mult, None, [ctw, cfi], [ctmp])
                            stt(ctmpb[:], c0, cfr[:, d_, j:j + 1], ctmp[:], ALU.mult, ALU.add, [ctw, cfr, ctmp], [ctmpb])
                            ts(ctmp[:], c0, cfi[:, d_, j:j + 1], None, ALU.mult, None, [ctw, cfi], [ctmp])
                            stt(c1, c1, cfr[:, d_, j:j + 1], ctmp[:], ALU.mult, ALU.subtract, [ctw, cfr, ctmp], [ctw])
                            cp(c0, ctmpb[:], [ctmpb], [ctw])
                    sv = sbuf(p2, "sv", [128, 513], F32)
                    dma(sv[:], svals_d[:, :], [], [sv], q="sync")
                    ones5 = sbuf(p2, "ones5", [128, 512], F32)
                    V(lambda e: e.memset(ones5[:], 1.0), [], [ones5])
                    a1s = [sbuf(p2, "a1%d" % i, [128, 513], F32) for i in range(1)] * 2
                    a2s = [sbuf(p2, "a2%d" % i, [128, 513], F32) for i in range(1)] * 2
                    ais = [sbuf(p2, "ai%d" % i, [128, 513], I32) for i in range(1)] * 2
                    sinTs = [sbuf(p2, "sinT%d" % i, [128, 513], F32) for i in range(2)]
                    cosTs = [sbuf(p2, "cosT%d" % i, [128, 513], F32) for i in range(2)]
                    rfill = sbuf(p2, "rfill", [128, 512], F32)
                    mts4 = [[sbuf(p2, "mt%d%d" % (i, q), [128, 512], F32) for q in range(4)] for i in range(2)]
                    ris = [sbuf(p2, "ris%d" % i, [128, 2, 2], F32) for i in range(2)]
                    g1_ = sbuf(p2, "g1_", [128, 512], F32)
                    g2_ = sbuf(p2, "g2_", [128, 512], F32)
                    bps = [[sbuf(p2, "bp%d%d" % (i, q), [128, 512], F32) for q in range(2)] for i in range(2)]
                    kks = [sbuf(p2, "kk%d" % i, [128, 2, 512], F32) for i in range(2)]
                    ini = sbuf(p2, "ini", [128, 4], F32)
                    hh = [sbuf(p2, "hh%d" % d_, [128, 2, OWN], BF16) for d_ in range(2)]
                    pbu = [psum(p2, "pbu%d" % i, [128, 512], F32) for i in range(4)]
                    py = [psum(p2, "py%d" % i, [128, 512], F32) for i in range(4)]
                    seqs = [
                        [(L, 256, False, None)] + [(i * 512, 512, False, i * 512) for i in range(4)],
                        [(L, 256, True, None)] + [(i * 512, 512, True, (i * 512 if i < 4 else None))
                                                  for i in range(7, -1, -1)],
                    ]
                    kb = 0
                    for j in range(16):
                        kc = j // 4
                        for d_ in range(2):
                            sinT, cosT = sinTs[d_], cosTs[d_]
                            a1, a2, ai = a1s[d_], a2s[d_], ais[d_]
                            ts(a1[:], sv[:], thp[:, d_, j:j + 1], None, ALU.mult, None, [sv, thp], [a1])
                            cp(ai[:], a1[:], [a1], [ai])
                            cp(a2[:], ai[:], [ai], [a2])
                            tt(a2[:], a1[:], a2[:], ALU.subtract, [a1, a2], [a2])
                            act(sinT[:], a2[:], AF.Sin, [a2], [sinT], scale=6.28318)
                            ts(a1[:], a1[:], 0.25, None, ALU.add, None, [a1], [a1])
                            cp(ai[:], a1[:], [a1], [ai])
                            cp(a2[:], ai[:], [ai], [a2])
                            tt(a2[:], a1[:], a2[:], ALU.subtract, [a1, a2], [a2])
                            act(cosT[:], a2[:], AF.Sin, [a2], [cosT], scale=6.28318)
                            ts(rfill[:], ones5[:], rr[:, d_, j:j + 1], None, ALU.mult, None, [ones5, rr], [rfill])
                            for ni, nn_ in enumerate((256, 512)):
                                ts(ris[d_][:, ni, 0:1], sinT[:, nn_:nn_ + 1], -1.0, None, ALU.mult, None, [sinT], [ris[d_]])
                                cp(ris[d_][:, ni, 1:2], sinT[:, nn_:nn_ + 1], [sinT], [ris[d_]])
                            first = True
                            for (m0, n, rev, own) in seqs[d_]:
                                pr, pi = pbu[kb % 4], pbu[(kb + 1) % 4]
                                sl_ = (kb // 2) % 2
                                kk_ = kks[sl_]
                                kr, ki = kk_[:, 0, :], kk_[:, 1, :]
                                ma, mb, mc, md = mts4[sl_]
                                bpr, bpi = bps[sl_]
                                kb += 2
                                MM(pr[:, 0:n], braw[:, 0, d_, j, :], dT[:, kc, m0:m0 + n], True, True, [braw, dT], [pr])
                                MM(pi[:, 0:n], braw[:, 1, d_, j, :], dT[:, kc, m0:m0 + n], True, True, [braw, dT], [pi])
                                ur_ = pr[:, 0:n][:, ::-1] if rev else pr[:, 0:n]
                                ui_ = pi[:, 0:n][:, ::-1] if rev else pi[:, 0:n]
                                tt(ma[:, 0:n], ur_, cosT[:, 0:n], ALU.mult, [pr, cosT], [ma])
                                tt(mb[:, 0:n], ui_, sinT[:, 0:n], ALU.mult, [pi, sinT], [mb])
                                tt(mc[:, 0:n], ui_, cosT[:, 0:n], ALU.mult, [pi, cosT], [mc])
                                tt(md[:, 0:n], ur_, sinT[:, 0:n], ALU.mult, [pr, sinT], [md])
                                tt(bpr[:, 0:n], ma[:, 0:n], mb[:, 0:n], ALU.add, [ma, mb], [bpr], eng="gpsimd")
                                tt(bpi[:, 0:n], mc[:, 0:n], md[:, 0:n], ALU.subtract, [mc, md], [bpi], eng="gpsimd")
                                i_r = 0.0 if first else ini[:, 0:1]
                                i_i = 0.0 if first else ini[:, 1:2]
                                V(lambda e, o=kr[:, 0:n], a=rfill[:, 0:n], b=bpr[:, 0:n], iv=i_r:
                                  e.tensor_tensor_scan(out=o, data0=a, data1=b, initial=iv, op0=ALU.mult, op1=ALU.add),
                                  [rfill, bpr, ini], [kk_])
                                V(lambda e, o=ki[:, 0:n], a=rfill[:, 0:n], b=bpi[:, 0:n], iv=i_i:
                                  e.tensor_tensor_scan(out=o, data0=a, data1=b, initial=iv, op0=ALU.mult, op1=ALU.add),
                                  [rfill, bpi, ini], [kk_])
                                first = False
                                ni = 0 if n == 256 else 1
                                tt(ini[:, 2:4], kk_[:, ::-1, n - 1], ris[d_][:, ni, :], ALU.mult, [kk_, ris[d_]], [ini])
                                stt(ini[:, 0:2], kk_[:, :, n - 1], cosT[:, n:n + 1], ini[:, 2:4], ALU.mult, ALU.add,
                                    [kk_, cosT, ini], [ini])
                                if own is not None:
                                    h_ = hh[d_]
                                    o_r = h_[:, 0, own:own + n]
                                    o_i = h_[:, 1, own:own + n]
                                    if rev:
                                        o_r = o_r[:, ::-1]
                                        o_i = o_i[:, ::-1]
                                    tt(g1_[:, 0:n], cosT[:, 0:n], kr[:, 0:n], ALU.mult, [cosT, kk_], [g1_], eng="gpsimd")
                                    tt(g2_[:, 0:n], sinT[:, 0:n], ki[:, 0:n], ALU.mult, [sinT, kk_], [g2_], eng="gpsimd")
                                    tt(o_r, g1_[:, 0:n], g2_[:, 0:n], ALU.subtract, [g1_, g2_], [h_], eng="gpsimd")
                                    tt(g1_[:, 0:n], cosT[:, 0:n], ki[:, 0:n], ALU.mult, [cosT, kk_], [g1_], eng="gpsimd")
                                    tt(g2_[:, 0:n], sinT[:, 0:n], kr[:, 0:n], ALU.mult, [sinT, kk_], [g2_], eng="gpsimd")
                                    tt(o_i, g1_[:, 0:n], g2_[:, 0:n], ALU.add, [g1_, g2_], [h_], eng="gpsimd")
                        for tc in range(4):
                            for d_ in range(2):
                                for ri in range(2):
                                    MM(py[tc][:], ctw2[:, ri, d_, j, :], hh[d_][:, ri, tc * 512:(tc + 1) * 512],
                                       (j % 4 == 0 and d_ == 0 and ri == 0), (j % 4 == 3 and d_ == 1 and ri == 1),
                                       [ctw2, hh[d_]], [py[tc]])
                        if j % 4 == 3:
                            for tc in range(4):
                                cp(ysb[:, kc, tc * 512:(tc + 1) * 512], py[tc][:], [py[tc]], [ysb])
                    S.flush()
                with ExitStack() as p2:
                    pbu = [psum(p2, "pbu%d" % i, [128, 512], F32) for i in range(4)]
                    dcol = sbuf(p2, "dcol", [128, 4], F32)
                    dma(dcol[:], s5d[:, :], [], [dcol], q="sync")
                    wg = sbuf(p2, "wg", [128, 4, 4, 128], BF16)
                    for mt in range(4):
                        dma(wg[:, mt, :, :], w_glu[mt, :, :, :], [], [wg], q="gpsimd")
                    yb = sbuf(p2, "yb", [128, 4, OWN], BF16)
                    g1 = sbuf(p2, "g1", [128, OWN], F32)
                    g2 = sbuf(p2, "g2", [128, OWN], F32)
                    for kc in range(4):
                        stt(ysb[:, kc, :], dT[:, kc, 0:OWN], dcol[:, kc:kc + 1], ysb[:, kc, :], ALU.mult, ALU.add,
                            [dT, dcol, ysb], [ysb])
                        tt(g1[:], ysb[:, kc, :], ysb[:, kc, :], ALU.mult, [ysb], [g1])
                        ts(g1[:], g1[:], 0.044715, 1.0, ALU.mult, ALU.add, [g1], [g1])
                        tt(g1[:], g1[:], ysb[:, kc, :], ALU.mult, [g1, ysb], [g1])
                        act(g2[:], g1[:], AF.Sigmoid, [g1], [g2], scale=1.5957691216057308)
                        tt(yb[:, kc, :], ysb[:, kc, :], g2[:], ALU.mult, [ysb, g2], [yb])
                    mo = [sbuf(p2, "mo%d" % i, [128, 512], BF16) for i in range(2)]
                    km = 0
                    for mt in range(4):
                        for tc in range(4):
                            ps = pbu[km % 4]
                            mo_ = mo[km % 2]
                            km += 1
                            for k in range(4):
                                MM(ps[:], wg[:, mt, k, :], yb[:, k, tc * 512:(tc + 1) * 512], k == 0, k == 3,
                                   [wg, yb], [ps])
                            act(g1[:, 0:512], ps[:], AF.Sigmoid, [ps], [g1])
                            tt(g1[:, 0:512], g1[:, 0:512], yb[:, mt, tc * 512:(tc + 1) * 512], ALU.mult, [g1, yb], [g1])
                            tt(mo_[:], g1[:, 0:512], sdg[:, mt, tc * 512:(tc + 1) * 512], ALU.mult, [g1, sdg], [mo_])
                            r0 = 1536 + mt * 128
                            dma(mixT_d[r0:r0 + 128, tc * 512:(tc + 1) * 512], mo_[:], [mo_], [R_mix], q="gpsimd")
                    S.flush()

        phase_p1(0, xin, T(None))
        phase_conv()
        phase_fourier()
        if debug == "l0":
            phase_out(0, w_out0, xin, T(None), 34, h1_d, R_h1, True)
            V(lambda e: e.memset(onesf[:], 1.0), [], [onesf])
            S.flush(final=True)
            return nc
        phase_out(0, w_out0, xin, T(None), 34, h1_d, R_h1, False)
        phase_p1(1, h1_d, R_h1)
        phase_na()
        phase_s5()
        phase_out(1, w_out1, h1_d, R_h1, 16, out_d, T(None), True)
        V(lambda e: e.memset(onesf[:], 1.0), [], [onesf])
        S.flush(final=True)
    return nc


_CONST_CACHE = {}


def _consts(inp):
    if "dft" not in _CONST_CACHE:
        _CONST_CACHE["dft"] = [_dft_consts(0), _dft_consts(1)]
    rpb = inp["na_rpb"][0]
    return {"dft": _CONST_CACHE["dft"], "na_bias": [_na_tables(rpb, 0), _na_tables(rpb, 1)]}


def kernel(**inputs):
    inp = {k: np.asarray(v) for k, v in inputs.items()}
    consts = _consts(inp)
    nc = build()
    in_maps = [prep_core(inp, core, consts) for core in range(8)]
    res = run_bass_kernel_spmd(nc, in_maps, core_ids=list(range(8)))
    out = np.empty((4, L, D), np.float32)
    for core in range(8):
        b, par = core // 2, core % 2
        o = res.results[core]["out"]
        if par:
            out[b, L - OWN:] = o[::-1]
        else:
            out[b, :OWN] = o
    return out
```

```python
import math
from contextlib import ExitStack
import numpy as np
import ml_dtypes
import concourse.bass as bass
import concourse.mybir as mybir
from concourse.bass_utils import run_bass_kernel_spmd

F32 = mybir.dt.float32
BF16 = mybir.dt.bfloat16
I32 = mybir.dt.int32
AF = mybir.ActivationFunctionType
ALU = mybir.AluOpType

ENGS = ("tensor", "vector", "scalar", "gpsimd", "sync")
NDMA = 24
D = 2048
L = 4096
LC = 256
TOK = L + LC
OWN = 2048
NKV = 2304
EPS = 1e-6
CHUNKS = [(i * 512, 512) for i in range(8)] + [(4096, 256)]
BF = ml_dtypes.bfloat16


class Res:
    __slots__ = ("w", "r")

    def __init__(self):
        self.w = None
        self.r = {}


class T:
    def __init__(self, t):
        self.t = t
        self.res = Res()

    def __getitem__(self, idx):
        return self.t[idx]


class Sched:
    def __init__(self, nc, sems):
        self.nc = nc
        self.sems = sems
        self.q = {e: [] for e in ENGS}
        self.cnt = {e: 0 for e in ENGS}
        self.seen = {e: {} for e in ENGS}
        self.dma_rr = 0
        self.dma_cnt = [0] * NDMA
        self.out_toks = []
        self.dq = 0
        self.barrier = {}

    def _deps(self, eng, reads, writes, pe_chain):
        need = {}

        def add(tok):
            if tok is None:
                return
            s, v = tok
            if pe_chain and s == "tensor" and eng == "tensor":
                return
            if need.get(s, 0) < v:
                need[s] = v

        for r in reads:
            add(r.res.w)
        for w in writes:
            add(w.res.w)
            for s, v in w.res.r.items():
                add((s, v))
        for s, v in self.barrier.items():
            if need.get(s, 0) < v:
                need[s] = v
        waits = []
        for s, v in need.items():
            if self.seen[eng].get(s, 0) < v:
                waits.append((s, v))
                self.seen[eng][s] = v
        return waits

    def _commit(self, tok, reads, writes):
        s, v = tok
        for r in reads:
            if r.res.r.get(s, 0) < v:
                r.res.r[s] = v
        for w in writes:
            w.res.w = tok
            w.res.r = {}

    def op(self, eng, emit, reads=(), writes=(), pe_chain=False):
        waits = self._deps(eng, reads, writes, pe_chain)
        self.cnt[eng] += 1
        tok = (eng, self.cnt[eng])
        self.q[eng].append((waits, emit, (eng, 1)))
        self._commit(tok, reads, writes)
        return tok

    def dma(self, emit, reads=(), writes=(), q=None, is_output=False):
        if q is None:
            q = ("sync", "gpsimd")[self.dq % 2]
            self.dq += 1
        slot = self.dma_rr % NDMA
        self.dma_rr += 1
        semkey = ("dma", slot)
        waits = self._deps(q, reads, writes, False)
        prev = self.dma_cnt[slot]
        if prev and self.seen[q].get(semkey, 0) < prev:
            waits.append((semkey, prev))
            self.seen[q][semkey] = prev
        self.dma_cnt[slot] = prev + 16
        tok = (semkey, prev + 16)
        self.q[q].append((waits, emit, (semkey, 16)))
        self._commit(tok, reads, writes)
        if is_output:
            self.out_toks.append(tok)
        return tok

    def flush(self, final=False):
        if final:
            fin = {}
            for s, v in self.out_toks:
                fin[s] = max(fin.get(s, 0), v)
            for e in ENGS:
                if self.cnt[e]:
                    fin[e] = max(fin.get(e, 0), self.cnt[e])
            for s in range(NDMA):
                if self.dma_cnt[s]:
                    fin[("dma", s)] = self.dma_cnt[s]
            self.q["sync"].append((list(fin.items()), None, None))
        qs = self.q
        sems = self.sems

        def run(name):
            def body(e):
                for waits, emit, inc in qs[name]:
                    for s, v in waits:
                        e.wait_ge(sems[s], v)
                    if emit is not None:
                        emit(e).then_inc(sems[inc[0]], inc[1])
            return body

        with self.nc.Block() as block:
            block.tensor(run("tensor"))
            block.vector(run("vector"))
            block.scalar(run("scalar"))
            block.gpsimd(run("gpsimd"))
            block.sync(run("sync"))
        self.q = {e: [] for e in ENGS}
        self.barrier = {e: self.cnt[e] for e in ENGS if self.cnt[e]}
        for s_ in range(NDMA):
            if self.dma_cnt[s_]:
                self.barrier[("dma", s_)] = self.dma_cnt[s_]


def _cols(v, n):
    return np.ascontiguousarray(v.reshape(n, 128).T)


def _mt(W):
    K, F = W.shape
    return np.ascontiguousarray(W.reshape(K // 128, 128, F // 128, 128).transpose(2, 1, 0, 3))


def _kt(W):
    K, F = W.shape
    return np.ascontiguousarray(W.reshape(K // 128, 128, F).transpose(1, 0, 2))


def _na_tables(rpb, par):
    out = np.full((3, 12, 640, 128), -30000.0, np.float32)
    for cls, j in enumerate((0, 1, 2)):
        ws = min(max(2 * j - 4, 0), 26)
        qi = np.arange(128)
        qr_o = 2 * j + qi // 64
        qc_o = qi % 64
        ki = np.arange(640)
        kr_o = ws + ki // 64
        kc_o = ki % 64
        if par:
            qr, qc, kr, kc = 63 - qr_o, 63 - qc_o, 63 - kr_o, 63 - kc_o
        else:
            qr, qc, kr, kc = qr_o, qc_o, kr_o, kc_o
        rs = np.clip(qr - 4, 0, 56)
        cs = np.clip(qc - 8, 0, 48)
        ok = ((kr[:, None] >= rs[None]) & (kr[:, None] < rs[None] + 8) &
              (kc[:, None] >= cs[None]) & (kc[:, None] < cs[None] + 16))
        dr = np.clip(kr[:, None] - qr[None] + 7, 0, 14)
        dc = np.clip(kc[:, None] - qc[None] + 15, 0, 30)
        g = rpb[:, dr, dc]
        out[cls] = np.where(ok[None], g, np.float32(-30000.0))
    return np.ascontiguousarray(out.reshape(3, 12, 5, 128, 128).transpose(0, 1, 3, 2, 4))


def _dft_consts(par):
    n = np.arange(256)
    a = 2.0 * np.pi * ((n[:, None] * n[None, :]) % 256) / 256.0
    c256, s256 = np.cos(a), np.sin(a)
    cs256 = np.concatenate([c256, s256], axis=1)
    cs256 = cs256.reshape(2, 128, 512).transpose(1, 0, 2)
    if par:
        cc, sc = c256[::-1, ::-1], s256[::-1, ::-1]
    else:
        cc, sc = c256, s256
    cctx = np.concatenate([cc, -sc], axis=1).reshape(2, 128, 512).transpose(1, 0, 2)
    a = np.arange(64)
    if par:
        e1 = (a[:, None] * (a[None, :] + 1)) % 64
        e3 = ((a[:, None] + 1) * a[None, :]) % 64
        e2 = ((a[:, None] + 1) * (a[None, :] + 1)) % 4096
    else:
        e1 = (a[:, None] * a[None, :]) % 64
        e3 = e1
        e2 = (a[:, None] * a[None, :]) % 4096

    def bd(m):
        z = np.zeros((128, 128))
        z[:64, :64] = m
        z[64:, 64:] = m
        return z

    th1 = 2.0 * np.pi * e1 / 64.0
    th3 = 2.0 * np.pi * e3 / 64.0
    th2 = 2.0 * np.pi * e2 / 4096.0
    w1 = np.stack([bd(np.cos(th1)), bd(-np.sin(th1)), bd(-np.cos(th1))], axis=1)
    w3 = np.stack([bd(np.cos(th3)), bd(np.sin(th3))], axis=1)
    tw = np.stack([np.cos(th2).reshape(-1), np.sin(th2).reshape(-1)], axis=0)
    tw = np.broadcast_to(tw[None], (128, 2, 4096))
    return (np.ascontiguousarray(cs256).astype(BF), np.ascontiguousarray(cctx).astype(BF),
            np.ascontiguousarray(w1).astype(BF), np.ascontiguousarray(w3).astype(BF),
            np.ascontiguousarray(tw).astype(BF))


def _s5_layout(inp, par):
    dirs = (1, 0) if par else (0, 1)
    G, P, H = 32, 64, 16
    out = {}
    a_re = inp["s5_a_re"][0][list(dirs)]
    a_im = inp["s5_a_im"][0][list(dirs)]
    ldt = inp["s5_log_dt"][0][list(dirs)]

    def st(v):
        return np.ascontiguousarray(v.reshape(2, 16, 2, 64).transpose(2, 3, 0, 1).reshape(128, 2, 16))

    out["s5are"] = st(a_re)
    out["s5aim"] = st(a_im)
    out["s5ldt"] = st(np.broadcast_to(ldt[:, :, None], (2, G, P)))
    b_re = inp["s5_b_re"][0][list(dirs)]
    b_im = inp["s5_b_im"][0][list(dirs)]
    c_re = inp["s5_c_re"][0][list(dirs)]
    c_im = inp["s5_c_im"][0][list(dirs)]
    braw = np.zeros((2, 2, 16, 128, 128), np.float32)
    ct = np.zeros((2, 2, 16, 128, 128), np.float32)
    for g in range(G):
        j = g // 2
        r0 = (g % 8) * 16
        s0 = (g % 2) * 64
        for ri, (bb, cc) in enumerate(((b_re, c_re), (b_im, c_im))):
            braw[ri, :, j, r0:r0 + 16, s0:s0 + 64] = bb[:, g].transpose(0, 2, 1)
            ct[ri, :, j, s0:s0 + 64, r0:r0 + 16] = cc[:, g].transpose(0, 2, 1)
    out["s5braw"] = np.ascontiguousarray(braw.transpose(3, 0, 1, 2, 4))
    out["s5ct"] = np.ascontiguousarray(ct.transpose(3, 0, 1, 2, 4))
    return out


def prep_core(inp, core, consts):
    b, par = core // 2, core % 2
    x = inp["x"][b]
    ctx = inp["ctx"][b]
    if par:
        x = x[::-1]
        ctx = ctx[::-1]
    m = {}
    m["xin"] = np.ascontiguousarray(np.concatenate([x, ctx], axis=0))
    m["cvec"] = np.ascontiguousarray(np.stack([_cols(inp["c"][b], 16), _cols(inp["c_ctx"], 16)], axis=2))
    m["ada_w"] = np.ascontiguousarray(inp["ada_w"].reshape(2, 16, 128, 6144).transpose(0, 2, 1, 3))
    m["ada_b"] = np.stack([_cols(inp["ada_b"][l], 48) for l in range(2)], axis=1)
    m["pre_g"] = np.stack([_cols(inp["pre_g"][l], 16) for l in range(2)], axis=1)
    m["post_g"] = np.stack([_cols(inp["post_g"][l], 16) for l in range(2)], axis=1)
    m["w_in0"] = _mt(inp["ab_w_in"][0])
    m["w_out0"] = _kt(inp["ab_w_out"][0])
    cw = inp["conv_w"][0]
    if par:
        cw = cw[::-1]
    m["conv_w"] = np.ascontiguousarray(cw.T.reshape(8, 128, 31).transpose(1, 0, 2))
    m["conv_v"] = np.ascontiguousarray(np.stack([_cols(inp["conv_b"][0], 8), _cols(inp["conv_ln_g"][0], 8),
                                                 _cols(inp["conv_ln_b"][0], 8)], axis=1))
    fg = inp["fourier_g"][0]
    m["four_g"] = np.ascontiguousarray(fg.reshape(4, 2, 128).transpose(2, 0, 1))
    m["w_in1"] = _mt(inp["cd_w_in"][0])
    m["w_out1"] = _kt(inp["cd_w_out"][0])
    m["na_bias"] = consts["na_bias"][par]
    m["s5d"] = _cols(inp["s5_d"][0], 4)
    m["w_glu"] = _mt(inp["s5_w_glu"][0])
    m.update(_s5_layout(inp, par))
    cs256, cctx, fw1, fw3, ftw = consts["dft"][par]
    m["cs256"], m["cctx"], m["fw1"], m["fw3"], m["ftw"] = cs256, cctx, fw1, fw3, ftw
    m["ident"] = np.eye(128, dtype=np.float32)
    m["svals"] = np.ascontiguousarray(np.broadcast_to(np.arange(513, dtype=np.float32)[None], (128, 513)))
    return m


def build(debug=None):
    nc = bass.Bass("TRN2", target_bir_lowering=False)

    def din(name, shape, dtype=F32):
        return nc.dram_tensor(name, list(shape), dtype, kind="ExternalInput").ap()

    def dscr(name, shape, dtype):
        return nc.dram_tensor(name, list(shape), dtype, kind="Internal").ap()

    xin = din("xin", [TOK, D])
    cvec = din("cvec", [128, 16, 2])
    ada_w = din("ada_w", [2, 128, 16, 6144])
    ada_b = din("ada_b", [128, 2, 48])
    pre_g = din("pre_g", [128, 2, 16])
    post_g = din("post_g", [128, 2, 16])
    w_in0 = din("w_in0", [40, 128, 16, 128])
    w_out0 = din("w_out0", [128, 16, 2048])
    conv_w = din("conv_w", [128, 8, 31])
    conv_v = din("conv_v", [128, 3, 8])
    four_g = din("four_g", [128, 4, 2])
    w_in1 = din("w_in1", [56, 128, 16, 128])
    w_out1 = din("w_out1", [128, 16, 2048])
    na_bias = din("na_bias", [3, 12, 128, 5, 128])
    s5d = din("s5d", [128, 4])
    w_glu = din("w_glu", [4, 128, 4, 128])
    s5are = din("s5are", [128, 2, 16])
    s5aim = din("s5aim", [128, 2, 16])
    s5ldt = din("s5ldt", [128, 2, 16])
    s5braw = din("s5braw", [128, 2, 2, 16, 128])
    s5ct = din("s5ct", [128, 2, 2, 16, 128])
    cs256_d = din("cs256", [128, 2, 512], BF16)
    cctx_d = din("cctx", [128, 2, 512], BF16)
    w1_d = din("fw1", [128, 3, 128], BF16)
    w3_d = din("fw3", [128, 2, 128], BF16)
    twd_d = din("ftw", [128, 2, L], BF16)
    ident_d = din("ident", [128, 128])
    svals_d = din("svals", [128, 513])
    out_d = nc.dram_tensor("out", [OWN, D], F32, kind="ExternalOutput").ap()
    h1_kind = "ExternalOutput" if debug == "l0" else "Internal"
    h1_d = nc.dram_tensor("h1", [TOK, D], F32, kind=h1_kind).ap()
    uT_d = dscr("uT", [D, 4608], BF16)
    mixT_d = dscr("mixT", [D, TOK], BF16)
    convT_d = dscr("convT", [1024, TOK], BF16)

    with ExitStack() as top:
        sems = {}
        for e in ENGS:
            sems[e] = top.enter_context(nc.semaphore("s_" + e))
        for i in range(NDMA):
            sems[("dma", i)] = top.enter_context(nc.semaphore("d%d" % i))
        S = Sched(nc, sems)

        uid = [0]

        def sbuf(es, name, shape, dtype):
            uid[0] += 1
            return T(es.enter_context(nc.sbuf_tensor("%s_%d" % (name, uid[0]), list(shape), dtype)))

        def psum(es, name, shape, dtype):
            uid[0] += 1
            return T(es.enter_context(nc.psum_tensor("%s_%d" % (name, uid[0]), list(shape), dtype)))

        R_uT, R_mix, R_conv, R_h1 = T(None), T(None), T(None), T(None)

        identb = sbuf(top, "identb", [128, 128], BF16)
        identf = sbuf(top, "identf", [128, 128], F32)
        onesb = sbuf(top, "onesb", [128, 128], BF16)
        onesf = sbuf(top, "onesf", [128, 128], F32)
        modT = sbuf(top, "modT", [128, 2, 48, 2], F32)
        gsT = sbuf(top, "gsT", [128, 2, 2, 16], F32)
        shT = sbuf(top, "shT", [128, 2, 2, 16], F32)
        gpT = sbuf(top, "gpT", [128, 2, 2, 16], F32)
        preg = sbuf(top, "preg", [128, 2, 16], F32)
        postg = sbuf(top, "postg", [128, 2, 16], F32)

        def V(fn, reads, writes):
            return S.op("vector", fn, reads, writes)

        def A(fn, reads, writes):
            return S.op("scalar", fn, reads, writes)

        def G(fn, reads, writes):
            return S.op("gpsimd", fn, reads, writes)

        def MM(out, lhsT, rhs, start, stop, reads, writes):
            return S.op("tensor", lambda e: e.matmul(out, lhsT=lhsT, rhs=rhs, start=start, stop=stop),
                        reads, writes, pe_chain=True)

        def act(out, in_, func, reads, writes, **kw):
            return A(lambda e: e.activation(out=out, in_=in_, func=func, **kw), reads, writes)

        def tt(out, in0, in1, op, reads, writes, eng="vector"):
            return S.op(eng, lambda e: e.tensor_tensor(out=out, in0=in0, in1=in1, op=op), reads, writes)

        def ts(out, in0, s1, s2, op0, op1, reads, writes, eng="vector"):
            if op1 is None:
                return S.op(eng, lambda e: e.tensor_scalar(out=out, in0=in0, scalar1=s1, scalar2=None, op0=op0),
                            reads, writes)
            return S.op(eng, lambda e: e.tensor_scalar(out=out, in0=in0, scalar1=s1, scalar2=s2, op0=op0, op1=op1),
                        reads, writes)

        def stt(out, in0, scalar, in1, op0, op1, reads, writes):
            return V(lambda e: e.scalar_tensor_tensor(out=out, in0=in0, scalar=scalar, in1=in1, op0=op0, op1=op1),
                     reads, writes)

        def cp(out, in_, reads, writes, eng="vector"):
            return S.op(eng, lambda e: e.tensor_copy(out=out, in_=in_), reads, writes)

        def dma(out, in_, reads, writes, q=None, is_output=False):
            return S.dma(lambda e: e.dma_start(out=out, in_=in_), reads, writes, q=q, is_output=is_output)

        condT = sbuf(top, "condT", [128, 16, 2], F32)
        adab = sbuf(top, "adab", [128, 2, 48], F32)

        def ada_block(l, cb, aw, psA):
            w = aw[cb % 2]
            dma(w[:], ada_w[l, :, :, cb * 512:(cb + 1) * 512], [], [w])
            for m in range(4):
                j = cb * 4 + m
                for k in range(16):
                    MM(psA[:, j, :], w[:, k, m * 128:(m + 1) * 128], condT[:, k, :],
                       k == 0, k == 15, [w, condT], [psA])

        def ada_finalize(l, psA):
            for i in range(2):
                tt(modT[:, l, :, i], psA[:, :, i], adab[:, l, :], ALU.add, [psA, adab], [modT])
            for i in range(2):
                stt(gsT[:, l, i, :], modT[:, l, 16:32, i], 1.0, preg[:, l, :], ALU.add, ALU.mult,
                    [modT, preg], [gsT])
                cp(shT[:, l, i, :], modT[:, l, 0:16, i], [modT], [shT])
                tt(gpT[:, l, i, :], modT[:, l, 32:48, i], postg[:, l, :], ALU.mult, [modT, postg], [gpT])

        with ExitStack() as ph:
            aw = [sbuf(ph, "aw%d" % i, [128, 16, 512], F32) for i in range(2)]
            psA = psum(ph, "psA", [128, 48, 2], F32)
            dma(identf[:], ident_d[:, :], [], [identf], q="sync")
            dma(identb[:], ident_d[:, :], [], [identb], q="gpsimd")
            dma(condT[:], cvec[:, :, :], [], [condT], q="sync")
            dma(adab[:], ada_b[:, :, :], [], [adab], q="sync")
            dma(preg[:], pre_g[:, :, :], [], [preg], q="sync")
            dma(postg[:], post_g[:, :, :], [], [postg], q="sync")
            V(lambda e: e.memset(onesf[:], 1.0), [], [onesf])
            V(lambda e: e.memset(onesb[:], 1.0), [], [onesb])
            act(condT[:], condT[:], AF.Silu, [condT], [condT])
            for cb in range(12):
                ada_block(0, cb, aw, psA)
            ada_finalize(0, psA)
            S.flush()

        def phase_p1(l, h_d, R_h):
            with ExitStack() as ph:
                xs = [sbuf(ph, "xs%d" % i, [128, D], F32) for i in range(2)]
                xn = [sbuf(ph, "xn%d" % i, [128, D], BF16) for i in range(2)]
                junk = sbuf(ph, "junk", [128, D], BF16)
                st = [sbuf(ph, "ust%d" % i, [128, 16, 512], BF16) for i in range(2)]
                ss = sbuf(ph, "ss", [128, 34], F32)
                rs = sbuf(ph, "rs", [128, 34], F32)
                pT = [psum(ph, "pT%d" % i, [128, 16, 128], BF16) for i in range(2)]
                if l == 0:
                    aw1 = [sbuf(ph, "aw1%d" % i, [128, 16, 512], F32) for i in range(2)]
                    psA1 = psum(ph, "psA1", [128, 48, 2], F32)
                for tix in range(34):
                    if l == 0 and tix % 2 == 1 and tix // 2 < 12:
                        ada_block(1, tix // 2, aw1, psA1)
                    i = 0 if tix < 32 else 1
                    x_ = xs[tix % 2]
                    xn_ = xn[tix % 2]
                    p_ = pT[tix % 2]
                    st_ = st[(tix // 4) % 2]
                    dma(x_[:], h_d[tix * 128:(tix + 1) * 128, :], [R_h], [x_])
                    act(junk[:], x_[:], AF.Square, [x_], [junk, ss], accum_out=ss[:, tix:tix + 1])
                    ts(rs[:, tix:tix + 1], ss[:, tix:tix + 1], 1.0 / D, EPS, ALU.mult, ALU.add, [ss], [rs])
                    act(rs[:, tix:tix + 1], rs[:, tix:tix + 1], AF.Sqrt, [rs], [rs])
                    V(lambda e, a=rs[:, tix:tix + 1]: e.reciprocal(out=a, in_=a), [rs], [rs])
                    act(xn_[:], x_[:], AF.Copy, [x_, rs], [xn_], scale=rs[:, tix:tix + 1])
                    for c in range(16):
                        S.op("tensor", lambda e, o=p_[:, c, :], a=xn_[:, c * 128:(c + 1) * 128]:
                             e.transpose(out=o, in_=a, identity=identb[:]), [xn_, identb], [p_], pe_chain=True)
                    q4 = tix % 4
                    for c in range(16):
                        ts(st_[:, c, q4 * 128:(q4 + 1) * 128], p_[:, c, :], gsT[:, l, i, c:c + 1],
                           shT[:, l, i, c:c + 1], ALU.mult, ALU.add, [p_, gsT, shT], [st_])
                    if q4 == 3 or tix == 33:
                        t0 = (tix // 4) * 512
                        n = (q4 + 1) * 128
                        dma(uT_d.rearrange("(c p) t -> p c t", p=128)[:, :, t0:t0 + n], st_[:, :, 0:n], [st_], [R_uT])
                if l == 0:
                    ada_finalize(1, psA1)
                S.flush()

        def load_w(wsb, wsrc, mtiles):
            for i, mt in enumerate(mtiles):
                dma(wsb[:, i, :, :], wsrc[mt, :, :, :], [], [wsb], q="gpsimd")

        rot = {"ps": 0, "ut": 0}

        def proj_fm(ph, wsrc, mtiles, chunks, epilogue, tag, ps_tiles, ut_tiles, wsb=None):
            nm = len(mtiles)
            if wsb is None:
                wsb = sbuf(ph, "w_" + tag, [128, nm, 16, 128], BF16)
                load_w(wsb, wsrc, mtiles)
            for ci, (t0, n) in enumerate(chunks):
                ut = ut_tiles[rot["ut"] % len(ut_tiles)]
                rot["ut"] += 1
                dma(ut[:, :, 0:n], uT_d.rearrange("(c p) t -> p c t", p=128)[:, :, t0:t0 + n], [R_uT], [ut], q="sync")
                for i in range(nm):
                    ps = ps_tiles[rot["ps"] % len(ps_tiles)]
                    rot["ps"] += 1
                    for k in range(16):
                        MM(ps[:, 0:n], wsb[:, i, k, :], ut[:, k, 0:n], k == 0, k == 15, [wsb, ut], [ps])
                    epilogue(i, ci, t0, n, ps)

        def phase_conv():
            with ExitStack() as ph:
                cw = sbuf(ph, "cw", [128, 8, 31], F32)
                cv = sbuf(ph, "cv", [128, 3, 8], F32)
                s1 = sbuf(ph, "s1", [128, TOK], F32)
                s2 = sbuf(ph, "s2", [128, TOK], F32)
                dma(cw[:], conv_w[:, :, :], [], [cw], q="sync")
                dma(cv[:], conv_v[:, :, :], [], [cv], q="sync")
                with ExitStack() as p1:
                    ut_tiles = [sbuf(p1, "ut%d" % i, [128, 16, 512], BF16) for i in range(2)]
                    ps_tiles = [psum(p1, "ps%d" % i, [128, 512], F32) for i in range(4)]
                    pst = [psum(p1, "pst%d" % i, [128, 512], F32) for i in range(2)]
                    apad = [sbuf(p1, "apad%d" % i, [128, 4400], BF16) for i in range(4)]
                    dgk = [sbuf(p1, "dgk%d" % i, [128, 31, 128], BF16) for i in range(2)]
                    pcv = [psum(p1, "pcv%d" % i, [128, 512], F32) for i in range(2)]
                    cvbs = [sbuf(p1, "cvb%d" % i, [128, TOK], BF16) for i in range(2)]
                    sqbs = [sbuf(p1, "sqb%d" % i, [128, TOK], BF16) for i in range(2)]
                    wA8 = sbuf(p1, "wA8", [128, 8, 16, 128], BF16)
                    sig = [sbuf(p1, "sig%d" % i, [128, 512], BF16) for i in range(2)]
                    for a_ in apad:
                        V(lambda e, a_=a_: e.memset(a_[:], 0.0), [], [a_])
                    for half in range(2):
                        cs = [4 * half + q for q in range(4)]
                        mts = []
                        for c in cs:
                            mts += [8 + c, c]
                        load_w(wA8, w_in0, mts)

                        def epi(i, ci, t0, n, ps):
                            sg = sig[ci % 2]
                            ap_ = apad[i // 2]
                            if i % 2 == 0:
                                act(sg[:, 0:n], ps[:, 0:n], AF.Sigmoid, [ps], [sg])
                            else:
                                off = 15 + t0 if t0 < L else 4126
                                tt(ap_[:, off:off + n], ps[:, 0:n], sg[:, 0:n], ALU.mult, [ps, sg], [ap_])

                        proj_fm(p1, w_in0, mts, CHUNKS, epi, "a1", ps_tiles, ut_tiles, wsb=wA8)
                        for q, c in enumerate(cs):
                            ap_ = apad[q]
                            dg_ = dgk[c % 2]
                            cvb, sqb = cvbs[c % 2], sqbs[c % 2]
                            tt(dg_[:], identb[:].unsqueeze(1).to_broadcast([128, 31, 128]),
                               cw[:, c, :].unsqueeze(2).to_broadcast([128, 31, 128]), ALU.mult, [identb, cw], [dg_])
                            for ci, (t0, n) in enumerate(CHUNKS):
                                i0 = t0 if t0 < L else 4111
                                pc_ = pcv[ci % 2]
                                for k in range(31):
                                    MM(pc_[:, 0:n], dg_[:, k, :], ap_[:, i0 + k:i0 + k + n], k == 0, k == 30,
                                       [dg_, ap_], [pc_])
                                act(cvb[:, t0:t0 + n], pc_[:, 0:n], AF.Identity, [pc_, cv], [cvb], bias=cv[:, 0, c:c + 1])
                                act(sqb[:, t0:t0 + n], pc_[:, 0:n], AF.Square, [pc_, cv], [sqb], bias=cv[:, 0, c:c + 1])
                            for ci, (t0, n) in enumerate(CHUNKS):
                                MM(pst[0][:, 0:n], onesb[:], cvb[:, t0:t0 + n], True, True, [onesb, cvb], [pst[0]])
                                MM(pst[1][:, 0:n], onesb[:], sqb[:, t0:t0 + n], True, True, [onesb, sqb], [pst[1]])
                                if c == 0:
                                    cp(s1[:, t0:t0 + n], pst[0][:, 0:n], [pst[0]], [s1])
                                    cp(s2[:, t0:t0 + n], pst[1][:, 0:n], [pst[1]], [s2])
                                else:
                                    tt(s1[:, t0:t0 + n], pst[0][:, 0:n], s1[:, t0:t0 + n], ALU.add, [pst[0], s1], [s1])
                                    tt(s2[:, t0:t0 + n], pst[1][:, 0:n], s2[:, t0:t0 + n], ALU.add, [pst[1], s2], [s2])
                            dma(convT_d[c * 128:(c + 1) * 128, :], cvb[:], [cvb], [R_conv], q="sync")
                    msq = sbuf(p1, "msq", [128, 512], F32)
                    for (t0, n) in CHUNKS:
                        ts(s1[:, t0:t0 + n], s1[:, t0:t0 + n], 1.0 / 1024, None, ALU.mult, None, [s1], [s1])
                        tt(msq[:, 0:n], s1[:, t0:t0 + n], s1[:, t0:t0 + n], ALU.mult, [s1], [msq])
                        stt(s2[:, t0:t0 + n], s2[:, t0:t0 + n], 1.0 / 1024, msq[:, 0:n], ALU.mult, ALU.subtract,
                            [s2, msq], [s2])
                    ts(s2[:], s2[:], EPS, None, ALU.add, None, [s2], [s2])
                    act(s2[:], s2[:], AF.Sqrt, [s2], [s2])
                    V(lambda e: e.reciprocal(out=s2[:], in_=s2[:]), [s2], [s2])
                    S.flush()
                with ExitStack() as p2:
                    ut_tiles = [sbuf(p2, "ut%d" % i, [128, 16, 512], BF16) for i in range(2)]
                    ps_tiles = [psum(p2, "ps%d" % i, [128, 512], F32) for i in range(4)]
                    cvl = [sbuf(p2, "cvl%d" % i, [128, 8, 512], BF16) for i in range(2)]
                    sgt = [sbuf(p2, "sgt%d" % i, [128, 512], F32) for i in range(2)]
                    t1 = [sbuf(p2, "t1%d" % i, [128, 512], F32) for i in range(2)]
                    mo = [sbuf(p2, "mo%d" % i, [128, 512], BF16) for i in range(2)]
                    wG8 = sbuf(p2, "wG8", [128, 8, 16, 128], BF16)
                    load_w(wG8, w_in0, [16 + c for c in range(8)])
                    kk2 = [0]

                    def epi(i, ci, t0, n, ps):
                        c = i
                        cvl_ = cvl[ci % 2]
                        if c == 0:
                            dma(cvl_[:, :, 0:n], convT_d.rearrange("(c p) t -> p c t", p=128)[:, :, t0:t0 + n],
                                [R_conv], [cvl_], q="sync")
                        k2 = kk2[0]
                        kk2[0] += 1
                        sg, t1_, mo_ = sgt[k2 % 2], t1[k2 % 2], mo[k2 % 2]
                        act(sg[:, 0:n], ps[:, 0:n], AF.Silu, [ps], [sg])
                        tt(t1_[:, 0:n], cvl_[:, c, 0:n], s1[:, t0:t0 + n], ALU.subtract, [cvl_, s1], [t1_])
                        tt(t1_[:, 0:n], t1_[:, 0:n], s2[:, t0:t0 + n], ALU.mult, [t1_, s2], [t1_])
                        act(t1_[:, 0:n], t1_[:, 0:n], AF.Silu, [t1_, cv], [t1_],
                            scale=cv[:, 1, c:c + 1], bias=cv[:, 2, c:c + 1])
                        tt(mo_[:, 0:n], t1_[:, 0:n], sg[:, 0:n], ALU.mult, [t1_, sg], [mo_])
                        dma(mixT_d[c * 128:(c + 1) * 128, t0:t0 + n], mo_[:, 0:n], [mo_], [R_mix], q="gpsimd")

                    proj_fm(p2, w_in0, [16 + c for c in range(8)], CHUNKS, epi, "a2", ps_tiles, ut_tiles, wsb=wG8)
                    S.flush()

        def phase_fourier():
            with ExitStack() as ph:
                fg = sbuf(ph, "fg", [128, 4, 2], F32)
                cs256 = sbuf(ph, "cs256", [128, 2, 512], BF16)
                cctx = sbuf(ph, "cctx", [128, 2, 512], BF16)
                bnT = sbuf(ph, "bnT", [128, 2, TOK], BF16)
                sgT = sbuf(ph, "sgT", [128, 2, TOK], BF16)
                dma(fg[:], four_g[:, :, :], [], [fg], q="sync")
                dma(cs256[:], cs256_d[:, :, :], [], [cs256], q="sync")
                dma(cctx[:], cctx_d[:, :, :], [], [cctx], q="sync")
                w1t = sbuf(ph, "w1t", [128, 3, 128], BF16)
                w3t = sbuf(ph, "w3t", [128, 2, 128], BF16)
                twd = sbuf(ph, "twd", [128, 2, L], BF16)
                dma(w1t[:], w1_d[:, :, :], [], [w1t], q="sync")
                dma(w3t[:], w3_d[:, :, :], [], [w3t], q="sync")
                dma(twd[:], twd_d[:, :, :], [], [twd], q="sync")
                wB = [sbuf(ph, "wB%d" % i, [128, 4, 16, 128], BF16) for i in range(2)]
                fmt = lambda g_: [24 + 2 * g_, 25 + 2 * g_, 32 + 2 * g_, 33 + 2 * g_]
                load_w(wB[0], w_in0, fmt(0))
                for g in range(4):
                    with ExitStack() as p1:
                        ut_tiles = [sbuf(p1, "ut%d" % i, [128, 16, 512], BF16) for i in range(2)]
                        ps_tiles = [psum(p1, "ps%d" % i, [128, 512], F32) for i in range(6)]
                        pss = [psum(p1, "pss%d" % i, [128, 512], F32) for i in range(2)]
                        sq = [sbuf(p1, "sq%d" % i, [128, 512], BF16) for i in range(4)]
                        rst = [sbuf(p1, "rst%d" % i, [128, 512], F32) for i in range(2)]
                        held = {}

                        def epi(i, ci, t0, n, ps, g=g):
                            if i >= 2:
                                act(sgT[:, i - 2, t0:t0 + n], ps[:, 0:n], AF.Silu, [ps], [sgT])
                                return
                            sq_ = sq[(ci % 2) * 2 + i]
                            act(sq_[:, 0:n], ps[:, 0:n], AF.Square, [ps], [sq_])
                            held[i] = (ps, sq_)
                            if i == 1:
                                pss_ = pss[ci % 2]
                                rst_ = rst[ci % 2]
                                for jj in range(2):
                                    MM(pss_[:, 0:n], onesb[:], held[jj][1][:, 0:n], jj == 0, jj == 1,
                                       [onesb, held[jj][1]], [pss_])
                                ts(rst_[:, 0:n], pss_[:, 0:n], 1.0 / 256, EPS, ALU.mult, ALU.add, [pss_], [rst_])
                                act(rst_[:, 0:n], rst_[:, 0:n], AF.Sqrt, [rst_], [rst_])
                                V(lambda e, a=rst_[:, 0:n]: e.reciprocal(out=a, in_=a), [rst_], [rst_])
                                for jj in range(2):
                                    if t0 < L:
                                        o_ = bnT[:, jj, 0:L].rearrange("p (b a) -> p a b", a=64)[:, 8 * ci:8 * ci + 8, :]
                                        stt(o_, held[jj][0][:, 0:512].rearrange("p (a b) -> p a b", b=64),
                                            fg[:, g, jj:jj + 1], rst_[:, 0:512].rearrange("p (a b) -> p a b", b=64),
                                            ALU.mult, ALU.mult, [held[jj][0], fg, rst_], [bnT])
                                    else:
                                        stt(bnT[:, jj, t0:t0 + n], held[jj][0][:, 0:n], fg[:, g, jj:jj + 1],
                                            rst_[:, 0:n], ALU.mult, ALU.mult, [held[jj][0], fg, rst_], [bnT])

                        proj_fm(p1, w_in0, fmt(g), CHUNKS, epi, "b1", ps_tiles, ut_tiles, wsb=wB[g % 2])
                        S.flush()
                    if g < 3:
                        load_w(wB[(g + 1) % 2], w_in0, fmt(g + 1))
                    with ExitStack() as p2:
                        YB = sbuf(p2, "YB", [128, 32, 512], BF16)
                        Yc_ = sbuf(p2, "Yc_", [128, 2, 512], BF16)
                        Bsb = sbuf(p2, "Bsb", [128, 2, 2, L], BF16)
                        fo = [sbuf(p2, "fo%d" % i, [128, L], BF16) for i in range(2)]
                        foc = sbuf(p2, "foc", [128, 256], BF16)
                        m1 = [sbuf(p2, "fm1%d" % i, [128, 512], F32) for i in range(2)]
                        m2 = [sbuf(p2, "fm2%d" % i, [128, 512], F32) for i in range(2)]
                        psy = [psum(p2, "psy%d" % i, [128, 512], F32) for i in range(2)]
                        pa = [psum(p2, "pa%d" % i, [128, 4, 128], F32) for i in range(4)]
                        ptr = [psum(p2, "ptr%d" % i, [128, 8, 128], BF16) for i in range(2)]
                        for beta in range(32):
                            p_ = psy[beta % 2]
                            for jj in range(2):
                                MM(p_[:], bnT[:, jj, beta * 128:(beta + 1) * 128], cs256[:, jj, :], jj == 0, jj == 1,
                                   [bnT, cs256], [p_])
                            if beta % 2 == 0:
                                cp(YB[:, beta, :], p_[:], [p_], [YB])
                            else:
                                act(YB[:, beta, :], p_[:], AF.Copy, [p_], [YB])
                        for t_ in range(2):
                            p_ = psy[t_ % 2]
                            for jj in range(2):
                                MM(p_[:], bnT[:, jj, L + t_ * 128:L + (t_ + 1) * 128], cs256[:, jj, :], jj == 0, jj == 1,
                                   [bnT, cs256], [p_])
                            cp(Yc_[:, t_, :], p_[:], [p_], [Yc_])
                        sc_x = 1.0 / math.sqrt(L * 256.0)
                        sc_c = 1.0 / math.sqrt(LC * 256.0)
                        kq = 0
                        for kk in range(2):
                            for bq in range(8):
                                par_, pai_ = pa[(kq % 2) * 2], pa[(kq % 2) * 2 + 1]
                                m1_, m2_ = m1[kq % 2], m2[kq % 2]
                                kq += 1
                                for q in range(4):
                                    beta = bq * 4 + q
                                    yc = YB[:, beta, kk * 128:(kk + 1) * 128]
                                    ys = YB[:, beta, 256 + kk * 128:256 + (kk + 1) * 128]
                                    MM(par_[:, q, :], yc, w1t[:, 0, :], True, False, [YB, w1t], [par_])
                                    MM(par_[:, q, :], ys, w1t[:, 1, :], False, True, [YB, w1t], [par_])
                                    MM(pai_[:, q, :], yc, w1t[:, 1, :], True, False, [YB, w1t], [pai_])
                                    MM(pai_[:, q, :], ys, w1t[:, 2, :], False, True, [YB, w1t], [pai_])
                                sl = slice(bq * 512, (bq + 1) * 512)
                                arv = par_[:].rearrange("p a b -> p (a b)")
                                aiv = pai_[:].rearrange("p a b -> p (a b)")
                                tt(m1_[:], arv, twd[:, 0, sl], ALU.mult, [par_, twd], [m1_])
                                tt(m2_[:], aiv, twd[:, 1, sl], ALU.mult, [pai_, twd], [m2_])
                                ob = lambda ri: Bsb[:, kk, ri, :].rearrange("p (m b) -> p b m", b=64)[:, 8 * bq:8 * bq + 8, :]
                                v3 = lambda t_: t_[:].rearrange("p (b m) -> p b m", m=64)
                                tt(ob(0), v3(m1_), v3(m2_), ALU.add, [m1_, m2_], [Bsb], eng="gpsimd")
                                tt(m1_[:], aiv, twd[:, 0, sl], ALU.mult, [pai_, twd], [m1_])
                                tt(m2_[:], arv, twd[:, 1, sl], ALU.mult, [par_, twd], [m2_])
                                tt(ob(1), v3(m1_), v3(m2_), ALU.subtract, [m1_, m2_], [Bsb], eng="gpsimd")
                        BT = YB
                        ke = 0
                        for mq in range(8):
                            for ri in range(2):
                                pt_ = ptr[ke % 2]
                                for q in range(4):
                                    mu = mq * 4 + q
                                    for kk in range(2):
                                        src = Bsb[:, kk, ri, mu * 128:(mu + 1) * 128]
                                        S.op("tensor", lambda e, o=pt_[:, q * 2 + kk, :], a=src:
                                             e.transpose(out=o, in_=a, identity=identb[:]), [Bsb, identb], [pt_],
                                             pe_chain=True)
                                dst = BT[:, mq * 4:(mq + 1) * 4, ri * 256:(ri + 1) * 256]
                                srcp = pt_[:].rearrange("p (q k) c -> p q (k c)", k=2)
                                if ke % 2 == 0:
                                    cp(dst, srcp, [pt_], [BT])
                                else:
                                    act(dst, srcp, AF.Copy, [pt_], [BT])
                                ke += 1
                        kq = 0
                        for kk in range(2):
                            fo_ = fo[kk]
                            fov = fo_[:].rearrange("p (mb ma) -> p ma mb", ma=64)
                            sgv = sgT[:, kk, 0:L].rearrange("p (mb ma) -> p ma mb", ma=64)
                            for mq in range(8):
                                pf_ = pa[kq % 4]
                                kq += 1
                                for q in range(4):
                                    mu = mq * 4 + q
                                    MM(pf_[:, q, :], BT[:, mu, kk * 128:(kk + 1) * 128], w3t[:, 0, :], True, False,
                                       [BT, w3t], [pf_])
                                    MM(pf_[:, q, :], BT[:, mu, 256 + kk * 128:256 + (kk + 1) * 128], w3t[:, 1, :],
                                       False, True, [BT, w3t], [pf_])
                                stt(fov[:, mq * 8:(mq + 1) * 8, :], pf_[:].rearrange("p q (l m) -> p (q l) m", l=2), sc_x,
                                    sgv[:, mq * 8:(mq + 1) * 8, :], ALU.mult, ALU.mult, [pf_, sgT], [fo_])
                            r0 = 1024 + g * 256 + kk * 128
                            dma(mixT_d[r0:r0 + 128, 0:L], fo_[:], [fo_], [R_mix], q="sync")
                        for kk in range(2):
                            pc = psy[kk]
                            for t_ in range(2):
                                MM(pc[:, 0:256], Yc_[:, t_, kk * 128:(kk + 1) * 128], cctx[:, t_, 0:256],
                                   t_ == 0, False, [Yc_, cctx], [pc])
                                MM(pc[:, 0:256], Yc_[:, t_, 256 + kk * 128:256 + (kk + 1) * 128], cctx[:, t_, 256:512],
                                   False, t_ == 1, [Yc_, cctx], [pc])
                            stt(foc[:], pc[:, 0:256], sc_c, sgT[:, kk, L:TOK], ALU.mult, ALU.mult, [pc, sgT], [foc])
                            r0 = 1024 + g * 256 + kk * 128
                            dma(mixT_d[r0:r0 + 128, L:TOK], foc[:], [foc], [R_mix], q="gpsimd")
                        S.flush()

        def phase_out(l, wout_d, hin_d, R_hin, ntiles, hout_d, R_hout, is_output):
            with ExitStack() as ph:
                wo = sbuf(ph, "wo", [128, 16, D], BF16)
                for k4 in range(4):
                    dma(wo[:, k4 * 4:(k4 + 1) * 4, :], wout_d[:, k4 * 4:(k4 + 1) * 4, :], [], [wo], q="gpsimd")
                gbc = [sbuf(ph, "gbc%d" % i, [128, D], F32) for i in range(2)]
                dg = sbuf(ph, "dg", [128, 128], F32)
                mx = [sbuf(ph, "mx%d" % i, [128, 16, 512], BF16) for i in range(2)]
                hr = [sbuf(ph, "hr%d" % i, [128, D], F32) for i in range(2)]
                tm = [sbuf(ph, "tm%d" % i, [128, D], F32) for i in range(2)]
                junk = sbuf(ph, "junk", [128, D], BF16)
                ss = sbuf(ph, "ss", [128, 34], F32)
                po = [psum(ph, "po%d" % i, [128, 4, 512], F32) for i in range(2)]
                nseq = 2 if ntiles > 32 else 1
                for i in range(nseq):
                    for c in range(16):
                        ts(dg[:], identf[:], gpT[:, l, i, c:c + 1], None, ALU.mult, None, [identf, gpT], [dg])
                        MM(po[0][:, c // 4, (c % 4) * 128:(c % 4 + 1) * 128], onesf[:], dg[:], True, True,
                           [onesf, dg], [po[0]])
                    for q in range(4):
                        cp(gbc[i][:, q * 512:(q + 1) * 512], po[0][:, q, :], [po[0]], [gbc[i]])
                for tix in range(ntiles):
                    i = 0 if tix < 32 else 1
                    mx_ = mx[(tix // 4) % 2]
                    if tix % 4 == 0:
                        n = min(512, ntiles * 128 - tix * 128)
                        t0 = tix * 128
                        dma(mx_[:, :, 0:n], mixT_d.rearrange("(c p) t -> p c t", p=128)[:, :, t0:t0 + n],
                            [R_mix], [mx_], q="sync")
                    hr_, tm_, po_ = hr[tix % 2], tm[tix % 2], po[tix % 2]
                    dma(hr_[:], hin_d[tix * 128:(tix + 1) * 128, :], [R_hin], [hr_], q="sync")
                    q4 = tix % 4
                    for nn in range(4):
                        for k in range(16):
                            MM(po_[:, nn, :], mx_[:, k, q4 * 128:(q4 + 1) * 128], wo[:, k, nn * 512:(nn + 1) * 512],
                               k == 0, k == 15, [mx_, wo], [po_])
                    act(junk[:], po_[:].rearrange("p a b -> p (a b)"), AF.Square, [po_], [junk, ss],
                        accum_out=ss[:, tix:tix + 1])
                    ts(ss[:, tix:tix + 1], ss[:, tix:tix + 1], 1.0 / D, EPS, ALU.mult, ALU.add, [ss], [ss])
                    act(ss[:, tix:tix + 1], ss[:, tix:tix + 1], AF.Sqrt, [ss], [ss])
                    V(lambda e, a=ss[:, tix:tix + 1]: e.reciprocal(out=a, in_=a), [ss], [ss])
                    tt(tm_[:], po_[:].rearrange("p a b -> p (a b)"), gbc[i][:], ALU.mult, [po_, gbc[i]], [tm_])
                    stt(tm_[:], tm_[:], ss[:, tix:tix + 1], hr_[:], ALU.mult, ALU.add, [tm_, ss, hr_], [tm_])
                    dma(hout_d[tix * 128:(tix + 1) * 128, :], tm_[:], [tm_], [R_hout], q="gpsimd", is_output=is_output)
                S.flush()

        def phase_na():
            with ExitStack() as ph:
                ur = sbuf(ph, "ur", [128, 16, 2560], BF16)
                uv = uT_d.rearrange("(c p) t -> p c t", p=128)
                for q in range(4):
                    dma(ur[:, q * 4:(q + 1) * 4, 0:NKV], uv[:, q * 4:(q + 1) * 4, 0:NKV], [R_uT], [ur])
                dma(ur[:, :, NKV:2560], uv[:, :, L:TOK], [R_uT], [ur], q="sync")
                wq = [sbuf(ph, "wq%d" % i, [128, 4, 16, 128], BF16) for i in range(2)]
                bt = [sbuf(ph, "bt%d" % i, [128, 3, 5, 128], BF16) for i in range(2)]
                qT = [sbuf(ph, "qT%d" % i, [128, OWN], BF16) for i in range(2)]
                kT = [sbuf(ph, "kT%d" % i, [128, 2560], BF16) for i in range(2)]
                sg = [sbuf(ph, "sg%d" % i, [128, OWN], BF16) for i in range(2)]
                Vh = [sbuf(ph, "Vh%d" % i, [128, 20, 128], BF16) for i in range(2)]
                PT = [sbuf(ph, "PT%d" % i, [128, 7, 128], BF16) for i in range(2)]
                vTf = sbuf(ph, "vTf", [128, 2560], F32)
                rd = [sbuf(ph, "rd%d" % i, [128, 128], F32) for i in range(2)]
                ot = [sbuf(ph, "ot%d" % i, [128, 128], F32) for i in range(2)]
                naT = [sbuf(ph, "naT%d" % i, [128, OWN], BF16) for i in range(2)]
                pp = [psum(ph, "pp%d" % i, [128, 512], F32) for i in range(2)]
                pS = [psum(ph, "pS%d" % i, [128, 8, 128], F32) for i in range(2)]
                pO = [psum(ph, "pO%d" % i, [128, 2, 128], F32) for i in range(2)]
                kp = 0
                isq = 1.0 / math.sqrt(128.0)
                for h in range(12):
                    w_ = wq[h % 2]
                    bt_ = bt[h % 2]
                    for i, mt in enumerate((h, 12 + h, 24 + h, 36 + h)):
                        dma(w_[:, i, :, :], w_in1[mt, :, :, :], [], [w_], q="gpsimd")
                    dma(bt_[:], na_bias[:, h, :, :, :].rearrange("c p k q -> p c k q"), [], [bt_], q="gpsimd")
                    qT_, kT_, sg_, Vh_, naT_ = qT[h % 2], kT[h % 2], sg[h % 2], Vh[h % 2], naT[h % 2]
                    for ci in range(5):
                        t0 = ci * 512
                        ps = pp[kp % 2]; kp += 1
                        for k in range(16):
                            MM(ps[:], w_[:, 1, k, :], ur[:, k, t0:t0 + 512], k == 0, k == 15, [w_, ur], [ps])
                        cp(kT_[:, t0:t0 + 512], ps[:], [ps], [kT_])
                        if ci < 4:
                            ps = pp[kp % 2]; kp += 1
                            for k in range(16):
                                MM(ps[:], w_[:, 0, k, :], ur[:, k, t0:t0 + 512], k == 0, k == 15, [w_, ur], [ps])
                            act(qT_[:, t0:t0 + 512], ps[:], AF.Copy, [ps], [qT_], scale=isq)
                            ps = pp[kp % 2]; kp += 1
                            for k in range(16):
                                MM(ps[:], w_[:, 3, k, :], ur[:, k, t0:t0 + 512], k == 0, k == 15, [w_, ur], [ps])
                            act(sg_[:, t0:t0 + 512], ps[:], AF.Silu, [ps], [sg_])
                    for ci in range(5):
                        t0 = ci * 512
                        ps = pp[kp % 2]; kp += 1
                        for k in range(16):
                            MM(ps[:], w_[:, 2, k, :], ur[:, k, t0:t0 + 512], k == 0, k == 15, [w_, ur], [ps])
                        act(vTf[:, t0:t0 + 512], ps[:], AF.Copy, [ps], [vTf])
                    for t4 in range(5):
                        ps = pp[kp % 2]; kp += 1
                        for q in range(4):
                            tix = t4 * 4 + q
                            S.op("tensor", lambda e, o=ps[:, q * 128:(q + 1) * 128], a=vTf[:, tix * 128:(tix + 1) * 128]:
                                 e.transpose(out=o, in_=a, identity=identf[:]), [vTf, identf], [ps], pe_chain=True)
                        cp(Vh_[:, t4 * 4:(t4 + 1) * 4, :], ps[:].rearrange("p (a b) -> p a b", b=128), [ps], [Vh_])
                    for j in range(16):
                        cls = min(j, 2)
                        ws = min(max(2 * j - 4, 0), 26)
                        pS_, pO_, PT_ = pS[j % 2], pO[j % 2], PT[j % 2]
                        rd_, ot_ = rd[j % 2], ot[j % 2]
                        for kt in range(7):
                            k0 = ws * 64 + kt * 128 if kt < 5 else NKV + (kt - 5) * 128
                            MM(pS_[:, kt, :], kT_[:, k0:k0 + 128], qT_[:, j * 128:(j + 1) * 128], True, kt >= 5,
                               [kT_, qT_], [pS_])
                            if kt < 5:
                                MM(pS_[:, kt, :], identb[:], bt_[:, cls, kt, :], False, True, [identb, bt_], [pS_])
                        act(PT_[:, 0:4, :], pS_[:, 0:4, :], AF.Exp, [pS_], [PT_])
                        act(PT_[:, 4:7, :], pS_[:, 4:7, :], AF.Exp, [pS_], [PT_])
                        for kt in range(7):
                            vt = ws // 2 + kt if kt < 5 else 18 + (kt - 5)
                            MM(pO_[:, 0, :], Vh_[:, vt, :], PT_[:, kt, :], kt == 0, kt == 6, [Vh_, PT_], [pO_])
                        for kt in range(7):
                            MM(pO_[:, 1, :], onesb[:], PT_[:, kt, :], kt == 0, kt == 6, [onesb, PT_], [pO_])
                        V(lambda e, o=rd_[:], a=pO_[:, 1, :]: e.reciprocal(out=o, in_=a), [pO_], [rd_])
                        tt(ot_[:], pO_[:, 0, :], rd_[:], ALU.mult, [pO_, rd_], [ot_])
                        tt(naT_[:, j * 128:(j + 1) * 128], ot_[:], sg_[:, j * 128:(j + 1) * 128], ALU.mult,
                           [ot_, sg_], [naT_])
                    dma(mixT_d[h * 128:(h + 1) * 128, 0:OWN], naT_[:], [naT_], [R_mix], q="sync")
                S.flush()

        def phase_s5():
            with ExitStack() as ph:
                dT = sbuf(ph, "dT", [128, 4, TOK], BF16)
                sdg = sbuf(ph, "sdg", [128, 4, OWN], BF16)
                ysb = sbuf(ph, "ysb", [128, 4, OWN], F32)
                with ExitStack() as p1:
                    ut_tiles = [sbuf(p1, "ut%d" % i, [128, 16, 512], BF16) for i in range(2)]
                    ps_tiles = [psum(p1, "ps%d" % i, [128, 512], F32) for i in range(4)]

                    def epi(i, ci, t0, n, ps):
                        if i < 4:
                            cp(dT[:, i, t0:t0 + n], ps[:, 0:n], [ps], [dT])
                        elif t0 < OWN:
                            act(sdg[:, i - 4, t0:t0 + n], ps[:, 0:n], AF.Silu, [ps], [sdg])

                    proj_fm(p1, w_in1, [48 + i for i in range(8)], CHUNKS, epi, "s5p", ps_tiles, ut_tiles)
                    S.flush()
                with ExitStack() as p2:
                    def small(name):
                        return sbuf(p2, name, [128, 2, 16], F32)
                    are, aim, ldt = small("are"), small("aim"), small("ldt")
                    dtt, rr, thp, cfr, cfi = small("dtt"), small("rr"), small("thp"), small("cfr"), small("cfi")
                    w1, w2, w3, w4 = small("w1"), small("w2"), small("w3"), small("w4")
                    wi = sbuf(p2, "wi", [128, 2, 16], I32)
                    dma(are[:], s5are[:, :, :], [], [are], q="sync")
                    dma(aim[:], s5aim[:, :, :], [], [aim], q="sync")
                    dma(ldt[:], s5ldt[:, :, :], [], [ldt], q="sync")
                    act(dtt[:], ldt[:], AF.Exp, [ldt], [dtt])
                    tt(w1[:], are[:], dtt[:], ALU.mult, [are, dtt], [w1])
                    act(rr[:], w1[:], AF.Exp, [w1], [rr])
                    tt(w1[:], aim[:], dtt[:], ALU.mult, [aim, dtt], [w1])
                    ts(thp[:], w1[:], 1.0 / (2.0 * math.pi), None, ALU.mult, None, [w1], [thp])

                    def sincos(src, sin_out, cos_out):
                        cp(wi[:], src[:], [src], [wi])
                        cp(w2[:], wi[:], [wi], [w2])
                        tt(w2[:], src[:], w2[:], ALU.subtract, [src, w2], [w2])
                        act(sin_out[:], w2[:], AF.Sin, [w2], [sin_out], scale=6.28318)
                        ts(w3[:], src[:], 0.25, None, ALU.add, None, [src], [w3])
                        cp(wi[:], w3[:], [w3], [wi])
                        cp(w2[:], wi[:], [wi], [w2])
                        tt(w2[:], w3[:], w2[:], ALU.subtract, [w3, w2], [w2])
                        act(cos_out[:], w2[:], AF.Sin, [w2], [cos_out], scale=6.28318)

                    sn, cs_ = small("sn"), small("cs_")
                    sincos(thp, sn, cs_)
                    tt(w1[:], rr[:], cs_[:], ALU.mult, [rr, cs_], [w1])
                    ts(w1[:], w1[:], -1.0, None, ALU.add, None, [w1], [w1])
                    tt(w4[:], rr[:], sn[:], ALU.mult, [rr, sn], [w4])
                    tt(w2[:], are[:], are[:], ALU.mult, [are], [w2])
                    tt(w3[:], aim[:], aim[:], ALU.mult, [aim], [w3])
                    tt(w2[:], w2[:], w3[:], ALU.add, [w2, w3], [w2])
                    V(lambda e: e.reciprocal(out=w2[:], in_=w2[:]), [w2], [w2])
                    tt(cfr[:], w1[:], are[:], ALU.mult, [w1, are], [cfr])
                    tt(w3[:], w4[:], aim[:], ALU.mult, [w4, aim], [w3])
                    tt(cfr[:], cfr[:], w3[:], ALU.add, [cfr, w3], [cfr])
                    tt(cfr[:], cfr[:], w2[:], ALU.mult, [cfr, w2], [cfr])
                    tt(cfi[:], w4[:], are[:], ALU.mult, [w4, are], [cfi])
                    tt(w3[:], w1[:], aim[:], ALU.mult, [w1, aim], [w3])
                    tt(cfi[:], cfi[:], w3[:], ALU.subtract, [cfi, w3], [cfi])
                    tt(cfi[:], cfi[:], w2[:], ALU.mult, [cfi, w2], [cfi])

                    braw = sbuf(p2, "braw", [128, 2, 2, 16, 128], BF16)
                    ctw = sbuf(p2, "ctw", [128, 2, 2, 16, 128], BF16)
                    dma(braw[:], s5braw[:, :, :, :, :], [], [braw], q="gpsimd")
                    dma(ctw[:], s5ct[:, :, :, :, :], [], [ctw], q="gpsimd")
                    ts(ctw[:, 1], ctw[:, 1], -1.0, None, ALU.mult, None, [ctw], [ctw])
                    ctw2 = ctw
                    ctmp = sbuf(p2, "ctmp", [128, 128], F32)
                    ctmpb = sbuf(p2, "ctmpb", [128, 128], F32)
                    for d_ in range(2):
                        for j in range(16):
                            c0, c1 = ctw[:, 0, d_, j, :], ctw[:, 1, d_, j, :]
                            ts(ctmp[:], c1, cfi[:, d_, j:j + 1], None, ALU.mult, None, [ctw, cfi], [ctmp])
                            stt(ctmpb[:], c0, cfr[:, d_, j:j + 1], ctmp[:], ALU.mult, ALU.add, [ctw, cfr, ctmp], [ctmpb])
                            ts(ctmp[:], c0, cfi[:, d_, j:j + 1], None, ALU.mult, None, [ctw, cfi], [ctmp])
                            stt(c1, c1, cfr[:, d_, j:j + 1], ctmp[:], ALU.mult, ALU.subtract, [ctw, cfr, ctmp], [ctw])
                            cp(c0, ctmpb[:], [ctmpb], [ctw])
                    sv = sbuf(p2, "sv", [128, 513], F32)
                    dma(sv[:], svals_d[:, :], [], [sv], q="sync")
                    ones5 = sbuf(p2, "ones5", [128, 512], F32)
                    V(lambda e: e.memset(ones5[:], 1.0), [], [ones5])
                    a1s = [sbuf(p2, "a1%d" % i, [128, 513], F32) for i in range(1)] * 2
                    a2s = [sbuf(p2, "a2%d" % i, [128, 513], F32) for i in range(1)] * 2
                    ais = [sbuf(p2, "ai%d" % i, [128, 513], I32) for i in range(1)] * 2
                    sinTs = [sbuf(p2, "sinT%d" % i, [128, 513], F32) for i in range(2)]
                    cosTs = [sbuf(p2, "cosT%d" % i, [128, 513], F32) for i in range(2)]
                    rfills = [sbuf(p2, "rfill%d" % i, [128, 512], F32) for i in range(2)]
                    mts4 = [[sbuf(p2, "mt%d%d" % (i, q), [128, 512], F32) for q in range(4)] for i in range(2)]
                    ris = [sbuf(p2, "ris%d" % i, [128, 2, 2], F32) for i in range(2)]
                    g1s = [sbuf(p2, "g1_%d" % i, [128, 512], F32) for i in range(2)]
                    g2s = [sbuf(p2, "g2_%d" % i, [128, 512], F32) for i in range(2)]
                    bps = [[sbuf(p2, "bp%d%d" % (i, q), [128, 512], F32) for q in range(2)] for i in range(2)]
                    kks = [[sbuf(p2, "kk%d%d" % (i, q), [128, 2, 512], F32) for q in range(2)] for i in range(2)]
                    inis = [sbuf(p2, "ini%d" % i, [128, 4], F32) for i in range(2)]
                    hh = [sbuf(p2, "hh%d" % d_, [128, 2, OWN], BF16) for d_ in range(2)]
                    pbu = [psum(p2, "pbu%d" % i, [128, 512], F32) for i in range(4)]
                    py = [psum(p2, "py%d" % i, [128, 512], F32) for i in range(4)]
                    seqs = [
                        [(L, 256, False, None)] + [(i * 512, 512, False, i * 512) for i in range(4)],
                        [(L, 256, True, None)] + [(i * 512, 512, True, (i * 512 if i < 4 else None))
                                                  for i in range(7, -1, -1)],
                    ]
                    for j in range(16):
                        kc = j // 4
                        for d_ in range(2):
                            rfill = rfills[d_]
                            sinT, cosT = sinTs[d_], cosTs[d_]
                            a1, a2, ai = a1s[d_], a2s[d_], ais[d_]
                            ts(a1[:], sv[:], thp[:, d_, j:j + 1], None, ALU.mult, None, [sv, thp], [a1])
                            cp(ai[:], a1[:], [a1], [ai])
                            cp(a2[:], ai[:], [ai], [a2])
                            tt(a2[:], a1[:], a2[:], ALU.subtract, [a1, a2], [a2])
                            act(sinT[:], a2[:], AF.Sin, [a2], [sinT], scale=6.28318)
                            ts(a1[:], a1[:], 0.25, None, ALU.add, None, [a1], [a1])
                            cp(ai[:], a1[:], [a1], [ai])
                            cp(a2[:], ai[:], [ai], [a2])
                            tt(a2[:], a1[:], a2[:], ALU.subtract, [a1, a2], [a2])
                            act(cosT[:], a2[:], AF.Sin, [a2], [cosT], scale=6.28318)
                            ts(rfill[:], ones5[:], rr[:, d_, j:j + 1], None, ALU.mult, None, [ones5, rr], [rfill])
                            for ni, nn_ in enumerate((256, 512)):
                                ts(ris[d_][:, ni, 0:1], sinT[:, nn_:nn_ + 1], -1.0, None, ALU.mult, None, [sinT], [ris[d_]])
                                cp(ris[d_][:, ni, 1:2], sinT[:, nn_:nn_ + 1], [sinT], [ris[d_]])
                        for step in range(9):
                            for d_ in range(2):
                                if step >= len(seqs[d_]):
                                    continue
                                (m0, n, rev, own) = seqs[d_][step]
                                sinT, cosT = sinTs[d_], cosTs[d_]
                                rfill, ini = rfills[d_], inis[d_]
                                g1_, g2_ = g1s[d_], g2s[d_]
                                pr, pi = pbu[2 * d_], pbu[2 * d_ + 1]
                                sl_ = d_
                                kk_ = kks[sl_][step % 2]
                                kr, ki = kk_[:, 0, :], kk_[:, 1, :]
                                ma, mb, mc, md = mts4[sl_]
                                bpr, bpi = bps[sl_]
                                MM(pr[:, 0:n], braw[:, 0, d_, j, :], dT[:, kc, m0:m0 + n], True, True, [braw, dT], [pr])
                                MM(pi[:, 0:n], braw[:, 1, d_, j, :], dT[:, kc, m0:m0 + n], True, True, [braw, dT], [pi])
                                ur_ = pr[:, 0:n][:, ::-1] if rev else pr[:, 0:n]
                                ui_ = pi[:, 0:n][:, ::-1] if rev else pi[:, 0:n]
                                tt(ma[:, 0:n], ur_, cosT[:, 0:n], ALU.mult, [pr, cosT], [ma])
                                tt(mb[:, 0:n], ui_, sinT[:, 0:n], ALU.mult, [pi, sinT], [mb])
                                tt(mc[:, 0:n], ui_, cosT[:, 0:n], ALU.mult, [pi, cosT], [mc])
                                tt(md[:, 0:n], ur_, sinT[:, 0:n], ALU.mult, [pr, sinT], [md])
                                tt(bpr[:, 0:n], ma[:, 0:n], mb[:, 0:n], ALU.add, [ma, mb], [bpr], eng="gpsimd")
                                tt(bpi[:, 0:n], mc[:, 0:n], md[:, 0:n], ALU.subtract, [mc, md], [bpi], eng="gpsimd")
                                i_r = 0.0 if step == 0 else ini[:, 0:1]
                                i_i = 0.0 if step == 0 else ini[:, 1:2]
                                V(lambda e, o=kr[:, 0:n], a=rfill[:, 0:n], b=bpr[:, 0:n], iv=i_r:
                                  e.tensor_tensor_scan(out=o, data0=a, data1=b, initial=iv, op0=ALU.mult, op1=ALU.add),
                                  [rfill, bpr, ini], [kk_])
                                V(lambda e, o=ki[:, 0:n], a=rfill[:, 0:n], b=bpi[:, 0:n], iv=i_i:
                                  e.tensor_tensor_scan(out=o, data0=a, data1=b, initial=iv, op0=ALU.mult, op1=ALU.add),
                                  [rfill, bpi, ini], [kk_])
                                ni = 0 if n == 256 else 1
                                tt(ini[:, 2:4], kk_[:, ::-1, n - 1], ris[d_][:, ni, :], ALU.mult, [kk_, ris[d_]], [ini])
                                stt(ini[:, 0:2], kk_[:, :, n - 1], cosT[:, n:n + 1], ini[:, 2:4], ALU.mult, ALU.add,
                                    [kk_, cosT, ini], [ini])
                                if own is not None:
                                    h_ = hh[d_]
                                    o_r = h_[:, 0, own:own + n]
                                    o_i = h_[:, 1, own:own + n]
                                    if rev:
                                        o_r = o_r[:, ::-1]
                                        o_i = o_i[:, ::-1]
                                    tt(g1_[:, 0:n], cosT[:, 0:n], kr[:, 0:n], ALU.mult, [cosT, kk_], [g1_], eng="gpsimd")
                                    tt(g2_[:, 0:n], sinT[:, 0:n], ki[:, 0:n], ALU.mult, [sinT, kk_], [g2_], eng="gpsimd")
                                    tt(o_r, g1_[:, 0:n], g2_[:, 0:n], ALU.subtract, [g1_, g2_], [h_], eng="gpsimd")
                                    tt(g1_[:, 0:n], cosT[:, 0:n], ki[:, 0:n], ALU.mult, [cosT, kk_], [g1_], eng="gpsimd")
                                    tt(g2_[:, 0:n], sinT[:, 0:n], kr[:, 0:n], ALU.mult, [sinT, kk_], [g2_], eng="gpsimd")
                                    tt(o_i, g1_[:, 0:n], g2_[:, 0:n], ALU.add, [g1_, g2_], [h_], eng="gpsimd")
                        for tc in range(4):
                            for d_ in range(2):
                                for ri in range(2):
                                    MM(py[tc][:], ctw2[:, ri, d_, j, :], hh[d_][:, ri, tc * 512:(tc + 1) * 512],
                                       (j % 4 == 0 and d_ == 0 and ri == 0), (j % 4 == 3 and d_ == 1 and ri == 1),
                                       [ctw2, hh[d_]], [py[tc]])
                        if j % 4 == 3:
                            for tc in range(4):
                                cp(ysb[:, kc, tc * 512:(tc + 1) * 512], py[tc][:], [py[tc]], [ysb])
                    S.flush()
                with ExitStack() as p2:
                    pbu = [psum(p2, "pbu%d" % i, [128, 512], F32) for i in range(4)]
                    dcol = sbuf(p2, "dcol", [128, 4], F32)
                    dma(dcol[:], s5d[:, :], [], [dcol], q="sync")
                    wg = sbuf(p2, "wg", [128, 4, 4, 128], BF16)
                    for mt in range(4):
                        dma(wg[:, mt, :, :], w_glu[mt, :, :, :], [], [wg], q="gpsimd")
                    yb = sbuf(p2, "yb", [128, 4, OWN], BF16)
                    g1 = sbuf(p2, "g1", [128, OWN], F32)
                    g2 = sbuf(p2, "g2", [128, OWN], F32)
                    for kc in range(4):
                        stt(ysb[:, kc, :], dT[:, kc, 0:OWN], dcol[:, kc:kc + 1], ysb[:, kc, :], ALU.mult, ALU.add,
                            [dT, dcol, ysb], [ysb])
                        tt(g1[:], ysb[:, kc, :], ysb[:, kc, :], ALU.mult, [ysb], [g1])
                        ts(g1[:], g1[:], 0.044715, 1.0, ALU.mult, ALU.add, [g1], [g1])
                        tt(g1[:], g1[:], ysb[:, kc, :], ALU.mult, [g1, ysb], [g1])
                        act(g2[:], g1[:], AF.Sigmoid, [g1], [g2], scale=1.5957691216057308)
                        tt(yb[:, kc, :], ysb[:, kc, :], g2[:], ALU.mult, [ysb, g2], [yb])
                    mo = [sbuf(p2, "mo%d" % i, [128, 512], BF16) for i in range(2)]
                    km = 0
                    for mt in range(4):
                        for tc in range(4):
                            ps = pbu[km % 4]
                            mo_ = mo[km % 2]
                            km += 1
                            for k in range(4):
                                MM(ps[:], wg[:, mt, k, :], yb[:, k, tc * 512:(tc + 1) * 512], k == 0, k == 3,
                                   [wg, yb], [ps])
                            act(g1[:, 0:512], ps[:], AF.Sigmoid, [ps], [g1])
                            tt(g1[:, 0:512], g1[:, 0:512], yb[:, mt, tc * 512:(tc + 1) * 512], ALU.mult, [g1, yb], [g1])
                            tt(mo_[:], g1[:, 0:512], sdg[:, mt, tc * 512:(tc + 1) * 512], ALU.mult, [g1, sdg], [mo_])
                            r0 = 1536 + mt * 128
                            dma(mixT_d[r0:r0 + 128, tc * 512:(tc + 1) * 512], mo_[:], [mo_], [R_mix], q="gpsimd")
                    S.flush()

        phase_p1(0, xin, T(None))
        phase_conv()
        phase_fourier()
        if debug == "l0":
            phase_out(0, w_out0, xin, T(None), 34, h1_d, R_h1, True)
            V(lambda e: e.memset(onesf[:], 1.0), [], [onesf])
            S.flush(final=True)
            return nc
        phase_out(0, w_out0, xin, T(None), 34, h1_d, R_h1, False)
        phase_p1(1, h1_d, R_h1)
        phase_na()
        phase_s5()
        phase_out(1, w_out1, h1_d, R_h1, 16, out_d, T(None), True)
        V(lambda e: e.memset(onesf[:], 1.0), [], [onesf])
        S.flush(final=True)
    return nc


_CONST_CACHE = {}


def _consts(inp):
    if "dft" not in _CONST_CACHE:
        _CONST_CACHE["dft"] = [_dft_consts(0), _dft_consts(1)]
    rpb = inp["na_rpb"][0]
    return {"dft": _CONST_CACHE["dft"], "na_bias": [_na_tables(rpb, 0), _na_tables(rpb, 1)]}


def kernel(**inputs):
    inp = {k: np.asarray(v) for k, v in inputs.items()}
    consts = _consts(inp)
    nc = build()
    in_maps = [prep_core(inp, core, consts) for core in range(8)]
    res = run_bass_kernel_spmd(nc, in_maps, core_ids=list(range(8)))
    out = np.empty((4, L, D), np.float32)
    for core in range(8):
        b, par = core // 2, core % 2
        o = res.results[core]["out"]
        if par:
            out[b, L - OWN:] = o[::-1]
        else:
            out[b, :OWN] = o
    return out
```

```python
import math
from contextlib import ExitStack
import numpy as np
import ml_dtypes
import concourse.bass as bass
import concourse.mybir as mybir
from concourse.bass_utils import run_bass_kernel_spmd

F32 = mybir.dt.float32
BF16 = mybir.dt.bfloat16
I32 = mybir.dt.int32
AF = mybir.ActivationFunctionType
ALU = mybir.AluOpType

ENGS = ("tensor", "vector", "scalar", "gpsimd", "sync")
NDMA = 24
D = 2048
L = 4096
LC = 256
TOK = L + LC
OWN = 2048
NKV = 2304
EPS = 1e-6
CHUNKS = [(i * 512, 512) for i in range(8)] + [(4096, 256)]
BF = ml_dtypes.bfloat16


class Res:
    __slots__ = ("w", "r")

    def __init__(self):
        self.w = None
        self.r = {}


class T:
    def __init__(self, t):
        self.t = t
        self.res = Res()

    def __getitem__(self, idx):
        return self.t[idx]


class Sched:
    def __init__(self, nc, sems):
        self.nc = nc
        self.sems = sems
        self.q = {e: [] for e in ENGS}
        self.cnt = {e: 0 for e in ENGS}
        self.seen = {e: {} for e in ENGS}
        self.dma_rr = 0
        self.dma_rrq = {"sync": 0, "gpsimd": 0, "scalar": 0}
        self.dma_cnt = [0] * NDMA
        self.out_toks = []
        self.dq = 0
        self.barrier = {}

    def _deps(self, eng, reads, writes, pe_chain):
        need = {}

        def add(tok):
            if tok is None:
                return
            s, v = tok
            if pe_chain and s == "tensor" and eng == "tensor":
                return
            if need.get(s, 0) < v:
                need[s] = v

        for r in reads:
            add(r.res.w)
        for w in writes:
            add(w.res.w)
            for s, v in w.res.r.items():
                add((s, v))
        for s, v in self.barrier.items():
            if need.get(s, 0) < v:
                need[s] = v
        waits = []
        for s, v in need.items():
            if self.seen[eng].get(s, 0) < v:
                waits.append((s, v))
                self.seen[eng][s] = v
        return waits

    def _commit(self, tok, reads, writes):
        s, v = tok
        for r in reads:
            if r.res.r.get(s, 0) < v:
                r.res.r[s] = v
        for w in writes:
            w.res.w = tok
            w.res.r = {}

    def op(self, eng, emit, reads=(), writes=(), pe_chain=False):
        waits = self._deps(eng, reads, writes, pe_chain)
        self.cnt[eng] += 1
        tok = (eng, self.cnt[eng])
        self.q[eng].append((waits, emit, (eng, 1)))
        self._commit(tok, reads, writes)
        return tok

    def dma(self, emit, reads=(), writes=(), q=None, is_output=False):
        if q is None:
            q = ("sync", "gpsimd")[self.dq % 2]
            self.dq += 1
        half = NDMA // 2
        base = 0 if q == "sync" else half
        slot = base + self.dma_rrq[q] % half
        self.dma_rrq[q] += 1
        semkey = ("dma", slot)
        waits = self._deps(q, reads, writes, False)
        prev = self.dma_cnt[slot]
        if prev and self.seen[q].get(semkey, 0) < prev:
            waits.append((semkey, prev))
            self.seen[q][semkey] = prev
        self.dma_cnt[slot] = prev + 16
        tok = (semkey, prev + 16)
        self.q[q].append((waits, emit, (semkey, 16)))
        self._commit(tok, reads, writes)
        if is_output:
            self.out_toks.append(tok)
        return tok

    def flush(self, final=False):
        if final:
            fin = {}
            for s, v in self.out_toks:
                fin[s] = max(fin.get(s, 0), v)
            for e in ENGS:
                if self.cnt[e]:
                    fin[e] = max(fin.get(e, 0), self.cnt[e])
            for s in range(NDMA):
                if self.dma_cnt[s]:
                    fin[("dma", s)] = self.dma_cnt[s]
            self.q["sync"].append((list(fin.items()), None, None))
        qs = self.q
        sems = self.sems

        def run(name):
            def body(e):
                for waits, emit, inc in qs[name]:
                    for s, v in waits:
                        e.wait_ge(sems[s], v)
                    if emit is not None:
                        emit(e).then_inc(sems[inc[0]], inc[1])
            return body

        with self.nc.Block() as block:
            block.tensor(run("tensor"))
            block.vector(run("vector"))
            block.scalar(run("scalar"))
            block.gpsimd(run("gpsimd"))
            block.sync(run("sync"))
        self.q = {e: [] for e in ENGS}
        self.barrier = {e: self.cnt[e] for e in ENGS if self.cnt[e]}
        for s_ in range(NDMA):
            if self.dma_cnt[s_]:
                self.barrier[("dma", s_)] = self.dma_cnt[s_]


def _cols(v, n):
    return np.ascontiguousarray(v.reshape(n, 128).T)


def _mt(W):
    K, F = W.shape
    return np.ascontiguousarray(W.reshape(K // 128, 128, F // 128, 128).transpose(2, 1, 0, 3))


def _kt(W):
    K, F = W.shape
    return np.ascontiguousarray(W.reshape(K // 128, 128, F).transpose(1, 0, 2))


def _na_tables(rpb, par):
    out = np.full((3, 12, 640, 128), -30000.0, np.float32)
    for cls, j in enumerate((0, 1, 2)):
        ws = min(max(2 * j - 4, 0), 26)
        qi = np.arange(128)
        qr_o = 2 * j + qi // 64
        qc_o = qi % 64
        ki = np.arange(640)
        kr_o = ws + ki // 64
        kc_o = ki % 64
        if par:
            qr, qc, kr, kc = 63 - qr_o, 63 - qc_o, 63 - kr_o, 63 - kc_o
        else:
            qr, qc, kr, kc = qr_o, qc_o, kr_o, kc_o
        rs = np.clip(qr - 4, 0, 56)
        cs = np.clip(qc - 8, 0, 48)
        ok = ((kr[:, None] >= rs[None]) & (kr[:, None] < rs[None] + 8) &
              (kc[:, None] >= cs[None]) & (kc[:, None] < cs[None] + 16))
        dr = np.clip(kr[:, None] - qr[None] + 7, 0, 14)
        dc = np.clip(kc[:, None] - qc[None] + 15, 0, 30)
        g = rpb[:, dr, dc]
        out[cls] = np.where(ok[None], g, np.float32(-30000.0))
    return np.ascontiguousarray(out.reshape(3, 12, 5, 128, 128).transpose(0, 1, 3, 2, 4))


def _dft_consts(par):
    n = np.arange(256)
    a = 2.0 * np.pi * ((n[:, None] * n[None, :]) % 256) / 256.0
    c256, s256 = np.cos(a), np.sin(a)
    cs256 = np.concatenate([c256, s256], axis=1)
    cs256 = cs256.reshape(2, 128, 512).transpose(1, 0, 2)
    if par:
        cc, sc = c256[::-1, ::-1], s256[::-1, ::-1]
    else:
        cc, sc = c256, s256
    cctx = np.concatenate([cc, -sc], axis=1).reshape(2, 128, 512).transpose(1, 0, 2)
    a = np.arange(64)
    if par:
        e1 = (a[:, None] * (a[None, :] + 1)) % 64
        e3 = ((a[:, None] + 1) * a[None, :]) % 64
        e2 = ((a[:, None] + 1) * (a[None, :] + 1)) % 4096
    else:
        e1 = (a[:, None] * a[None, :]) % 64
        e3 = e1
        e2 = (a[:, None] * a[None, :]) % 4096

    def bd(m):
        z = np.zeros((128, 128))
        z[:64, :64] = m
        z[64:, 64:] = m
        return z

    th1 = 2.0 * np.pi * e1 / 64.0
    th3 = 2.0 * np.pi * e3 / 64.0
    th2 = 2.0 * np.pi * e2 / 4096.0
    w1 = np.stack([bd(np.cos(th1)), bd(-np.sin(th1)), bd(-np.cos(th1))], axis=1)
    w3 = np.stack([bd(np.cos(th3)), bd(np.sin(th3))], axis=1)
    tw = np.stack([np.cos(th2).reshape(-1), np.sin(th2).reshape(-1)], axis=0)
    tw = np.broadcast_to(tw[None], (128, 2, 4096))
    return (np.ascontiguousarray(cs256).astype(BF), np.ascontiguousarray(cctx).astype(BF),
            np.ascontiguousarray(w1).astype(BF), np.ascontiguousarray(w3).astype(BF),
            np.ascontiguousarray(tw).astype(BF))


def _s5_layout(inp, par):
    dirs = (1, 0) if par else (0, 1)
    G, P, H = 32, 64, 16
    out = {}
    a_re = inp["s5_a_re"][0][list(dirs)]
    a_im = inp["s5_a_im"][0][list(dirs)]
    ldt = inp["s5_log_dt"][0][list(dirs)]

    def st(v):
        return np.ascontiguousarray(v.reshape(2, 16, 2, 64).transpose(2, 3, 0, 1).reshape(128, 2, 16))

    out["s5are"] = st(a_re)
    out["s5aim"] = st(a_im)
    out["s5ldt"] = st(np.broadcast_to(ldt[:, :, None], (2, G, P)))
    b_re = inp["s5_b_re"][0][list(dirs)]
    b_im = inp["s5_b_im"][0][list(dirs)]
    c_re = inp["s5_c_re"][0][list(dirs)]
    c_im = inp["s5_c_im"][0][list(dirs)]
    braw = np.zeros((2, 2, 16, 128, 128), np.float32)
    ct = np.zeros((2, 2, 16, 128, 128), np.float32)
    for g in range(G):
        j = g // 2
        r0 = (g % 8) * 16
        s0 = (g % 2) * 64
        for ri, (bb, cc) in enumerate(((b_re, c_re), (b_im, c_im))):
            braw[ri, :, j, r0:r0 + 16, s0:s0 + 64] = bb[:, g].transpose(0, 2, 1)
            ct[ri, :, j, s0:s0 + 64, r0:r0 + 16] = cc[:, g].transpose(0, 2, 1)
    out["s5braw"] = np.ascontiguousarray(braw.transpose(3, 0, 1, 2, 4))
    out["s5ct"] = np.ascontiguousarray(ct.transpose(3, 0, 1, 2, 4))
    return out


def prep_core(inp, core, consts):
    b, par = core // 2, core % 2
    x = inp["x"][b]
    ctx = inp["ctx"][b]
    if par:
        x = x[::-1]
        ctx = ctx[::-1]
    m = {}
    m["xin"] = np.ascontiguousarray(np.concatenate([x, ctx], axis=0))
    m["cvec"] = np.ascontiguousarray(np.stack([_cols(inp["c"][b], 16), _cols(inp["c_ctx"], 16)], axis=2))
    m["ada_w"] = np.ascontiguousarray(inp["ada_w"].reshape(2, 16, 128, 6144).transpose(0, 2, 1, 3))
    m["ada_b"] = np.stack([_cols(inp["ada_b"][l], 48) for l in range(2)], axis=1)
    m["pre_g"] = np.stack([_cols(inp["pre_g"][l], 16) for l in range(2)], axis=1)
    m["post_g"] = np.stack([_cols(inp["post_g"][l], 16) for l in range(2)], axis=1)
    m["w_in0"] = _mt(inp["ab_w_in"][0])
    m["w_out0"] = _kt(inp["ab_w_out"][0])
    cw = inp["conv_w"][0]
    if par:
        cw = cw[::-1]
    m["conv_w"] = np.ascontiguousarray(cw.T.reshape(8, 128, 31).transpose(1, 0, 2))
    m["conv_v"] = np.ascontiguousarray(np.stack([_cols(inp["conv_b"][0], 8), _cols(inp["conv_ln_g"][0], 8),
                                                 _cols(inp["conv_ln_b"][0], 8)], axis=1))
    fg = inp["fourier_g"][0]
    m["four_g"] = np.ascontiguousarray(fg.reshape(4, 2, 128).transpose(2, 0, 1))
    m["w_in1"] = _mt(inp["cd_w_in"][0])
    m["w_out1"] = _kt(inp["cd_w_out"][0])
    m["na_bias"] = consts["na_bias"][par]
    m["s5d"] = _cols(inp["s5_d"][0], 4)
    m["w_glu"] = _mt(inp["s5_w_glu"][0])
    m.update(_s5_layout(inp, par))
    cs256, cctx, fw1, fw3, ftw = consts["dft"][par]
    m["cs256"], m["cctx"], m["fw1"], m["fw3"], m["ftw"] = cs256, cctx, fw1, fw3, ftw
    m["ident"] = np.eye(128, dtype=np.float32)
    m["svals"] = np.ascontiguousarray(np.broadcast_to(np.arange(513, dtype=np.float32)[None], (128, 513)))
    return m


def build(debug=None):
    nc = bass.Bass("TRN2", target_bir_lowering=False)

    def din(name, shape, dtype=F32):
        return nc.dram_tensor(name, list(shape), dtype, kind="ExternalInput").ap()

    def dscr(name, shape, dtype):
        return nc.dram_tensor(name, list(shape), dtype, kind="Internal").ap()

    xin = din("xin", [TOK, D])
    cvec = din("cvec", [128, 16, 2])
    ada_w = din("ada_w", [2, 128, 16, 6144])
    ada_b = din("ada_b", [128, 2, 48])
    pre_g = din("pre_g", [128, 2, 16])
    post_g = din("post_g", [128, 2, 16])
    w_in0 = din("w_in0", [40, 128, 16, 128])
    w_out0 = din("w_out0", [128, 16, 2048])
    conv_w = din("conv_w", [128, 8, 31])
    conv_v = din("conv_v", [128, 3, 8])
    four_g = din("four_g", [128, 4, 2])
    w_in1 = din("w_in1", [56, 128, 16, 128])
    w_out1 = din("w_out1", [128, 16, 2048])
    na_bias = din("na_bias", [3, 12, 128, 5, 128])
    s5d = din("s5d", [128, 4])
    w_glu = din("w_glu", [4, 128, 4, 128])
    s5are = din("s5are", [128, 2, 16])
    s5aim = din("s5aim", [128, 2, 16])
    s5ldt = din("s5ldt", [128, 2, 16])
    s5braw = din("s5braw", [128, 2, 2, 16, 128])
    s5ct = din("s5ct", [128, 2, 2, 16, 128])
    cs256_d = din("cs256", [128, 2, 512], BF16)
    cctx_d = din("cctx", [128, 2, 512], BF16)
    w1_d = din("fw1", [128, 3, 128], BF16)
    w3_d = din("fw3", [128, 2, 128], BF16)
    twd_d = din("ftw", [128, 2, L], BF16)
    ident_d = din("ident", [128, 128])
    svals_d = din("svals", [128, 513])
    out_d = nc.dram_tensor("out", [OWN, D], F32, kind="ExternalOutput").ap()
    h1_kind = "ExternalOutput" if debug == "l0" else "Internal"
    h1_d = nc.dram_tensor("h1", [TOK, D], F32, kind=h1_kind).ap()
    uT_d = dscr("uT", [D, 4608], BF16)
    mixT_d = dscr("mixT", [D, TOK], BF16)
    convT_d = dscr("convT", [1024, TOK], BF16)

    with ExitStack() as top:
        sems = {}
        for e in ENGS:
            sems[e] = top.enter_context(nc.semaphore("s_" + e))
        for i in range(NDMA):
            sems[("dma", i)] = top.enter_context(nc.semaphore("d%d" % i))
        S = Sched(nc, sems)

        uid = [0]

        def sbuf(es, name, shape, dtype):
            uid[0] += 1
            return T(es.enter_context(nc.sbuf_tensor("%s_%d" % (name, uid[0]), list(shape), dtype)))

        def psum(es, name, shape, dtype):
            uid[0] += 1
            return T(es.enter_context(nc.psum_tensor("%s_%d" % (name, uid[0]), list(shape), dtype)))

        R_uT, R_mix, R_conv, R_h1 = T(None), T(None), T(None), T(None)

        identb = sbuf(top, "identb", [128, 128], BF16)
        identf = sbuf(top, "identf", [128, 128], F32)
        onesb = sbuf(top, "onesb", [128, 128], BF16)
        onesf = sbuf(top, "onesf", [128, 128], F32)
        modT = sbuf(top, "modT", [128, 2, 48, 2], F32)
        gsT = sbuf(top, "gsT", [128, 2, 2, 16], F32)
        shT = sbuf(top, "shT", [128, 2, 2, 16], F32)
        gpT = sbuf(top, "gpT", [128, 2, 2, 16], F32)
        preg = sbuf(top, "preg", [128, 2, 16], F32)
        postg = sbuf(top, "postg", [128, 2, 16], F32)

        def V(fn, reads, writes):
            return S.op("vector", fn, reads, writes)

        def A(fn, reads, writes):
            return S.op("scalar", fn, reads, writes)

        def G(fn, reads, writes):
            return S.op("gpsimd", fn, reads, writes)

        def MM(out, lhsT, rhs, start, stop, reads, writes):
            return S.op("tensor", lambda e: e.matmul(out, lhsT=lhsT, rhs=rhs, start=start, stop=stop),
                        reads, writes, pe_chain=True)

        def act(out, in_, func, reads, writes, **kw):
            return A(lambda e: e.activation(out=out, in_=in_, func=func, **kw), reads, writes)

        def tt(out, in0, in1, op, reads, writes, eng="vector"):
            return S.op(eng, lambda e: e.tensor_tensor(out=out, in0=in0, in1=in1, op=op), reads, writes)

        def ts(out, in0, s1, s2, op0, op1, reads, writes, eng="vector"):
            if op1 is None:
                return S.op(eng, lambda e: e.tensor_scalar(out=out, in0=in0, scalar1=s1, scalar2=None, op0=op0),
                            reads, writes)
            return S.op(eng, lambda e: e.tensor_scalar(out=out, in0=in0, scalar1=s1, scalar2=s2, op0=op0, op1=op1),
                        reads, writes)

        def stt(out, in0, scalar, in1, op0, op1, reads, writes):
            return V(lambda e: e.scalar_tensor_tensor(out=out, in0=in0, scalar=scalar, in1=in1, op0=op0, op1=op1),
                     reads, writes)

        def cp(out, in_, reads, writes, eng="vector"):
            return S.op(eng, lambda e: e.tensor_copy(out=out, in_=in_), reads, writes)

        def dma(out, in_, reads, writes, q=None, is_output=False):
            return S.dma(lambda e: e.dma_start(out=out, in_=in_), reads, writes, q=q, is_output=is_output)

        condT = sbuf(top, "condT", [128, 16, 2], F32)
        condTb = sbuf(top, "condTb", [128, 16, 2], BF16)
        adab = sbuf(top, "adab", [128, 2, 48], F32)

        def ada_block(l, cb, aw, psA):
            w = aw[cb % 2]
            dma(w[:], ada_w[l, :, :, cb * 512:(cb + 1) * 512], [], [w], q="gpsimd")
            for m in range(4):
                j = cb * 4 + m
                for k in range(16):
                    MM(psA[:, j, :], w[:, k, m * 128:(m + 1) * 128], condTb[:, k, :],
                       k == 0, k == 15, [w, condTb], [psA])

        def ada_finalize(l, psA):
            for i in range(2):
                tt(modT[:, l, :, i], psA[:, :, i], adab[:, l, :], ALU.add, [psA, adab], [modT])
            for i in range(2):
                stt(gsT[:, l, i, :], modT[:, l, 16:32, i], 1.0, preg[:, l, :], ALU.add, ALU.mult,
                    [modT, preg], [gsT])
                cp(shT[:, l, i, :], modT[:, l, 0:16, i], [modT], [shT])
                tt(gpT[:, l, i, :], modT[:, l, 32:48, i], postg[:, l, :], ALU.mult, [modT, postg], [gpT])

        with ExitStack() as ph:
            aw = [sbuf(ph, "aw%d" % i, [128, 16, 512], BF16) for i in range(2)]
            psA = psum(ph, "psA", [128, 48, 2], F32)
            dma(identf[:], ident_d[:, :], [], [identf], q="sync")
            dma(identb[:], ident_d[:, :], [], [identb], q="gpsimd")
            dma(condT[:], cvec[:, :, :], [], [condT], q="sync")
            dma(adab[:], ada_b[:, :, :], [], [adab], q="sync")
            dma(preg[:], pre_g[:, :, :], [], [preg], q="sync")
            dma(postg[:], post_g[:, :, :], [], [postg], q="sync")
            V(lambda e: e.memset(onesf[:], 1.0), [], [onesf])
            V(lambda e: e.memset(onesb[:], 1.0), [], [onesb])
            act(condT[:], condT[:], AF.Silu, [condT], [condT])
            cp(condTb[:], condT[:], [condT], [condTb])
            for cb in range(12):
                ada_block(0, cb, aw, psA)
            ada_finalize(0, psA)
            S.flush()

        def phase_p1(l, h_d, R_h):
            with ExitStack() as ph:
                xs = [sbuf(ph, "xs%d" % i, [128, D], F32) for i in range(2)]
                xn = [sbuf(ph, "xn%d" % i, [128, D], BF16) for i in range(2)]
                junk = sbuf(ph, "junk", [128, D], BF16)
                st = [sbuf(ph, "ust%d" % i, [128, 16, 512], BF16) for i in range(2)]
                ss = sbuf(ph, "ss", [128, 34], F32)
                rs = sbuf(ph, "rs", [128, 34], F32)
                pT = [psum(ph, "pT%d" % i, [128, 16, 128], BF16) for i in range(2)]
                if l == 0:
                    aw1 = [sbuf(ph, "aw1%d" % i, [128, 16, 512], BF16) for i in range(2)]
                    psA1 = psum(ph, "psA1", [128, 48, 2], F32)
                for tix in range(34):
                    if l == 0 and tix % 2 == 1 and tix // 2 < 12:
                        ada_block(1, tix // 2, aw1, psA1)
                    i = 0 if tix < 32 else 1
                    x_ = xs[tix % 2]
                    xn_ = xn[tix % 2]
                    p_ = pT[tix % 2]
                    st_ = st[(tix // 4) % 2]
                    dma(x_[:], h_d[tix * 128:(tix + 1) * 128, :], [R_h], [x_])
                    act(junk[:], x_[:], AF.Square, [x_], [junk, ss], accum_out=ss[:, tix:tix + 1])
                    ts(rs[:, tix:tix + 1], ss[:, tix:tix + 1], 1.0 / D, EPS, ALU.mult, ALU.add, [ss], [rs])
                    act(rs[:, tix:tix + 1], rs[:, tix:tix + 1], AF.Sqrt, [rs], [rs])
                    V(lambda e, a=rs[:, tix:tix + 1]: e.reciprocal(out=a, in_=a), [rs], [rs])
                    act(xn_[:], x_[:], AF.Copy, [x_, rs], [xn_], scale=rs[:, tix:tix + 1])
                    for c in range(16):
                        S.op("tensor", lambda e, o=p_[:, c, :], a=xn_[:, c * 128:(c + 1) * 128]:
                             e.transpose(out=o, in_=a, identity=identb[:]), [xn_, identb], [p_], pe_chain=True)
                    q4 = tix % 4
                    for c in range(16):
                        ts(st_[:, c, q4 * 128:(q4 + 1) * 128], p_[:, c, :], gsT[:, l, i, c:c + 1],
                           shT[:, l, i, c:c + 1], ALU.mult, ALU.add, [p_, gsT, shT], [st_])
                    if q4 == 3 or tix == 33:
                        t0 = (tix // 4) * 512
                        n = (q4 + 1) * 128
                        dma(uT_d.rearrange("(c p) t -> p c t", p=128)[:, :, t0:t0 + n], st_[:, :, 0:n], [st_], [R_uT])
                if l == 0:
                    ada_finalize(1, psA1)
                S.flush()

        def load_w(wsb, wsrc, mtiles):
            for i, mt in enumerate(mtiles):
                dma(wsb[:, i, :, :], wsrc[mt, :, :, :], [], [wsb], q="gpsimd")

        rot = {"ps": 0, "ut": 0}

        def proj_fm(ph, wsrc, mtiles, chunks, epilogue, tag, ps_tiles, ut_tiles, wsb=None):
            nm = len(mtiles)
            if wsb is None:
                wsb = sbuf(ph, "w_" + tag, [128, nm, 16, 128], BF16)
                load_w(wsb, wsrc, mtiles)
            for ci, (t0, n) in enumerate(chunks):
                ut = ut_tiles[rot["ut"] % len(ut_tiles)]
                rot["ut"] += 1
                dma(ut[:, :, 0:n], uT_d.rearrange("(c p) t -> p c t", p=128)[:, :, t0:t0 + n], [R_uT], [ut], q="sync")
                for i in range(nm):
                    ps = ps_tiles[rot["ps"] % len(ps_tiles)]
                    rot["ps"] += 1
                    for k in range(16):
                        MM(ps[:, 0:n], wsb[:, i, k, :], ut[:, k, 0:n], k == 0, k == 15, [wsb, ut], [ps])
                    epilogue(i, ci, t0, n, ps)

        def phase_conv():
            with ExitStack() as ph:
                cw = sbuf(ph, "cw", [128, 8, 31], F32)
                cv = sbuf(ph, "cv", [128, 3, 8], F32)
                s1 = sbuf(ph, "s1", [128, TOK], F32)
                s2 = sbuf(ph, "s2", [128, TOK], F32)
                dma(cw[:], conv_w[:, :, :], [], [cw], q="sync")
                dma(cv[:], conv_v[:, :, :], [], [cv], q="sync")
                with ExitStack() as p1:
                    ut_tiles = [sbuf(p1, "ut%d" % i, [128, 16, 512], BF16) for i in range(2)]
                    ps_tiles = [psum(p1, "ps%d" % i, [128, 512], F32) for i in range(4)]
                    pst = [psum(p1, "pst%d" % i, [128, 512], F32) for i in range(2)]
                    apad = [sbuf(p1, "apad%d" % i, [128, 4400], BF16) for i in range(4)]
                    dgk = [sbuf(p1, "dgk%d" % i, [128, 31, 128], BF16) for i in range(2)]
                    pcv = [psum(p1, "pcv%d" % i, [128, 512], F32) for i in range(2)]
                    cvbs = [sbuf(p1, "cvb%d" % i, [128, TOK], BF16) for i in range(2)]
                    sqbs = [sbuf(p1, "sqb%d" % i, [128, TOK], BF16) for i in range(2)]
                    wA8 = sbuf(p1, "wA8", [128, 8, 16, 128], BF16)
                    sig = [sbuf(p1, "sig%d" % i, [128, 512], BF16) for i in range(2)]
                    for a_ in apad:
                        V(lambda e, a_=a_: e.memset(a_[:], 0.0), [], [a_])
                    for half in range(2):
                        cs = [4 * half + q for q in range(4)]
                        mts = []
                        for c in cs:
                            mts += [8 + c, c]
                        load_w(wA8, w_in0, mts)

                        def epi(i, ci, t0, n, ps):
                            sg = sig[ci % 2]
                            ap_ = apad[i // 2]
                            if i % 2 == 0:
                                act(sg[:, 0:n], ps[:, 0:n], AF.Sigmoid, [ps], [sg])
                            else:
                                off = 15 + t0 if t0 < L else 4126
                                tt(ap_[:, off:off + n], ps[:, 0:n], sg[:, 0:n], ALU.mult, [ps, sg], [ap_])

                        proj_fm(p1, w_in0, mts, CHUNKS, epi, "a1", ps_tiles, ut_tiles, wsb=wA8)
                        for q, c in enumerate(cs):
                            ap_ = apad[q]
                            dg_ = dgk[c % 2]
                            cvb, sqb = cvbs[c % 2], sqbs[c % 2]
                            tt(dg_[:], identb[:].unsqueeze(1).to_broadcast([128, 31, 128]),
                               cw[:, c, :].unsqueeze(2).to_broadcast([128, 31, 128]), ALU.mult, [identb, cw], [dg_])
                            for ci, (t0, n) in enumerate(CHUNKS):
                                i0 = t0 if t0 < L else 4111
                                pc_ = pcv[ci % 2]
                                for k in range(31):
                                    MM(pc_[:, 0:n], dg_[:, k, :], ap_[:, i0 + k:i0 + k + n], k == 0, k == 30,
                                       [dg_, ap_], [pc_])
                                act(cvb[:, t0:t0 + n], pc_[:, 0:n], AF.Identity, [pc_, cv], [cvb], bias=cv[:, 0, c:c + 1])
                                act(sqb[:, t0:t0 + n], pc_[:, 0:n], AF.Square, [pc_, cv], [sqb], bias=cv[:, 0, c:c + 1])
                            for ci, (t0, n) in enumerate(CHUNKS):
                                MM(pst[0][:, 0:n], onesb[:], cvb[:, t0:t0 + n], True, True, [onesb, cvb], [pst[0]])
                                MM(pst[1][:, 0:n], onesb[:], sqb[:, t0:t0 + n], True, True, [onesb, sqb], [pst[1]])
                                if c == 0:
                                    cp(s1[:, t0:t0 + n], pst[0][:, 0:n], [pst[0]], [s1])
                                    cp(s2[:, t0:t0 + n], pst[1][:, 0:n], [pst[1]], [s2])
                                else:
                                    tt(s1[:, t0:t0 + n], pst[0][:, 0:n], s1[:, t0:t0 + n], ALU.add, [pst[0], s1], [s1])
                                    tt(s2[:, t0:t0 + n], pst[1][:, 0:n], s2[:, t0:t0 + n], ALU.add, [pst[1], s2], [s2])
                            dma(convT_d[c * 128:(c + 1) * 128, :], cvb[:], [cvb], [R_conv], q="sync")
                    msq = sbuf(p1, "msq", [128, 512], F32)
                    for (t0, n) in CHUNKS:
                        ts(s1[:, t0:t0 + n], s1[:, t0:t0 + n], 1.0 / 1024, None, ALU.mult, None, [s1], [s1])
                        tt(msq[:, 0:n], s1[:, t0:t0 + n], s1[:, t0:t0 + n], ALU.mult, [s1], [msq])
                        stt(s2[:, t0:t0 + n], s2[:, t0:t0 + n], 1.0 / 1024, msq[:, 0:n], ALU.mult, ALU.subtract,
                            [s2, msq], [s2])
                    ts(s2[:], s2[:], EPS, None, ALU.add, None, [s2], [s2])
                    act(s2[:], s2[:], AF.Sqrt, [s2], [s2])
                    V(lambda e: e.reciprocal(out=s2[:], in_=s2[:]), [s2], [s2])
                    S.flush()
                with ExitStack() as p2:
                    ut_tiles = [sbuf(p2, "ut%d" % i, [128, 16, 512], BF16) for i in range(2)]
                    ps_tiles = [psum(p2, "ps%d" % i, [128, 512], F32) for i in range(4)]
                    cvl = [sbuf(p2, "cvl%d" % i, [128, 8, 512], BF16) for i in range(2)]
                    sgt = [sbuf(p2, "sgt%d" % i, [128, 512], F32) for i in range(2)]
                    t1 = [sbuf(p2, "t1%d" % i, [128, 512], F32) for i in range(2)]
                    mo = [sbuf(p2, "mo%d" % i, [128, 512], BF16) for i in range(2)]
                    wG8 = sbuf(p2, "wG8", [128, 8, 16, 128], BF16)
                    load_w(wG8, w_in0, [16 + c for c in range(8)])
                    kk2 = [0]

                    def epi(i, ci, t0, n, ps):
                        c = i
                        cvl_ = cvl[ci % 2]
                        if c == 0:
                            dma(cvl_[:, :, 0:n], convT_d.rearrange("(c p) t -> p c t", p=128)[:, :, t0:t0 + n],
                                [R_conv], [cvl_], q="sync")
                        k2 = kk2[0]
                        kk2[0] += 1
                        sg, t1_, mo_ = sgt[k2 % 2], t1[k2 % 2], mo[k2 % 2]
                        act(sg[:, 0:n], ps[:, 0:n], AF.Silu, [ps], [sg])
                        tt(t1_[:, 0:n], cvl_[:, c, 0:n], s1[:, t0:t0 + n], ALU.subtract, [cvl_, s1], [t1_])
                        tt(t1_[:, 0:n], t1_[:, 0:n], s2[:, t0:t0 + n], ALU.mult, [t1_, s2], [t1_])
                        act(t1_[:, 0:n], t1_[:, 0:n], AF.Silu, [t1_, cv], [t1_],
                            scale=cv[:, 1, c:c + 1], bias=cv[:, 2, c:c + 1])
                        tt(mo_[:, 0:n], t1_[:, 0:n], sg[:, 0:n], ALU.mult, [t1_, sg], [mo_])
                        dma(mixT_d[c * 128:(c + 1) * 128, t0:t0 + n], mo_[:, 0:n], [mo_], [R_mix], q="gpsimd")

                    proj_fm(p2, w_in0, [16 + c for c in range(8)], CHUNKS, epi, "a2", ps_tiles, ut_tiles, wsb=wG8)
                    S.flush()

        def phase_fourier():
            with ExitStack() as ph:
                fg = sbuf(ph, "fg", [128, 4, 2], F32)
                cs256 = sbuf(ph, "cs256", [128, 2, 512], BF16)
                cctx = sbuf(ph, "cctx", [128, 2, 512], BF16)
                bnT = sbuf(ph, "bnT", [128, 2, TOK], BF16)
                sgT = sbuf(ph, "sgT", [128, 2, TOK], BF16)
                dma(fg[:], four_g[:, :, :], [], [fg], q="sync")
                dma(cs256[:], cs256_d[:, :, :], [], [cs256], q="sync")
                dma(cctx[:], cctx_d[:, :, :], [], [cctx], q="sync")
                w1t = sbuf(ph, "w1t", [128, 3, 128], BF16)
                w3t = sbuf(ph, "w3t", [128, 2, 128], BF16)
                twd = sbuf(ph, "twd", [128, 2, L], BF16)
                dma(w1t[:], w1_d[:, :, :], [], [w1t], q="sync")
                dma(w3t[:], w3_d[:, :, :], [], [w3t], q="sync")
                dma(twd[:], twd_d[:, :, :], [], [twd], q="sync")
                wB = [sbuf(ph, "wB%d" % i, [128, 4, 16, 128], BF16) for i in range(2)]
                fmt = lambda g_: [24 + 2 * g_, 25 + 2 * g_, 32 + 2 * g_, 33 + 2 * g_]
                load_w(wB[0], w_in0, fmt(0))
                for g in range(4):
                    with ExitStack() as p1:
                        ut_tiles = [sbuf(p1, "ut%d" % i, [128, 16, 512], BF16) for i in range(2)]
                        ps_tiles = [psum(p1, "ps%d" % i, [128, 512], F32) for i in range(6)]
                        pss = [psum(p1, "pss%d" % i, [128, 512], F32) for i in range(2)]
                        sq = [sbuf(p1, "sq%d" % i, [128, 512], BF16) for i in range(4)]
                        rst = [sbuf(p1, "rst%d" % i, [128, 512], F32) for i in range(2)]
                        held = {}

                        def epi(i, ci, t0, n, ps, g=g):
                            if i >= 2:
                                act(sgT[:, i - 2, t0:t0 + n], ps[:, 0:n], AF.Silu, [ps], [sgT])
                                return
                            sq_ = sq[(ci % 2) * 2 + i]
                            act(sq_[:, 0:n], ps[:, 0:n], AF.Square, [ps], [sq_])
                            held[i] = (ps, sq_)
                            if i == 1:
                                pss_ = pss[ci % 2]
                                rst_ = rst[ci % 2]
                                for jj in range(2):
                                    MM(pss_[:, 0:n], onesb[:], held[jj][1][:, 0:n], jj == 0, jj == 1,
                                       [onesb, held[jj][1]], [pss_])
                                ts(rst_[:, 0:n], pss_[:, 0:n], 1.0 / 256, EPS, ALU.mult, ALU.add, [pss_], [rst_])
                                act(rst_[:, 0:n], rst_[:, 0:n], AF.Sqrt, [rst_], [rst_])
                                V(lambda e, a=rst_[:, 0:n]: e.reciprocal(out=a, in_=a), [rst_], [rst_])
                                for jj in range(2):
                                    if t0 < L:
                                        o_ = bnT[:, jj, 0:L].rearrange("p (b a) -> p a b", a=64)[:, 8 * ci:8 * ci + 8, :]
                                        stt(o_, held[jj][0][:, 0:512].rearrange("p (a b) -> p a b", b=64),
                                            fg[:, g, jj:jj + 1], rst_[:, 0:512].rearrange("p (a b) -> p a b", b=64),
                                            ALU.mult, ALU.mult, [held[jj][0], fg, rst_], [bnT])
                                    else:
                                        stt(bnT[:, jj, t0:t0 + n], held[jj][0][:, 0:n], fg[:, g, jj:jj + 1],
                                            rst_[:, 0:n], ALU.mult, ALU.mult, [held[jj][0], fg, rst_], [bnT])

                        proj_fm(p1, w_in0, fmt(g), CHUNKS, epi, "b1", ps_tiles, ut_tiles, wsb=wB[g % 2])
                        S.flush()
                    if g < 3:
                        load_w(wB[(g + 1) % 2], w_in0, fmt(g + 1))
                    with ExitStack() as p2:
                        YB = sbuf(p2, "YB", [128, 32, 512], BF16)
                        Yc_ = sbuf(p2, "Yc_", [128, 2, 512], BF16)
                        Bsb = sbuf(p2, "Bsb", [128, 2, 2, L], BF16)
                        fo = [sbuf(p2, "fo%d" % i, [128, L], BF16) for i in range(2)]
                        foc = sbuf(p2, "foc", [128, 256], BF16)
                        m1 = [sbuf(p2, "fm1%d" % i, [128, 512], F32) for i in range(2)]
                        m2 = [sbuf(p2, "fm2%d" % i, [128, 512], F32) for i in range(2)]
                        psy = [psum(p2, "psy%d" % i, [128, 512], F32) for i in range(2)]
                        pa = [psum(p2, "pa%d" % i, [128, 4, 128], F32) for i in range(4)]
                        ptr = [psum(p2, "ptr%d" % i, [128, 8, 128], BF16) for i in range(2)]
                        for beta in range(32):
                            p_ = psy[beta % 2]
                            for jj in range(2):
                                MM(p_[:], bnT[:, jj, beta * 128:(beta + 1) * 128], cs256[:, jj, :], jj == 0, jj == 1,
                                   [bnT, cs256], [p_])
                            if beta % 2 == 0:
                                cp(YB[:, beta, :], p_[:], [p_], [YB])
                            else:
                                act(YB[:, beta, :], p_[:], AF.Copy, [p_], [YB])
                        for t_ in range(2):
                            p_ = psy[t_ % 2]
                            for jj in range(2):
                                MM(p_[:], bnT[:, jj, L + t_ * 128:L + (t_ + 1) * 128], cs256[:, jj, :], jj == 0, jj == 1,
                                   [bnT, cs256], [p_])
                            cp(Yc_[:, t_, :], p_[:], [p_], [Yc_])
                        sc_x = 1.0 / math.sqrt(L * 256.0)
                        sc_c = 1.0 / math.sqrt(LC * 256.0)
                        kq = 0
                        for kk in range(2):
                            for bq in range(8):
                                par_, pai_ = pa[(kq % 2) * 2], pa[(kq % 2) * 2 + 1]
                                m1_, m2_ = m1[kq % 2], m2[kq % 2]
                                kq += 1
                                for q in range(4):
                                    beta = bq * 4 + q
                                    yc = YB[:, beta, kk * 128:(kk + 1) * 128]
                                    ys = YB[:, beta, 256 + kk * 128:256 + (kk + 1) * 128]
                                    MM(par_[:, q, :], yc, w1t[:, 0, :], True, False, [YB, w1t], [par_])
                                    MM(par_[:, q, :], ys, w1t[:, 1, :], False, True, [YB, w1t], [par_])
                                    MM(pai_[:, q, :], yc, w1t[:, 1, :], True, False, [YB, w1t], [pai_])
                                    MM(pai_[:, q, :], ys, w1t[:, 2, :], False, True, [YB, w1t], [pai_])
                                sl = slice(bq * 512, (bq + 1) * 512)
                                arv = par_[:].rearrange("p a b -> p (a b)")
                                aiv = pai_[:].rearrange("p a b -> p (a b)")
                                tt(m1_[:], arv, twd[:, 0, sl], ALU.mult, [par_, twd], [m1_])
                                tt(m2_[:], aiv, twd[:, 1, sl], ALU.mult, [pai_, twd], [m2_])
                                ob = lambda ri: Bsb[:, kk, ri, :].rearrange("p (m b) -> p b m", b=64)[:, 8 * bq:8 * bq + 8, :]
                                v3 = lambda t_: t_[:].rearrange("p (b m) -> p b m", m=64)
                                tt(ob(0), v3(m1_), v3(m2_), ALU.add, [m1_, m2_], [Bsb], eng="gpsimd")
                                tt(m1_[:], aiv, twd[:, 0, sl], ALU.mult, [pai_, twd], [m1_])
                                tt(m2_[:], arv, twd[:, 1, sl], ALU.mult, [par_, twd], [m2_])
                                tt(ob(1), v3(m1_), v3(m2_), ALU.subtract, [m1_, m2_], [Bsb], eng="gpsimd")
                        BT = YB
                        ke = 0
                        for mq in range(8):
                            for ri in range(2):
                                pt_ = ptr[ke % 2]
                                for q in range(4):
                                    mu = mq * 4 + q
                                    for kk in range(2):
                                        src = Bsb[:, kk, ri, mu * 128:(mu + 1) * 128]
                                        S.op("tensor", lambda e, o=pt_[:, q * 2 + kk, :], a=src:
                                             e.transpose(out=o, in_=a, identity=identb[:]), [Bsb, identb], [pt_],
                                             pe_chain=True)
                                dst = BT[:, mq * 4:(mq + 1) * 4, ri * 256:(ri + 1) * 256]
                                srcp = pt_[:].rearrange("p (q k) c -> p q (k c)", k=2)
                                if ke % 2 == 0:
                                    cp(dst, srcp, [pt_], [BT])
                                else:
                                    act(dst, srcp, AF.Copy, [pt_], [BT])
                                ke += 1
                        kq = 0
                        for kk in range(2):
                            fo_ = fo[kk]
                            fov = fo_[:].rearrange("p (mb ma) -> p ma mb", ma=64)
                            sgv = sgT[:, kk, 0:L].rearrange("p (mb ma) -> p ma mb", ma=64)
                            for mq in range(8):
                                pf_ = pa[kq % 4]
                                kq += 1
                                for q in range(4):
                                    mu = mq * 4 + q
                                    MM(pf_[:, q, :], BT[:, mu, kk * 128:(kk + 1) * 128], w3t[:, 0, :], True, False,
                                       [BT, w3t], [pf_])
                                    MM(pf_[:, q, :], BT[:, mu, 256 + kk * 128:256 + (kk + 1) * 128], w3t[:, 1, :],
                                       False, True, [BT, w3t], [pf_])
                                stt(fov[:, mq * 8:(mq + 1) * 8, :], pf_[:].rearrange("p q (l m) -> p (q l) m", l=2), sc_x,
                                    sgv[:, mq * 8:(mq + 1) * 8, :], ALU.mult, ALU.mult, [pf_, sgT], [fo_])
                            r0 = 1024 + g * 256 + kk * 128
                            dma(mixT_d[r0:r0 + 128, 0:L], fo_[:], [fo_], [R_mix], q="sync")
                        for kk in range(2):
                            pc = psy[kk]
                            for t_ in range(2):
                                MM(pc[:, 0:256], Yc_[:, t_, kk * 128:(kk + 1) * 128], cctx[:, t_, 0:256],
                                   t_ == 0, False, [Yc_, cctx], [pc])
                                MM(pc[:, 0:256], Yc_[:, t_, 256 + kk * 128:256 + (kk + 1) * 128], cctx[:, t_, 256:512],
                                   False, t_ == 1, [Yc_, cctx], [pc])
                            stt(foc[:], pc[:, 0:256], sc_c, sgT[:, kk, L:TOK], ALU.mult, ALU.mult, [pc, sgT], [foc])
                            r0 = 1024 + g * 256 + kk * 128
                            dma(mixT_d[r0:r0 + 128, L:TOK], foc[:], [foc], [R_mix], q="gpsimd")
                        S.flush()

        def phase_out(l, wout_d, hin_d, R_hin, ntiles, hout_d, R_hout, is_output):
            with ExitStack() as ph:
                wo = sbuf(ph, "wo", [128, 16, D], BF16)
                for k4 in range(4):
                    dma(wo[:, k4 * 4:(k4 + 1) * 4, :], wout_d[:, k4 * 4:(k4 + 1) * 4, :], [], [wo], q="gpsimd")
                gbc = [sbuf(ph, "gbc%d" % i, [128, D], F32) for i in range(2)]
                dg = sbuf(ph, "dg", [128, 128], F32)
                mx = [sbuf(ph, "mx%d" % i, [128, 16, 512], BF16) for i in range(2)]
                hr = [sbuf(ph, "hr%d" % i, [128, D], F32) for i in range(2)]
                tm = [sbuf(ph, "tm%d" % i, [128, D], F32) for i in range(2)]
                junk = sbuf(ph, "junk", [128, D], BF16)
                ss = sbuf(ph, "ss", [128, 34], F32)
                po = [psum(ph, "po%d" % i, [128, 4, 512], F32) for i in range(2)]
                nseq = 2 if ntiles > 32 else 1
                for i in range(nseq):
                    for c in range(16):
                        ts(dg[:], identf[:], gpT[:, l, i, c:c + 1], None, ALU.mult, None, [identf, gpT], [dg])
                        MM(po[0][:, c // 4, (c % 4) * 128:(c % 4 + 1) * 128], onesf[:], dg[:], True, True,
                           [onesf, dg], [po[0]])
                    for q in range(4):
                        cp(gbc[i][:, q * 512:(q + 1) * 512], po[0][:, q, :], [po[0]], [gbc[i]])
                for tix in range(ntiles):
                    i = 0 if tix < 32 else 1
                    mx_ = mx[(tix // 4) % 2]
                    if tix % 4 == 0:
                        n = min(512, ntiles * 128 - tix * 128)
                        t0 = tix * 128
                        dma(mx_[:, :, 0:n], mixT_d.rearrange("(c p) t -> p c t", p=128)[:, :, t0:t0 + n],
                            [R_mix], [mx_], q="sync")
                    hr_, tm_, po_ = hr[tix % 2], tm[tix % 2], po[tix % 2]
                    dma(hr_[:], hin_d[tix * 128:(tix + 1) * 128, :], [R_hin], [hr_], q="sync")
                    q4 = tix % 4
                    for nn in range(4):
                        for k in range(16):
                            MM(po_[:, nn, :], mx_[:, k, q4 * 128:(q4 + 1) * 128], wo[:, k, nn * 512:(nn + 1) * 512],
                               k == 0, k == 15, [mx_, wo], [po_])
                    act(junk[:], po_[:].rearrange("p a b -> p (a b)"), AF.Square, [po_], [junk, ss],
                        accum_out=ss[:, tix:tix + 1])
                    ts(ss[:, tix:tix + 1], ss[:, tix:tix + 1], 1.0 / D, EPS, ALU.mult, ALU.add, [ss], [ss])
                    act(ss[:, tix:tix + 1], ss[:, tix:tix + 1], AF.Sqrt, [ss], [ss])
                    V(lambda e, a=ss[:, tix:tix + 1]: e.reciprocal(out=a, in_=a), [ss], [ss])
                    tt(tm_[:], po_[:].rearrange("p a b -> p (a b)"), gbc[i][:], ALU.mult, [po_, gbc[i]], [tm_])
                    stt(tm_[:], tm_[:], ss[:, tix:tix + 1], hr_[:], ALU.mult, ALU.add, [tm_, ss, hr_], [tm_])
                    dma(hout_d[tix * 128:(tix + 1) * 128, :], tm_[:], [tm_], [R_hout], q="gpsimd", is_output=is_output)
                S.flush()

        def phase_na():
            with ExitStack() as ph:
                ur = sbuf(ph, "ur", [128, 16, 2560], BF16)
                uv = uT_d.rearrange("(c p) t -> p c t", p=128)
                for q in range(4):
                    dma(ur[:, q * 4:(q + 1) * 4, 0:NKV], uv[:, q * 4:(q + 1) * 4, 0:NKV], [R_uT], [ur])
                dma(ur[:, :, NKV:2560], uv[:, :, L:TOK], [R_uT], [ur], q="sync")
                wq = [sbuf(ph, "wq%d" % i, [128, 4, 16, 128], BF16) for i in range(2)]
                bt = [sbuf(ph, "bt%d" % i, [128, 3, 5, 128], BF16) for i in range(2)]
                qT = [sbuf(ph, "qT%d" % i, [128, OWN], BF16) for i in range(2)]
                kT = [sbuf(ph, "kT%d" % i, [128, 2560], BF16) for i in range(2)]
                sg = [sbuf(ph, "sg%d" % i, [128, OWN], BF16) for i in range(2)]
                Vh = [sbuf(ph, "Vh%d" % i, [128, 20, 128], BF16) for i in range(2)]
                PT = [sbuf(ph, "PT%d" % i, [128, 7, 128], BF16) for i in range(2)]
                vTf = sbuf(ph, "vTf", [128, 2560], F32)
                rd = [sbuf(ph, "rd%d" % i, [128, 128], F32) for i in range(2)]
                ot = [sbuf(ph, "ot%d" % i, [128, 128], F32) for i in range(2)]
                naT = [sbuf(ph, "naT%d" % i, [128, OWN], BF16) for i in range(2)]
                pp = [psum(ph, "pp%d" % i, [128, 512], F32) for i in range(2)]
                pS = [psum(ph, "pS%d" % i, [128, 8, 128], F32) for i in range(2)]
                pO = [psum(ph, "pO%d" % i, [128, 2, 128], F32) for i in range(2)]
                kp = 0
                isq = 1.0 / math.sqrt(128.0)
                for h in range(12):
                    w_ = wq[h % 2]
                    bt_ = bt[h % 2]
                    for i, mt in enumerate((h, 12 + h, 24 + h, 36 + h)):
                        dma(w_[:, i, :, :], w_in1[mt, :, :, :], [], [w_], q="gpsimd")
                    dma(bt_[:], na_bias[:, h, :, :, :].rearrange("c p k q -> p c k q"), [], [bt_], q="gpsimd")
                    qT_, kT_, sg_, Vh_, naT_ = qT[h % 2], kT[h % 2], sg[h % 2], Vh[h % 2], naT[h % 2]
                    for ci in range(5):
                        t0 = ci * 512
                        ps = pp[kp % 2]; kp += 1
                        for k in range(16):
                            MM(ps[:], w_[:, 1, k, :], ur[:, k, t0:t0 + 512], k == 0, k == 15, [w_, ur], [ps])
                        cp(kT_[:, t0:t0 + 512], ps[:], [ps], [kT_])
                        if ci < 4:
                            ps = pp[kp % 2]; kp += 1
                            for k in range(16):
                                MM(ps[:], w_[:, 0, k, :], ur[:, k, t0:t0 + 512], k == 0, k == 15, [w_, ur], [ps])
                            act(qT_[:, t0:t0 + 512], ps[:], AF.Copy, [ps], [qT_], scale=isq)
                            ps = pp[kp % 2]; kp += 1
                            for k in range(16):
                                MM(ps[:], w_[:, 3, k, :], ur[:, k, t0:t0 + 512], k == 0, k == 15, [w_, ur], [ps])
                            act(sg_[:, t0:t0 + 512], ps[:], AF.Silu, [ps], [sg_])
                    for ci in range(5):
                        t0 = ci * 512
                        ps = pp[kp % 2]; kp += 1
                        for k in range(16):
                            MM(ps[:], w_[:, 2, k, :], ur[:, k, t0:t0 + 512], k == 0, k == 15, [w_, ur], [ps])
                        act(vTf[:, t0:t0 + 512], ps[:], AF.Copy, [ps], [vTf])
                    for t4 in range(5):
                        ps = pp[kp % 2]; kp += 1
                        for q in range(4):
                            tix = t4 * 4 + q
                            S.op("tensor", lambda e, o=ps[:, q * 128:(q + 1) * 128], a=vTf[:, tix * 128:(tix + 1) * 128]:
                                 e.transpose(out=o, in_=a, identity=identf[:]), [vTf, identf], [ps], pe_chain=True)
                        cp(Vh_[:, t4 * 4:(t4 + 1) * 4, :], ps[:].rearrange("p (a b) -> p a b", b=128), [ps], [Vh_])
                    for j in range(16):
                        cls = min(j, 2)
                        ws = min(max(2 * j - 4, 0), 26)
                        pS_, pO_, PT_ = pS[j % 2], pO[j % 2], PT[j % 2]
                        rd_, ot_ = rd[j % 2], ot[j % 2]
                        for kt in range(7):
                            k0 = ws * 64 + kt * 128 if kt < 5 else NKV + (kt - 5) * 128
                            MM(pS_[:, kt, :], kT_[:, k0:k0 + 128], qT_[:, j * 128:(j + 1) * 128], True, kt >= 5,
                               [kT_, qT_], [pS_])
                            if kt < 5:
                                MM(pS_[:, kt, :], identb[:], bt_[:, cls, kt, :], False, True, [identb, bt_], [pS_])
                        act(PT_[:, 0:4, :], pS_[:, 0:4, :], AF.Exp, [pS_], [PT_])
                        act(PT_[:, 4:7, :], pS_[:, 4:7, :], AF.Exp, [pS_], [PT_])
                        for kt in range(7):
                            vt = ws // 2 + kt if kt < 5 else 18 + (kt - 5)
                            MM(pO_[:, 0, :], Vh_[:, vt, :], PT_[:, kt, :], kt == 0, kt == 6, [Vh_, PT_], [pO_])
                        for kt in range(7):
                            MM(pO_[:, 1, :], onesb[:], PT_[:, kt, :], kt == 0, kt == 6, [onesb, PT_], [pO_])
                        V(lambda e, o=rd_[:], a=pO_[:, 1, :]: e.reciprocal(out=o, in_=a), [pO_], [rd_])
                        tt(ot_[:], pO_[:, 0, :], rd_[:], ALU.mult, [pO_, rd_], [ot_])
                        tt(naT_[:, j * 128:(j + 1) * 128], ot_[:], sg_[:, j * 128:(j + 1) * 128], ALU.mult,
                           [ot_, sg_], [naT_])
                    dma(mixT_d[h * 128:(h + 1) * 128, 0:OWN], naT_[:], [naT_], [R_mix], q="sync")
                S.flush()

        def phase_s5():
            with ExitStack() as ph:
                dT = sbuf(ph, "dT", [128, 4, TOK], BF16)
                sdg = sbuf(ph, "sdg", [128, 4, OWN], BF16)
                ysb = sbuf(ph, "ysb", [128, 4, OWN], F32)
                with ExitStack() as p1:
                    ut_tiles = [sbuf(p1, "ut%d" % i, [128, 16, 512], BF16) for i in range(2)]
                    ps_tiles = [psum(p1, "ps%d" % i, [128, 512], F32) for i in range(4)]

                    def epi(i, ci, t0, n, ps):
                        if i < 4:
                            cp(dT[:, i, t0:t0 + n], ps[:, 0:n], [ps], [dT])
                        elif t0 < OWN:
                            act(sdg[:, i - 4, t0:t0 + n], ps[:, 0:n], AF.Silu, [ps], [sdg])

                    proj_fm(p1, w_in1, [48 + i for i in range(8)], CHUNKS, epi, "s5p", ps_tiles, ut_tiles)
                    S.flush()
                with ExitStack() as p2:
                    def small(name):
                        return sbuf(p2, name, [128, 2, 16], F32)
                    are, aim, ldt = small("are"), small("aim"), small("ldt")
                    dtt, rr, thp, cfr, cfi = small("dtt"), small("rr"), small("thp"), small("cfr"), small("cfi")
                    w1, w2, w3, w4 = small("w1"), small("w2"), small("w3"), small("w4")
                    wi = sbuf(p2, "wi", [128, 2, 16], I32)
                    dma(are[:], s5are[:, :, :], [], [are], q="sync")
                    dma(aim[:], s5aim[:, :, :], [], [aim], q="sync")
                    dma(ldt[:], s5ldt[:, :, :], [], [ldt], q="sync")
                    act(dtt[:], ldt[:], AF.Exp, [ldt], [dtt])
                    tt(w1[:], are[:], dtt[:], ALU.mult, [are, dtt], [w1])
                    act(rr[:], w1[:], AF.Exp, [w1], [rr])
                    tt(w1[:], aim[:], dtt[:], ALU.mult, [aim, dtt], [w1])
                    ts(thp[:], w1[:], 1.0 / (2.0 * math.pi), None, ALU.mult, None, [w1], [thp])

                    def sincos(src, sin_out, cos_out):
                        cp(wi[:], src[:], [src], [wi])
                        cp(w2[:], wi[:], [wi], [w2])
                        tt(w2[:], src[:], w2[:], ALU.subtract, [src, w2], [w2])
                        act(sin_out[:], w2[:], AF.Sin, [w2], [sin_out], scale=6.28318)
                        ts(w3[:], src[:], 0.25, None, ALU.add, None, [src], [w3])
                        cp(wi[:], w3[:], [w3], [wi])
                        cp(w2[:], wi[:], [wi], [w2])
                        tt(w2[:], w3[:], w2[:], ALU.subtract, [w3, w2], [w2])
                        act(cos_out[:], w2[:], AF.Sin, [w2], [cos_out], scale=6.28318)

                    sn, cs_ = small("sn"), small("cs_")
                    sincos(thp, sn, cs_)
                    tt(w1[:], rr[:], cs_[:], ALU.mult, [rr, cs_], [w1])
                    ts(w1[:], w1[:], -1.0, None, ALU.add, None, [w1], [w1])
                    tt(w4[:], rr[:], sn[:], ALU.mult, [rr, sn], [w4])
                    tt(w2[:], are[:], are[:], ALU.mult, [are], [w2])
                    tt(w3[:], aim[:], aim[:], ALU.mult, [aim], [w3])
                    tt(w2[:], w2[:], w3[:], ALU.add, [w2, w3], [w2])
                    V(lambda e: e.reciprocal(out=w2[:], in_=w2[:]), [w2], [w2])
                    tt(cfr[:], w1[:], are[:], ALU.mult, [w1, are], [cfr])
                    tt(w3[:], w4[:], aim[:], ALU.mult, [w4, aim], [w3])
                    tt(cfr[:], cfr[:], w3[:], ALU.add, [cfr, w3], [cfr])
                    tt(cfr[:], cfr[:], w2[:], ALU.mult, [cfr, w2], [cfr])
                    tt(cfi[:], w4[:], are[:], ALU.mult, [w4, are], [cfi])
                    tt(w3[:], w1[:], aim[:], ALU.mult, [w1, aim], [w3])
                    tt(cfi[:], cfi[:], w3[:], ALU.subtract, [cfi, w3], [cfi])
                    tt(cfi[:], cfi[:], w2[:], ALU.mult, [cfi, w2], [cfi])

                    braw = sbuf(p2, "braw", [128, 2, 2, 16, 128], BF16)
                    ctw = sbuf(p2, "ctw", [128, 2, 2, 16, 128], BF16)
                    dma(braw[:], s5braw[:, :, :, :, :], [], [braw], q="gpsimd")
                    dma(ctw[:], s5ct[:, :, :, :, :], [], [ctw], q="gpsimd")
                    ts(ctw[:, 1], ctw[:, 1], -1.0, None, ALU.mult, None, [ctw], [ctw])
                    ctw2 = ctw
                    ctmp = sbuf(p2, "ctmp", [128, 128], F32)
                    ctmpb = sbuf(p2, "ctmpb", [128, 128], F32)
                    for d_ in range(2):
                        for j in range(16):
                            c0, c1 = ctw[:, 0, d_, j, :], ctw[:, 1, d_, j, :]
                            ts(ctmp[:], c1, cfi[:, d_, j:j + 1], None, ALU.mult, None, [ctw, cfi], [ctmp])
                            stt(ctmpb[:], c0, cfr[:, d_, j:j + 1], ctmp[:], ALU.mult, ALU.add, [ctw, cfr, ctmp], [ctmpb])
                            ts(ctmp[:], c0, cfi[:, d_, j:j + 1], None, ALU.mult, None, [ctw, cfi], [ctmp])
                            stt(c1, c1, cfr[:, d_, j:j + 1], ctmp[:], ALU.mult, ALU.subtract, [ctw, cfr, ctmp], [ctw])
                            cp(c0, ctmpb[:], [ctmpb], [ctw])
                    sv = sbuf(p2, "sv", [128, 513], F32)
                    dma(sv[:], svals_d[:, :], [], [sv], q="sync")
                    ones5 = sbuf(p2, "ones5", [128, 512], F32)
                    V(lambda e: e.memset(ones5[:], 1.0), [], [ones5])
                    a1s = [sbuf(p2, "a1%d" % i, [128, 513], F32) for i in range(1)] * 2
                    a2s = [sbuf(p2, "a2%d" % i, [128, 513], F32) for i in range(1)] * 2
                    ais = [sbuf(p2, "ai%d" % i, [128, 513], I32) for i in range(1)] * 2
                    sinTs = [sbuf(p2, "sinT%d" % i, [128, 513], F32) for i in range(2)]
                    cosTs = [sbuf(p2, "cosT%d" % i, [128, 513], F32) for i in range(2)]
                    rfills = [sbuf(p2, "rfill%d" % i, [128, 512], F32) for i in range(2)]
                    mts4 = [[sbuf(p2, "mt%d%d" % (i, q), [128, 512], F32) for q in range(4)] for i in range(2)]
                    ris = [sbuf(p2, "ris%d" % i, [128, 2, 2], F32) for i in range(2)]
                    g1s = [sbuf(p2, "g1_%d" % i, [128, 512], F32) for i in range(2)]
                    g2s = [sbuf(p2, "g2_%d" % i, [128, 512], F32) for i in range(2)]
                    bps = [[sbuf(p2, "bp%d%d" % (i, q), [128, 512], F32) for q in range(2)] for i in range(2)]
                    kks = [[sbuf(p2, "kk%d%d" % (i, q), [128, 2, 512], F32) for q in range(2)] for i in range(2)]
                    inis = [sbuf(p2, "ini%d" % i, [128, 4], F32) for i in range(2)]
                    hh = [sbuf(p2, "hh%d" % d_, [128, 2, OWN], BF16) for d_ in range(2)]
                    pbu = [psum(p2, "pbu%d" % i, [128, 512], F32) for i in range(4)]
                    py = [psum(p2, "py%d" % i, [128, 512], F32) for i in range(4)]
                    seqs = [
                        [(L, 256, False, None)] + [(i * 512, 512, False, i * 512) for i in range(4)],
                        [(L, 256, True, None)] + [(i * 512, 512, True, (i * 512 if i < 4 else None))
                                                  for i in range(7, -1, -1)],
                    ]
                    for j in range(16):
                        kc = j // 4
                        for d_ in range(2):
                            rfill = rfills[d_]
                            sinT, cosT = sinTs[d_], cosTs[d_]
                            a1, a2, ai = a1s[d_], a2s[d_], ais[d_]
                            ts(a1[:], sv[:], thp[:, d_, j:j + 1], None, ALU.mult, None, [sv, thp], [a1])
                            cp(ai[:], a1[:], [a1], [ai])
                            cp(a2[:], ai[:], [ai], [a2])
                            tt(a2[:], a1[:], a2[:], ALU.subtract, [a1, a2], [a2])
                            act(sinT[:], a2[:], AF.Sin, [a2], [sinT], scale=6.28318)
                            ts(a1[:], a1[:], 0.25, None, ALU.add, None, [a1], [a1])
                            cp(ai[:], a1[:], [a1], [ai])
                            cp(a2[:], ai[:], [ai], [a2])
                            tt(a2[:], a1[:], a2[:], ALU.subtract, [a1, a2], [a2])
                            act(cosT[:], a2[:], AF.Sin, [a2], [cosT], scale=6.28318)
                            ts(rfill[:], ones5[:], rr[:, d_, j:j + 1], None, ALU.mult, None, [ones5, rr], [rfill])
                            for ni, nn_ in enumerate((256, 512)):
                                ts(ris[d_][:, ni, 0:1], sinT[:, nn_:nn_ + 1], -1.0, None, ALU.mult, None, [sinT], [ris[d_]])
                                cp(ris[d_][:, ni, 1:2], sinT[:, nn_:nn_ + 1], [sinT], [ris[d_]])
                        for step in range(9):
                            for d_ in range(2):
                                if step >= len(seqs[d_]):
                                    continue
                                (m0, n, rev, own) = seqs[d_][step]
                                sinT, cosT = sinTs[d_], cosTs[d_]
                                rfill, ini = rfills[d_], inis[d_]
                                g1_, g2_ = g1s[d_], g2s[d_]
                                pr, pi = pbu[2 * d_], pbu[2 * d_ + 1]
                                sl_ = d_
                                kk_ = kks[sl_][step % 2]
                                kr, ki = kk_[:, 0, :], kk_[:, 1, :]
                                ma, mb, mc, md = mts4[sl_]
                                bpr, bpi = bps[sl_]
                                MM(pr[:, 0:n], braw[:, 0, d_, j, :], dT[:, kc, m0:m0 + n], True, True, [braw, dT], [pr])
                                MM(pi[:, 0:n], braw[:, 1, d_, j, :], dT[:, kc, m0:m0 + n], True, True, [braw, dT], [pi])
                                ur_ = pr[:, 0:n][:, ::-1] if rev else pr[:, 0:n]
                                ui_ = pi[:, 0:n][:, ::-1] if rev else pi[:, 0:n]
                                tt(ma[:, 0:n], ur_, cosT[:, 0:n], ALU.mult, [pr, cosT], [ma])
                                tt(mb[:, 0:n], ui_, sinT[:, 0:n], ALU.mult, [pi, sinT], [mb])
                                tt(mc[:, 0:n], ui_, cosT[:, 0:n], ALU.mult, [pi, cosT], [mc])
                                tt(md[:, 0:n], ur_, sinT[:, 0:n], ALU.mult, [pr, sinT], [md])
                                tt(bpr[:, 0:n], ma[:, 0:n], mb[:, 0:n], ALU.add, [ma, mb], [bpr], eng="gpsimd")
                                tt(bpi[:, 0:n], mc[:, 0:n], md[:, 0:n], ALU.subtract, [mc, md], [bpi], eng="gpsimd")
                                i_r = 0.0 if step == 0 else ini[:, 0:1]
                                i_i = 0.0 if step == 0 else ini[:, 1:2]
                                V(lambda e, o=kr[:, 0:n], a=rfill[:, 0:n], b=bpr[:, 0:n], iv=i_r:
                                  e.tensor_tensor_scan(out=o, data0=a, data1=b, initial=iv, op0=ALU.mult, op1=ALU.add),
                                  [rfill, bpr, ini], [kk_])
                                V(lambda e, o=ki[:, 0:n], a=rfill[:, 0:n], b=bpi[:, 0:n], iv=i_i:
                                  e.tensor_tensor_scan(out=o, data0=a, data1=b, initial=iv, op0=ALU.mult, op1=ALU.add),
                                  [rfill, bpi, ini], [kk_])
                                ni = 0 if n == 256 else 1
                                tt(ini[:, 2:4], kk_[:, ::-1, n - 1], ris[d_][:, ni, :], ALU.mult, [kk_, ris[d_]], [ini])
                                stt(ini[:, 0:2], kk_[:, :, n - 1], cosT[:, n:n + 1], ini[:, 2:4], ALU.mult, ALU.add,
                                    [kk_, cosT, ini], [ini])
                                if own is not None:
                                    h_ = hh[d_]
                                    o_r = h_[:, 0, own:own + n]
                                    o_i = h_[:, 1, own:own + n]
                                    if rev:
                                        o_r = o_r[:, ::-1]
                                        o_i = o_i[:, ::-1]
                                    tt(g1_[:, 0:n], cosT[:, 0:n], kr[:, 0:n], ALU.mult, [cosT, kk_], [g1_], eng="gpsimd")
                                    tt(g2_[:, 0:n], sinT[:, 0:n], ki[:, 0:n], ALU.mult, [sinT, kk_], [g2_], eng="gpsimd")
                                    tt(o_r, g1_[:, 0:n], g2_[:, 0:n], ALU.subtract, [g1_, g2_], [h_], eng="gpsimd")
                                    tt(g1_[:, 0:n], cosT[:, 0:n], ki[:, 0:n], ALU.mult, [cosT, kk_], [g1_], eng="gpsimd")
                                    tt(g2_[:, 0:n], sinT[:, 0:n], kr[:, 0:n], ALU.mult, [sinT, kk_], [g2_], eng="gpsimd")
                                    tt(o_i, g1_[:, 0:n], g2_[:, 0:n], ALU.add, [g1_, g2_], [h_], eng="gpsimd")
                        for tc in range(4):
                            for d_ in range(2):
                                for ri in range(2):
                                    MM(py[tc][:], ctw2[:, ri, d_, j, :], hh[d_][:, ri, tc * 512:(tc + 1) * 512],
                                       (j % 4 == 0 and d_ == 0 and ri == 0), (j % 4 == 3 and d_ == 1 and ri == 1),
                                       [ctw2, hh[d_]], [py[tc]])
                        if j % 4 == 3:
                            for tc in range(4):
                                cp(ysb[:, kc, tc * 512:(tc + 1) * 512], py[tc][:], [py[tc]], [ysb])
                    S.flush()
                with ExitStack() as p2:
                    pbu = [psum(p2, "pbu%d" % i, [128, 512], F32) for i in range(4)]
                    dcol = sbuf(p2, "dcol", [128, 4], F32)
                    dma(dcol[:], s5d[:, :], [], [dcol], q="sync")
                    wg = sbuf(p2, "wg", [128, 4, 4, 128], BF16)
                    for mt in range(4):
                        dma(wg[:, mt, :, :], w_glu[mt, :, :, :], [], [wg], q="gpsimd")
                    yb = sbuf(p2, "yb", [128, 4, OWN], BF16)
                    g1 = sbuf(p2, "g1", [128, OWN], F32)
                    g2 = sbuf(p2, "g2", [128, OWN], F32)
                    for kc in range(4):
                        stt(ysb[:, kc, :], dT[:, kc, 0:OWN], dcol[:, kc:kc + 1], ysb[:, kc, :], ALU.mult, ALU.add,
                            [dT, dcol, ysb], [ysb])
                        tt(g1[:], ysb[:, kc, :], ysb[:, kc, :], ALU.mult, [ysb], [g1])
                        ts(g1[:], g1[:], 0.044715, 1.0, ALU.mult, ALU.add, [g1], [g1])
                        tt(g1[:], g1[:], ysb[:, kc, :], ALU.mult, [g1, ysb], [g1])
                        act(g2[:], g1[:], AF.Sigmoid, [g1], [g2], scale=1.5957691216057308)
                        tt(yb[:, kc, :], ysb[:, kc, :], g2[:], ALU.mult, [ysb, g2], [yb])
                    mo = [sbuf(p2, "mo%d" % i, [128, 512], BF16) for i in range(2)]
                    km = 0
                    for mt in range(4):
                        for tc in range(4):
                            ps = pbu[km % 4]
                            mo_ = mo[km % 2]
                            km += 1
                            for k in range(4):
                                MM(ps[:], wg[:, mt, k, :], yb[:, k, tc * 512:(tc + 1) * 512], k == 0, k == 3,
                                   [wg, yb], [ps])
                            act(g1[:, 0:512], ps[:], AF.Sigmoid, [ps], [g1])
                            tt(g1[:, 0:512], g1[:, 0:512], yb[:, mt, tc * 512:(tc + 1) * 512], ALU.mult, [g1, yb], [g1])
                            tt(mo_[:], g1[:, 0:512], sdg[:, mt, tc * 512:(tc + 1) * 512], ALU.mult, [g1, sdg], [mo_])
                            r0 = 1536 + mt * 128
                            dma(mixT_d[r0:r0 + 128, tc * 512:(tc + 1) * 512], mo_[:], [mo_], [R_mix], q="gpsimd")
                    S.flush()

        phase_p1(0, xin, T(None))
        phase_conv()
        phase_fourier()
        if debug == "l0":
            phase_out(0, w_out0, xin, T(None), 34, h1_d, R_h1, True)
            V(lambda e: e.memset(onesf[:], 1.0), [], [onesf])
            S.flush(final=True)
            return nc
        phase_out(0, w_out0, xin, T(None), 34, h1_d, R_h1, False)
        phase_p1(1, h1_d, R_h1)
        phase_na()
        phase_s5()
        phase_out(1, w_out1, h1_d, R_h1, 16, out_d, T(None), True)
        V(lambda e: e.memset(onesf[:], 1.0), [], [onesf])
        S.flush(final=True)
    return nc


_CONST_CACHE = {}


def _consts(inp):
    if "dft" not in _CONST_CACHE:
        _CONST_CACHE["dft"] = [_dft_consts(0), _dft_consts(1)]
    rpb = inp["na_rpb"][0]
    return {"dft": _CONST_CACHE["dft"], "na_bias": [_na_tables(rpb, 0), _na_tables(rpb, 1)]}


def kernel(**inputs):
    inp = {k: np.asarray(v) for k, v in inputs.items()}
    consts = _consts(inp)
    nc = build()
    in_maps = [prep_core(inp, core, consts) for core in range(8)]
    res = run_bass_kernel_spmd(nc, in_maps, core_ids=list(range(8)))
    out = np.empty((4, L, D), np.float32)
    for core in range(8):
        b, par = core // 2, core % 2
        o = res.results[core]["out"]
        if par:
            out[b, L - OWN:] = o[::-1]
        else:
            out[b, :OWN] = o
    return out
```
